# Optimizing a Trainium2 kernel written in Bass

```python
import jax, jax.numpy as jnp
from jax import lax
import numpy as np

D_MODEL = 2048
BATCH = 4
SEQ = 2048
DEPTH = 2
DEC_BATCH = 16
DEC_SEQ = 32
PAST_LEN = 1024

CHUNK = 64
LEFT_CHUNKS = 8
BAND_PAST = CHUNK * LEFT_CHUNKS
N_HEADS_A = 8
HEAD_DIM = 128
D_A = N_HEADS_A * HEAD_DIM
REL_CLIP = 128
D_B = 1024
N_BLOCKS_B = 8
BLOCK_B = D_B // N_BLOCKS_B
CONV_B = 4
LRU_C = 8.0
D_C = D_MODEL
CONV_C = 31
IN_AB = 3 * D_A + D_A + D_B + D_B
IN_CV = 3 * D_C
N_EVEN = (DEPTH + 1) // 2
N_ODD = DEPTH // 2
EPS = 1e-6

kernel_name = "chunk_band_attn_rglru_conformer_conv_stream"


def _rmsnorm(x, g):
    xf = x.astype(jnp.float32)
    y = xf * lax.rsqrt(jnp.mean(xf * xf, axis=-1, keepdims=True) + EPS) * g.astype(jnp.float32)
    return y.astype(x.dtype)


def _layernorm(x, g, b):
    xf = x.astype(jnp.float32)
    mu = jnp.mean(xf, axis=-1, keepdims=True)
    var = jnp.mean(jnp.square(xf - mu), axis=-1, keepdims=True)
    y = (xf - mu) * lax.rsqrt(var + EPS) * g.astype(jnp.float32) + b.astype(jnp.float32)
    return y.astype(x.dtype)


def _causal_dwconv(x, buf, w, b):
    W = w.shape[0]
    C = x.shape[-1]
    xp = jnp.concatenate([buf.astype(x.dtype), x], axis=1)
    y = lax.conv_general_dilated(xp, w[:, None, :].astype(x.dtype), window_strides=(1,), padding='VALID',
                                 dimension_numbers=('NWC', 'WIO', 'NWC'), feature_group_count=C)
    return y + b.astype(x.dtype), xp[:, xp.shape[1] - (W - 1):]


def _band_softmax(qb, kb, vb, q_pos, k_pos, rel_table):
    s = jnp.einsum('bnqhd,bnkhd->bnhqk', qb, kb, preferred_element_type=jnp.float32) * (HEAD_DIM ** -0.5)
    rel = jnp.clip(q_pos[:, :, None] - k_pos[:, None, :], -REL_CLIP, REL_CLIP) + REL_CLIP
    bias = jnp.moveaxis(rel_table[:, rel], 0, 1).astype(jnp.float32)
    s = s + bias[None]
    s = jnp.where((k_pos >= 0)[None, :, None, None, :], s, -jnp.inf)
    p = jax.nn.softmax(s, axis=-1).astype(vb.dtype)
    return jnp.einsum('bnhqk,bnkhd->bnqhd', p, vb)


def _attn_prompt(q, k, v, rel_table):
    B, S, H, Dh = q.shape
    nc = S // CHUNK
    pad = jnp.zeros((B, BAND_PAST, H, Dh), k.dtype)
    kp = jnp.concatenate([pad, k], axis=1).reshape(B, nc + LEFT_CHUNKS, CHUNK, H, Dh)
    vp = jnp.concatenate([pad, v], axis=1).reshape(B, nc + LEFT_CHUNKS, CHUNK, H, Dh)
    idx = jnp.arange(nc)[:, None] + jnp.arange(LEFT_CHUNKS + 1)[None, :]
    band = (LEFT_CHUNKS + 1) * CHUNK
    kb = kp[:, idx].reshape(B, nc, band, H, Dh)
    vb = vp[:, idx].reshape(B, nc, band, H, Dh)
    qb = q.reshape(B, nc, CHUNK, H, Dh)
    q_pos = jnp.arange(S).reshape(nc, CHUNK)
    k_pos = (jnp.arange(nc) * CHUNK - BAND_PAST)[:, None] + jnp.arange(band)[None, :]
    o = _band_softmax(qb, kb, vb, q_pos, k_pos, rel_table)
    return o.reshape(B, S, H * Dh)


def _attn_sample(q, k, v, k_cache, v_cache, rel_table):
    B, T, H, Dh = q.shape
    L = k_cache.shape[1]
    kb = jnp.concatenate([k_cache.astype(k.dtype), k], axis=1)[:, None]
    vb = jnp.concatenate([v_cache.astype(v.dtype), v], axis=1)[:, None]
    q_pos = (PAST_LEN + jnp.arange(T))[None]
    k_pos = (PAST_LEN - L + jnp.arange(L + T))[None]
    o = _band_softmax(q[:, None], kb, vb, q_pos, k_pos, rel_table)
    return o.reshape(B, T, H * Dh)


def _rg_lru(x, h0, w_a, b_a, w_x, b_x, lam):
    B, T, _ = x.shape
    xb = x.reshape(B, T, N_BLOCKS_B, BLOCK_B)
    r = jax.nn.sigmoid(jnp.einsum('btnd,nde->btne', xb, w_a).reshape(B, T, D_B) + b_a).astype(jnp.float32)
    i = jax.nn.sigmoid(jnp.einsum('btnd,nde->btne', xb, w_x).reshape(B, T, D_B) + b_x).astype(jnp.float32)
    log_a = LRU_C * r * jax.nn.log_sigmoid(lam.astype(jnp.float32))
    a = jnp.exp(log_a)
    bt = jnp.sqrt(-jnp.expm1(2.0 * log_a)) * (i * x.astype(jnp.float32))
    bt = bt.at[:, 0].add(a[:, 0] * h0.astype(jnp.float32))

    def comb(left, right):
        a1, b1 = left
        a2, b2 = right
        return a1 * a2, a2 * b1 + b2

    _, h = lax.associative_scan(comb, (a, bt), axis=1)
    return h, h[:, -1]


def _layer_ab(x, k_cache, v_cache, h0, lb0, g_norm, w_in, w_out, rel_table, cw, cb, w_a, b_a, w_x, b_x, lam):
    B, T, _ = x.shape
    u = _rmsnorm(x, g_norm) @ w_in
    q, k, v, g_a, xb, g_b = jnp.split(u, [D_A, 2 * D_A, 3 * D_A, 4 * D_A, 4 * D_A + D_B], axis=-1)
    q = q.reshape(B, T, N_HEADS_A, HEAD_DIM)
    k = k.reshape(B, T, N_HEADS_A, HEAD_DIM)
    v = v.reshape(B, T, N_HEADS_A, HEAD_DIM)
    if k_cache is None:
        o_a = _attn_prompt(q, k, v, rel_table)
        keep = min(BAND_PAST, T)
        new_k, new_v = k[:, T - keep:], v[:, T - keep:]
    else:
        o_a = _attn_sample(q, k, v, k_cache, v_cache, rel_table)
        new_k, new_v = k, v
    xc, new_lb = _causal_dwconv(xb, lb0, cw, cb)
    h, h_last = _rg_lru(xc, h0, w_a, b_a, w_x, b_x, lam)
    mix = jnp.concatenate([jax.nn.silu(g_a) * o_a, jax.nn.silu(g_b) * h.astype(x.dtype)], axis=-1)
    return x + mix @ w_out, new_k, new_v, h_last.astype(x.dtype), new_lb


def _layer_conv(x, buf0, g_norm, w_in, w_out, dw_w, dw_b, ln_g, ln_b):
    u = _rmsnorm(x, g_norm) @ w_in
    val, glu_gate, gate = jnp.split(u, [D_C, 2 * D_C], axis=-1)
    z = val * jax.nn.sigmoid(glu_gate)
    zc, new_buf = _causal_dwconv(z, buf0, dw_w, dw_b)
    y = jax.nn.silu(_layernorm(zc, ln_g, ln_b)) * jax.nn.silu(gate)
    return x + y @ w_out, new_buf


def _trunk(x, kc, vc, hc, lbc, cbc, norm_ab, w_in_ab, w_out_ab, rel_bias, lru_conv_w, lru_conv_b,
           lru_w_a, lru_b_a, lru_w_x, lru_b_x, lru_lambda, norm_cv, w_in_cv, w_out_cv, dw_w, dw_b,
           ln_g, ln_b, final_norm):
    prompt = kc is None
    B = x.shape[0]
    ks, vs, hs, lbs, cbs = [], [], [], [], []
    for l in range(DEPTH):
        if l % 2 == 0:
            e = l // 2
            h0 = jnp.zeros((B, D_B), jnp.float32) if prompt else hc[e]
            lb0 = jnp.zeros((B, CONV_B - 1, D_B), x.dtype) if prompt else lbc[e]
            x, nk, nv, nh, nlb = _layer_ab(x, None if prompt else kc[e], None if prompt else vc[e], h0, lb0,
                                           norm_ab[e], w_in_ab[e], w_out_ab[e], rel_bias[e], lru_conv_w[e],
                                           lru_conv_b[e], lru_w_a[e], lru_b_a[e], lru_w_x[e], lru_b_x[e],
                                           lru_lambda[e])
            ks.append(nk); vs.append(nv); hs.append(nh); lbs.append(nlb)
        else:
            o = l // 2
            cb0 = jnp.zeros((B, CONV_C - 1, D_C), x.dtype) if prompt else cbc[o]
            x, ncb = _layer_conv(x, cb0, norm_cv[o], w_in_cv[o], w_out_cv[o], dw_w[o], dw_b[o], ln_g[o], ln_b[o])
            cbs.append(ncb)
    y = _rmsnorm(x, final_norm)
    return y, jnp.stack(ks), jnp.stack(vs), jnp.stack(hs), jnp.stack(lbs), jnp.stack(cbs)


def setup_inputs(seed: int = 0) -> dict:
    key = jax.random.key(seed)
    ks = jax.random.split(key, 32)
    f32 = jnp.float32
    nrm = lambda k, s, sc: jax.random.normal(k, s, f32) * sc
    cache_len = min(BAND_PAST, PAST_LEN)
    u = jax.random.uniform(ks[16], (N_EVEN, D_B), f32, 0.9, 0.999)
    base = u ** (1.0 / LRU_C)
    return {
        "x_prompt": nrm(ks[0], (BATCH, SEQ, D_MODEL), 1.0),
        "x_sample": nrm(ks[1], (DEC_BATCH, DEC_SEQ, D_MODEL), 1.0),
        "cache_attn_k": nrm(ks[2], (N_EVEN, DEC_BATCH, cache_len, N_HEADS_A, HEAD_DIM), 1.0),
        "cache_attn_v": nrm(ks[3], (N_EVEN, DEC_BATCH, cache_len, N_HEADS_A, HEAD_DIM), 1.0),
        "state_lru_h": nrm(ks[4], (N_EVEN, DEC_BATCH, D_B), 0.5),
        "state_lru_conv": nrm(ks[5], (N_EVEN, DEC_BATCH, CONV_B - 1, D_B), 1.0),
        "state_conv": nrm(ks[6], (N_ODD, DEC_BATCH, CONV_C - 1, D_C), 0.5),
        "norm_ab": 1.0 + nrm(ks[7], (N_EVEN, D_MODEL), 0.01),
        "w_in_ab": nrm(ks[8], (N_EVEN, D_MODEL, IN_AB), D_MODEL ** -0.5),
        "w_out_ab": nrm(ks[9], (N_EVEN, D_A + D_B, D_MODEL), (D_A + D_B) ** -0.5),
        "rel_bias": nrm(ks[10], (N_EVEN, N_HEADS_A, 2 * REL_CLIP + 1), 0.1),
        "lru_conv_w": nrm(ks[11], (N_EVEN, CONV_B, D_B), CONV_B ** -0.5),
        "lru_conv_b": nrm(ks[12], (N_EVEN, D_B), 0.01),
        "lru_w_a": nrm(ks[13], (N_EVEN, N_BLOCKS_B, BLOCK_B, BLOCK_B), BLOCK_B ** -0.5),
        "lru_b_a": nrm(ks[14], (N_EVEN, D_B), 0.01),
        "lru_w_x": nrm(ks[15], (N_EVEN, N_BLOCKS_B, BLOCK_B, BLOCK_B), BLOCK_B ** -0.5),
        "lru_b_x": nrm(ks[17], (N_EVEN, D_B), 0.01),
        "lru_lambda": jnp.log(base) - jnp.log1p(-base),
        "norm_cv": 1.0 + nrm(ks[18], (N_ODD, D_MODEL), 0.01),
        "w_in_cv": nrm(ks[19], (N_ODD, D_MODEL, IN_CV), D_MODEL ** -0.5),
        "w_out_cv": nrm(ks[20], (N_ODD, D_C, D_MODEL), D_C ** -0.5),
        "dw_w": nrm(ks[21], (N_ODD, CONV_C, D_C), CONV_C ** -0.5),
        "dw_b": nrm(ks[22], (N_ODD, D_C), 0.01),
        "ln_g": 1.0 + nrm(ks[23], (N_ODD, D_C), 0.01),
        "ln_b": nrm(ks[24], (N_ODD, D_C), 0.01),
        "final_norm": 1.0 + nrm(ks[25], (D_MODEL,), 0.01),
    }


def reference(x_prompt, x_sample, cache_attn_k, cache_attn_v, state_lru_h, state_lru_conv, state_conv,
              norm_ab, w_in_ab, w_out_ab, rel_bias, lru_conv_w, lru_conv_b, lru_w_a, lru_b_a, lru_w_x,
              lru_b_x, lru_lambda, norm_cv, w_in_cv, w_out_cv, dw_w, dw_b, ln_g, ln_b, final_norm):
    weights = (norm_ab, w_in_ab, w_out_ab, rel_bias, lru_conv_w, lru_conv_b, lru_w_a, lru_b_a, lru_w_x,
               lru_b_x, lru_lambda, norm_cv, w_in_cv, w_out_cv, dw_w, dw_b, ln_g, ln_b, final_norm)
    y_prompt, k_p, v_p, h_p, lb_p, cb_p = _trunk(x_prompt, None, None, None, None, None, *weights)
    y_sample, k_s, v_s, h_s, lb_s, cb_s = _trunk(x_sample, cache_attn_k, cache_attn_v, state_lru_h,
                                                 state_lru_conv, state_conv, *weights)
    return (y_prompt, y_sample, k_p, v_p, h_p, lb_p, cb_p, k_s, v_s, h_s, lb_s, cb_s)
```

```python
import numpy as np
from contextlib import ExitStack
import concourse.bass as bass
import concourse.mybir as mybir
from concourse.bass_utils import run_bass_kernel_spmd

F32 = mybir.dt.float32
BF16 = mybir.dt.bfloat16
AF = mybir.ActivationFunctionType
ALU = mybir.AluOpType

ENGS = ("pe", "act", "dve", "pool", "sp")
import os
DEBUG = bool(os.environ.get("KDEBUG"))
DBG_COLS = 40000
DBG_MAP = {}
D = 2048
NKT = 16
TALL = 2112
QOFF = 960
NQ = 1152
EPS = 1e-6
NEG = -30000.0

PV_G0, PV_G1, PV_CW, PV_CB, PV_BA, PV_BX, PV_LAM = 0, 16, 32, 64, 72, 80, 88
PV_DWW, PV_DWB, PV_LNG, PV_LNB, PV_FLAG, PV_N = 96, 592, 608, 624, 640, 641
ST_HP, ST_HS, ST_LP, ST_LS, ST_N = 0, 8, 24, 48, 96


class Buf:
    __slots__ = ("name", "w", "r")

    def __init__(self, name=""):
        self.name = name
        self.w = None
        self.r = {}


class _Rec:
    def __getattr__(self, name):
        def f(*a, **k):
            return (name, a, k)
        return f


_REC = _Rec()


class Prog:
    NDMA = 6

    def __init__(self, nc):
        self.nc = nc
        self.q = {e: [] for e in ENGS}
        self.cnt = {e: 0 for e in ENGS}
        self.seen = {e: {} for e in ENGS}
        self.pe_pending = False
        self.dma_rr = {e: 0 for e in ENGS}
        self.dma_val = {}

    def _need(self, eng, deps):
        best = {}
        for t in deps:
            if t is None:
                continue
            key, val = t
            if key == eng and eng == "pe":
                continue
            if key == "pe":
                assert val <= self.cnt["pe"], "dependency on un-incremented PE op"
            if best.get(key, 0) < val:
                best[key] = val
        for key, val in best.items():
            if self.seen[eng].get(key, 0) >= val:
                continue
            self.seen[eng][key] = val
            self.q[eng].append(("wait", key, val))

    def _deps(self, reads, writes):
        deps = []
        for b in reads:
            deps.append(b.w)
        for b in writes:
            deps.append(b.w)
            deps.extend(b.r.values())
        return deps

    def _commit(self, ticket, reads, writes):
        k = ticket[0]
        for b in reads:
            if b.r.get(k, (k, 0))[1] < ticket[1]:
                b.r[k] = ticket
        for b in writes:
            b.w = ticket
            b.r = {}

    def op(self, eng, fn, reads=(), writes=(), inc=True):
        self._need(eng, self._deps(reads, writes))
        if inc:
            self.cnt[eng] += 1
            ticket = (eng, self.cnt[eng])
            if eng == "pe":
                self.pe_pending = False
        else:
            assert eng == "pe"
            ticket = (eng, self.cnt[eng] + 1)
            self.pe_pending = True
        self.q[eng].append(("op", fn(_REC), inc))
        self._commit(ticket, reads, writes)
        return ticket

    DESC_LIMIT = 1400

    def dma(self, queue, out, in_, reads=(), writes=(), **kw):
        idx = self.dma_rr[queue]
        self.dma_rr[queue] = (idx + 1) % self.NDMA
        key = ("dma", queue, idx)
        prev = self.dma_val.get(key, 0)
        deps = self._deps(reads, writes)
        if prev:
            deps.append((key, prev))
        if queue == "sp":
            shp = list(out.shape)
            nd = 1
            for d_ in shp[:-1]:
                nd *= d_
            nd *= (shp[-1] * 4 + 4095) // 4096
            fifo = self.__dict__.setdefault("_fifo", [])
            while fifo and sum(x[1] for x in fifo) + nd > self.DESC_LIMIT:
                deps.append(fifo.pop(0)[0])
            self._pending_nd = nd
        self._need(queue, deps)
        val = prev + 16
        self.dma_val[key] = val
        self.q[queue].append(("dma", out, in_, key, kw))
        ticket = (key, val)
        if queue == "sp":
            self._fifo.append((ticket, self._pending_nd))
        self._commit(ticket, reads, writes)
        return ticket

    def barrier(self):
        assert not self.pe_pending
        tickets = [(e, self.cnt[e]) for e in ENGS if self.cnt[e] > 0]
        tickets += [(k, v) for k, v in self.dma_val.items()]
        for e in ENGS:
            self._need(e, [t for t in tickets if t[0] != e])

    def finish(self):
        for key, val in self.dma_val.items():
            if self.seen["sp"].get(key, 0) < val:
                self.seen["sp"][key] = val
                self.q["sp"].append(("wait", key, val))

    def emit(self, stack):
        nc = self.nc
        assert not self.pe_pending
        sems = {}
        for e in ENGS:
            sems[e] = stack.enter_context(nc.semaphore("s_" + e))
        for key in self.dma_val:
            sems[key] = stack.enter_context(nc.semaphore("d_%s_%d" % (key[1], key[2])))
        block = stack.enter_context(nc.Block())
        handles = {"pe": block.tensor, "act": block.scalar, "dve": block.vector,
                   "pool": block.gpsimd, "sp": block.sync}

        def run(ename):
            items = self.q[ename]

            def body(eng):
                for it in items:
                    if it[0] == "wait":
                        eng.wait_ge(sems[it[1]], it[2])
                    elif it[0] == "op":
                        ins = getattr(eng, it[1][0])(*it[1][1], **it[1][2])
                        if it[2]:
                            ins.then_inc(sems[ename], 1)
                    else:
                        _, out, in_, key, kw = it
                        eng.dma_start(out=out, in_=in_, **kw).then_inc(sems[key], 16)
            return body

        for e in ENGS:
            if self.q[e]:
                handles[e](run(e))


ARENA_WORDS = 53000


class Arena:
    def __init__(self, ap):
        self.ap = ap
        self.top = 0

    def f32(self, n):
        na = (n + 7) // 8 * 8
        assert self.top + na <= ARENA_WORDS, ("arena overflow", self.top, na)
        v = self.ap[:, self.top:self.top + n]
        self.top += na
        return v

    def bf(self, n):
        assert n % 2 == 0
        v = self.f32(n // 2)
        return v.bitcast(BF16)


def nblocks(lo, hi, step=512):
    out = []
    c = lo
    while c < hi:
        n = min(step, hi - c)
        out.append((c, n))
        c += n
    return out


def build_program():
    nc = bass.Bass("TRN2", target_bir_lowering=False)

    def din(name, shape):
        return nc.dram_tensor(name, list(shape), F32, kind="ExternalInput").ap()

    def dout(name, shape):
        return nc.dram_tensor(name, list(shape), F32, kind="ExternalOutput").ap()

    x_ext = din("x_ext", [TALL, D])
    kc = din("kc", [2, 512, 1024])
    vc = din("vc", [2, 512, 1024])
    h0T = din("h0T", [128, 16])
    lb0T = din("lb0T", [128, 48])
    cs0T = din("cs0T", [128, 16 * 60])
    wA = din("wA", [8, D, 512])
    wB = din("wB", [8, D, 256])
    wC = din("wC", [16, D, 384])
    wo0 = din("wo0", [D, D])
    wo1 = din("wo1", [D, D])
    wax = din("wax", [128, 2 * 8 * 128])
    pvec_d = din("pvec", [128, PV_N])
    fnb_d = din("fnb", [128, D])
    biasT_d = din("biasT", [8, 128, 704])

    y_main = dout("y_main", [1024, D])
    y_samp = dout("y_samp", [64, D])
    kvo_p = dout("kvo_p", [2, 512, 1024])
    kvo_s = dout("kvo_s", [2, 64, 1024])
    sto_d = dout("sto", [128, ST_N])
    cbo_d = dout("cbo", [128, 16 * 90])
    x1_scr = nc.dram_tensor("x1_scr", [NQ, D], F32, kind="Internal").ap()
    dbg_d = dout("dbg", [128, DBG_COLS]) if DEBUG else None
    DBG_MAP.clear()
    dbg_state = {"off": 0}

    def dbg(name, ap, ncols, buf, nrows=128):
        if not DEBUG or name in DBG_MAP:
            return
        o = dbg_state["off"]
        DBG_MAP[name] = (o, ncols, nrows)
        dbg_state["off"] = o + ncols
        p.dma("sp", dbg_d[0:nrows, o:o + ncols], ap, reads=[buf])

    st = ExitStack()
    with st:
        arena_t = st.enter_context(nc.sbuf_tensor("arena", [128, ARENA_WORDS], F32))
        psT = [st.enter_context(nc.psum_tensor("psT%d" % i, [128, 1024], BF16)) for i in range(2)]
        psF = [st.enter_context(nc.psum_tensor("psF%d" % i, [128, 512], F32)) for i in range(6)]
        psT_b = [Buf("psT%d" % i) for i in range(2)]
        psF_b = [Buf("psF%d" % i) for i in range(6)]
        p = Prog(nc)
        ar = Arena(arena_t[:])
        rr = {"T": 0, "F": 0}

        def psum_f():
            i = rr["F"]
            rr["F"] = (i + 1) % 6
            return psF[i][:], psF_b[i]

        def psum_t():
            i = rr["T"]
            rr["T"] = (i + 1) % 2
            return psT[i][:], psT_b[i]

        pvec = ar.f32(PV_N)
        b_pvec = Buf("pvec")
        fnb = ar.f32(D)
        b_fnb = Buf("fnb")
        identf = ar.f32(128)
        ident = ar.bf(128)
        ones_bf = ar.bf(128)
        flag_bf = ar.bf(128)
        onesdiv = ar.bf(128)
        c8 = ar.f32(8)
        c16 = ar.f32(8)
        lsc = ar.f32(64)
        sto = ar.f32(ST_N)
        b_const = Buf("const")
        b_sto = Buf("sto")
        p.dma("sp", pvec, pvec_d, writes=[b_pvec])
        p.dma("sp", fnb, fnb_d, writes=[b_fnb])
        p.op("pool", lambda e: e.memset(identf, 0.0), writes=[b_const])
        p.op("pool", lambda e: e.affine_select(identf, identf, [[-1, 128]], ALU.not_equal, 1.0,
                                               base=0, channel_multiplier=1),
             reads=[b_const], writes=[b_const])
        p.op("dve", lambda e: e.tensor_copy(ident, identf), reads=[b_const], writes=[b_const])
        p.op("dve", lambda e: e.memset(ones_bf, 1.0), writes=[b_const])
        p.op("dve", lambda e: e.memset(onesdiv, 1.0 / D), writes=[b_const])
        p.op("dve", lambda e: e.memset(sto, 0.0), writes=[b_sto])
        flag = pvec[:, PV_FLAG:PV_FLAG + 1]
        p.op("dve", lambda e: e.tensor_scalar(flag_bf, ones_bf, flag, None, ALU.mult),
             reads=[b_const, b_pvec], writes=[b_const])
        lam = pvec[:, PV_LAM:PV_LAM + 8]
        t_abs, t_y, t_w, t_w2, t_s, t_m = (lsc[:, 8 * i:8 * i + 8] for i in range(6))
        RW = dict(reads=[b_const, b_pvec], writes=[b_const])
        p.op("dve", lambda e: e.tensor_scalar(t_abs, lam, -1.0, None, ALU.mult), **RW)
        p.op("dve", lambda e: e.tensor_tensor(t_abs, t_abs, lam, ALU.max), **RW)
        p.op("act", lambda e: e.activation(t_y, t_abs, AF.Exp, scale=-1.0), **RW)
        p.op("dve", lambda e: e.tensor_scalar(t_w, t_y, 2.0, None, ALU.add), **RW)
        p.op("dve", lambda e: e.reciprocal(t_w, t_w), **RW)
        p.op("dve", lambda e: e.tensor_tensor(t_w, t_w, t_y, ALU.mult), **RW)
        p.op("dve", lambda e: e.tensor_tensor(t_w2, t_w, t_w, ALU.mult), **RW)
        p.op("dve", lambda e: e.memset(t_s, 1.0 / 15.0), **RW)
        for kk in (13, 11, 9, 7, 5, 3, 1):
            p.op("dve", lambda e: e.tensor_tensor(t_s, t_s, t_w2, ALU.mult), **RW)
            p.op("dve", (lambda cst: (lambda e: e.tensor_scalar(t_s, t_s, cst, None, ALU.add)))(1.0 / kk), **RW)
        p.op("dve", lambda e: e.tensor_tensor(t_s, t_s, t_w, ALU.mult), **RW)
        p.op("dve", lambda e: e.tensor_scalar(t_m, lam, 0.0, None, ALU.min), **RW)
        p.op("dve", lambda e: e.scalar_tensor_tensor(t_s, t_s, -2.0, t_m, ALU.mult, ALU.add), **RW)
        p.op("dve", lambda e: e.tensor_scalar(c8, t_s, 8.0, None, ALU.mult), **RW)
        p.op("dve", lambda e: e.tensor_scalar(c16, t_s, 16.0, None, ALU.mult), **RW)

        base_top = ar.top

        xnT = ar.bf(NKT * TALL).rearrange("p (k c) -> p k c", c=TALL)
        b_xnT = [Buf("xnT%d" % t) for t in range(17)]
        mixT = ar.bf(NKT * NQ).rearrange("p (k c) -> p k c", c=NQ)
        b_mix = [Buf("mix%d" % k) for k in range(NKT)]
        l0_top = ar.top

        def xn_bufs(c0, n):
            return [b_xnT[t] for t in range(c0 // 128, (c0 + n - 1) // 128 + 1)]

        def rms_part1(nrows, xt, b_xt, xs, b_xs, ss, rstd, b_s):
            p.op("act", lambda e: e.activation(xs[0:nrows, :], xt[0:nrows, :], AF.Square, accum_out=ss[0:nrows, :]),
                 reads=[b_xt], writes=[b_xs, b_s])
            p.op("act", lambda e: e.activation(rstd[0:nrows, :], ss[0:nrows, :], AF.Sqrt, scale=1.0 / D, bias=EPS),
                 reads=[b_s], writes=[b_s])
            p.op("dve", lambda e: e.reciprocal(rstd[0:nrows, :], rstd[0:nrows, :]), reads=[b_s], writes=[b_s])
            p.op("act", lambda e: e.activation(xs[0:nrows, :], xt[0:nrows, :], AF.Copy, scale=rstd[0:nrows, :]),
                 reads=[b_xt, b_s], writes=[b_xs])

        def rms_part2(nrows, xs, b_xs, gcol, dstT, col0, b_dst):
            for half in range(2):
                pt, bpt = psum_t()
                for i in range(8):
                    kt = half * 8 + i
                    p.op("pe", lambda e: e.transpose(
                        pt[:, i * 128:i * 128 + nrows], xs[0:nrows, kt * 128:(kt + 1) * 128],
                        ident[0:nrows, 0:nrows]),
                        reads=[b_xs, b_const], writes=[bpt], inc=(i == 7))
                gb = pvec[:, gcol + half * 8:gcol + half * 8 + 8].unsqueeze(2).to_broadcast([128, 8, nrows])
                src = pt[:, 0:1024].rearrange("p (k c) -> p k c", c=128)[:, :, 0:nrows]
                p.op("dve", lambda e: e.tensor_tensor(dstT[:, half * 8:half * 8 + 8, col0:col0 + nrows], src, gb,
                                                      ALU.mult),
                     reads=[bpt, b_pvec], writes=[b_dst])

        wAb = [ar.bf(NKT * 512).rearrange("p (k c) -> p k c", c=512) for _ in range(2)]
        b_wA = [Buf() for _ in range(2)]
        for h_ in range(2):
            for g in range(4):
                p.dma("pool", wAb[h_][:, 4 * g:4 * g + 4, :],
                      wA[h_, 512 * g:512 * (g + 1), :].rearrange("(k p) c -> p k c", p=128),
                      writes=[b_wA[h_]])
        NXT = 4
        xt_ = [ar.f32(D) for _ in range(NXT)]
        xs_ = [ar.bf(D) for _ in range(2)]
        sst = [ar.f32(8) for _ in range(2)]
        b_xt = [Buf() for _ in range(NXT)]
        b_xs = [Buf() for _ in range(2)]
        b_ss = [Buf() for _ in range(2)]
        pend = None
        for t in range(17):
            nrows = 128 if t < 16 else 64
            i = t % 2
            ix = t % NXT
            p.dma("sp", xt_[ix][0:nrows, :], x_ext[t * 128:t * 128 + nrows, :], writes=[b_xt[ix]])
            rms_part1(nrows, xt_[ix], b_xt[ix], xs_[i], b_xs[i], sst[i][:, 0:1], sst[i][:, 1:2], b_ss[i])
            if pend is not None:
                rms_part2(*pend)
            pend = (nrows, xs_[i], b_xs[i], PV_G0, xnT, t * 128, b_xnT[t])
        rms_part2(*pend)
        p.barrier()
        ar.top = l0_top

        wAb = [ar.bf(NKT * 512).rearrange("p (k c) -> p k c", c=512) for _ in range(2)]
        HB = []
        for _par in range(2):
            hb = dict(
                QT=ar.bf(1216), KT=ar.bf(1728), sga=ar.bf(NQ),
                Vt=ar.bf(14 * 128).rearrange("p (t d) -> p t d", d=128),
                KcT=ar.bf(1024).rearrange("p (s k) -> p s k", k=512),
                Vc=ar.bf(1024).rearrange("p (s j d) -> p s j d", s=2, j=4),
                biasb=ar.f32(704),
                b_QT=Buf(), b_KT=Buf(), b_sga=Buf(), b_Vt=Buf(), b_KcT=Buf(), b_Vc=Buf(), b_bias=Buf())
            HB.append(hb)
        kst_g = [ar.f32(512).rearrange("p (j d) -> p j d", d=128) for _ in range(2)]
        b_kst_g = [Buf(), Buf()]
        for hb in HB:
            hb["kst"] = kst_g
            hb["b_kst"] = b_kst_g
        st32 = [ar.f32(512) for _ in range(2)]
        b_st32 = [Buf() for _ in range(2)]
        tnh = [ar.f32(512) for _ in range(2)]
        b_tnh = [Buf() for _ in range(2)]
        Ssb = [ar.f32(640) for _ in range(2)]
        PT = [ar.bf(640) for _ in range(2)]
        b_Ssb = [Buf() for _ in range(2)]
        b_PT = [Buf() for _ in range(2)]
        rec = [ar.f32(128) for _ in range(2)]
        otmp = [ar.f32(128) for _ in range(2)]
        b_rec = [Buf() for _ in range(2)]
        kvst = [ar.f32(128) for _ in range(2)]
        b_kvst = [Buf() for _ in range(2)]
        rr_att = {"i": 0, "st": 0, "kv": 0, "tn": 0}

        def load_wA(h):
            for g in range(4):
                p.dma("pool", wAb[h % 2][:, 4 * g:4 * g + 4, :],
                      wA[h, 512 * g:512 * (g + 1), :].rearrange("(k p) c -> p k c", p=128),
                      writes=[b_wA[h % 2]])

        def proj(wt, b_w, ccol, c0, n, ps):
            rd = [b_w] + xn_bufs(c0, n)
            for kt in range(NKT):
                p.op("pe", lambda e: e.matmul(ps[0][:, 0:n], wt[:, kt, ccol:ccol + 128], xnT[:, kt, c0:c0 + n],
                                              start=(kt == 0), stop=(kt == NKT - 1)),
                     reads=rd, writes=[ps[1]], inc=(kt == NKT - 1))

        def attn_S(hb, h, qa, nq, keytiles, mixcol, qlo, last64):
            i = rr_att["i"]
            rr_att["i"] = (i + 1) % 2
            QT, biasb = hb["QT"], hb["biasb"]
            psA = psum_f()
            psB = psum_f()
            rK = [hb["b_QT"], hb["b_KT"], hb["b_KcT"]]
            for j in range(4):
                p.op("pe", lambda e: e.matmul(psA[0][:, j * 128 + qlo:j * 128 + qlo + nq], keytiles[j][0],
                                              QT[:, qa:qa + nq], start=True, stop=True),
                     reads=rK, writes=[psA[1]], inc=(j == 3))
            kr = 64 if last64 else 128
            p.op("pe", lambda e: e.matmul(psB[0][0:kr, qlo:qlo + nq], keytiles[4][0], QT[:, qa:qa + nq],
                                          start=True, stop=True), reads=rK, writes=[psB[1]])
            S, P_ = Ssb[i], PT[i]
            S4 = S[:, 0:512].rearrange("p (j q) -> p j q", q=128)[:, :, qlo:qlo + nq]
            A4 = psA[0][:, 0:512].rearrange("p (j q) -> p j q", q=128)[:, :, qlo:qlo + nq]
            B4 = biasb[:, 0:512].rearrange("p (j q) -> p j q", q=128)[:, :, qlo:qlo + nq]
            P4 = P_[:, 0:512].rearrange("p (j q) -> p j q", q=128)[:, :, qlo:qlo + nq]
            p.op("dve", lambda e: e.tensor_tensor(S4, A4, B4, ALU.add), reads=[psA[1], hb["b_bias"]],
                 writes=[b_Ssb[i]])
            if last64:
                s_idx = keytiles[4][3]
                bsl = biasb[0:64, 640 + 32 * s_idx:640 + 32 * s_idx + 32]
            else:
                bsl = biasb[:, 512 + qlo:512 + qlo + nq]
            p.op("dve", lambda e: e.tensor_tensor(S[0:kr, 512 + qlo:512 + qlo + nq], psB[0][0:kr, qlo:qlo + nq], bsl,
                                                  ALU.add), reads=[psB[1], hb["b_bias"]], writes=[b_Ssb[i]])
            p.op("act", lambda e: e.activation(P4, S4, AF.Exp), reads=[b_Ssb[i]], writes=[b_PT[i]])
            p.op("act", lambda e: e.activation(P_[0:kr, 512 + qlo:512 + qlo + nq], S[0:kr, 512 + qlo:512 + qlo + nq],
                                               AF.Exp), reads=[b_Ssb[i]], writes=[b_PT[i]])
            return (hb, h, i, nq, keytiles, mixcol, qlo, kr)

        def attn_V(state):
            hb, h, i, nq, keytiles, mixcol, qlo, kr = state
            P_ = PT[i]
            psO = psum_f()
            rV = [b_PT[i], hb["b_Vt"], hb["b_Vc"], b_const]
            for j in range(5):
                k_ = kr if j == 4 else 128
                p.op("pe", lambda e: e.matmul(
                    psO[0][:, 0:nq], keytiles[j][1], P_[0:k_, j * 128 + qlo:j * 128 + qlo + nq],
                    start=(j == 0), stop=(j == 4)), reads=rV, writes=[psO[1]], inc=False)
            for j in range(5):
                k_ = kr if j == 4 else 128
                p.op("pe", lambda e: e.matmul(
                    psO[0][:, 128:128 + nq], keytiles[j][2], P_[0:k_, j * 128 + qlo:j * 128 + qlo + nq],
                    start=(j == 0), stop=(j == 4)), reads=rV, writes=[psO[1]], inc=(j == 4))
            r_, o_ = rec[i], otmp[i]
            p.op("dve", lambda e: e.tensor_scalar(r_[:, 0:nq], psO[0][:, 128:128 + nq], 2.0, 1e-30,
                                                  ALU.mult, ALU.max), reads=[psO[1]], writes=[b_rec[i]])
            p.op("dve", lambda e: e.reciprocal(r_[:, 0:nq], r_[:, 0:nq]), reads=[b_rec[i]], writes=[b_rec[i]])
            p.op("dve", lambda e: e.tensor_tensor(o_[:, 0:nq], psO[0][:, 0:nq], r_[:, 0:nq], ALU.mult),
                 reads=[psO[1], b_rec[i]], writes=[b_rec[i]])
            p.op("dve", lambda e: e.tensor_tensor(mixT[:, h, mixcol:mixcol + nq], o_[:, 0:nq],
                                                  hb["sga"][:, mixcol:mixcol + nq], ALU.mult),
                 reads=[b_rec[i], hb["b_sga"]], writes=[b_mix[h]])

        def p_tasks(h):
            hb = HB[h % 2]
            wt, b_w = wAb[h % 2], b_wA[h % 2]
            QT, KT, sga, Vt, KcT, Vc = hb["QT"], hb["KT"], hb["sga"], hb["Vt"], hb["KcT"], hb["Vc"]
            tasks = []

            def t_dmas():
                p.dma("sp", hb["biasb"], biasT_d[h], writes=[hb["b_bias"]])
                for s_ in range(2):
                    p.dma("sp", hb["kst"][s_],
                          kc[s_, :, h * 128:(h + 1) * 128].rearrange("(j p) d -> p j d", p=128),
                          writes=[hb["b_kst"][s_]])
                    p.dma("pool", Vc[:, s_, :, :],
                          vc[s_, :, h * 128:(h + 1) * 128].rearrange("(j p) d -> p j d", p=128),
                          writes=[hb["b_Vc"]])
            tasks.append(t_dmas)

            def t_q(c0, n):
                ps = psum_f()
                proj(wt, b_w, 0, c0, n, ps)
                p.op("act", lambda e: e.activation(QT[:, c0 - 896:c0 - 896 + n], ps[0][:, 0:n], AF.Copy,
                                                   scale=128.0 ** -0.5), reads=[ps[1]], writes=[hb["b_QT"]])
            for (c0, n) in nblocks(896, TALL):
                tasks.append(lambda c0=c0, n=n: t_q(c0, n))

            def t_ga(c0, n):
                ps = psum_f()
                proj(wt, b_w, 384, c0, n, ps)
                ti = rr_att["tn"]
                rr_att["tn"] = (ti + 1) % 2
                p.op("act", lambda e: e.activation(tnh[ti][:, 0:n], ps[0][:, 0:n], AF.Tanh, scale=0.5),
                     reads=[ps[1]], writes=[b_tnh[ti]])
                p.op("dve", lambda e: e.scalar_tensor_tensor(sga[:, c0 - QOFF:c0 - QOFF + n], tnh[ti][:, 0:n], 1.0,
                                                             ps[0][:, 0:n], ALU.add, ALU.mult),
                     reads=[ps[1], b_tnh[ti]], writes=[hb["b_sga"]])
            for (c0, n) in nblocks(QOFF, TALL):
                tasks.append(lambda c0=c0, n=n: t_ga(c0, n))

            def t_kc(s_):
                ps = psum_f()
                for j in range(4):
                    p.op("pe", lambda e: e.transpose(ps[0][:, j * 128:(j + 1) * 128], hb["kst"][s_][:, j, :], identf),
                         reads=[hb["b_kst"][s_], b_const], writes=[ps[1]], inc=(j == 3))
                p.op("act", lambda e: e.activation(KcT[:, s_, :], ps[0][:, 0:512], AF.Copy),
                     reads=[ps[1]], writes=[hb["b_KcT"]])
            tasks.append(lambda: t_kc(0))
            tasks.append(lambda: t_kc(1))

            def kv_proj(which, c0, n):
                ps = psum_f()
                proj(wt, b_w, 128 * which, c0, n, ps)
                need32 = (which == 2) or (c0 + n > 1536)
                si = None
                if which == 1:
                    p.op("act", lambda e: e.activation(KT[:, c0 - 384:c0 - 384 + n], ps[0][:, 0:n], AF.Copy),
                         reads=[ps[1]], writes=[hb["b_KT"]])
                if need32:
                    si = rr_att["st"]
                    rr_att["st"] = (si + 1) % 2
                    eng = "dve" if which == 1 else "act"
                    if eng == "act":
                        p.op("act", lambda e: e.activation(st32[si][:, 0:n], ps[0][:, 0:n], AF.Copy),
                             reads=[ps[1]], writes=[b_st32[si]])
                    else:
                        p.op("dve", lambda e: e.tensor_copy(st32[si][:, 0:n], ps[0][:, 0:n]),
                             reads=[ps[1], hb["b_KT"]], writes=[b_st32[si]])
                return si

            def kv_post(which, c0, n, si):
                if si is None:
                    return
                s32, bs32 = st32[si], b_st32[si]
                if which == 2:
                    pv = psum_f()
                    tiles = nblocks(0, n, 128)
                    for ti, (o, m_) in enumerate(tiles):
                        p.op("pe", lambda e: e.transpose(pv[0][0:m_, ti * 128:(ti + 1) * 128], s32[:, o:o + m_],
                                                         identf),
                             reads=[bs32, b_const], writes=[pv[1]], inc=(ti == len(tiles) - 1))
                    vt0 = (c0 - 384) // 128
                    nfull = sum(1 for (_, m_) in tiles if m_ == 128)
                    if nfull:
                        p.op("dve", lambda e: e.tensor_copy(
                            Vt[:, vt0:vt0 + nfull, :], pv[0][:, 0:128 * nfull].rearrange("p (t d) -> p t d", d=128)),
                            reads=[pv[1]], writes=[hb["b_Vt"]])
                    if nfull < len(tiles):
                        p.op("dve", lambda e: e.tensor_copy(Vt[0:64, vt0 + nfull, :],
                                                            pv[0][0:64, 128 * nfull:128 * nfull + 128]),
                             reads=[pv[1]], writes=[hb["b_Vt"]])
                for (o, m_) in nblocks(0, n, 128):
                    ta = c0 + o
                    if ta < 1536:
                        continue
                    pk = psum_f()
                    p.op("pe", lambda e: e.transpose(pk[0][0:m_, 0:128], s32[:, o:o + m_], identf),
                         reads=[bs32, b_const], writes=[pk[1]])
                    ki = rr_att["kv"]
                    rr_att["kv"] = (ki + 1) % 2
                    p.op("act", lambda e: e.activation(kvst[ki][0:m_, 0:128], pk[0][0:m_, 0:128], AF.Copy),
                         reads=[pk[1]], writes=[b_kvst[ki]])
                    if ta < 2048:
                        dst = kvo_p[which - 1, ta - 1536:ta - 1536 + m_, h * 128:(h + 1) * 128]
                    else:
                        dst = kvo_s[which - 1, 0:m_, h * 128:(h + 1) * 128]
                    p.dma("sp", dst, kvst[ki][0:m_, 0:128], reads=[b_kvst[ki]])

            kvb = [(w_, c0, n) for w_ in (1, 2) for (c0, n) in nblocks(384, TALL)]
            state = {}

            def t_kv(k):
                w_, c0, n = kvb[k]
                state[k] = kv_proj(w_, c0, n)
                if k >= 1:
                    w2, c2, n2 = kvb[k - 1]
                    kv_post(w2, c2, n2, state[k - 1])
                if k == len(kvb) - 1:
                    kv_post(w_, c0, n, state[k])
            for k in range(len(kvb)):
                tasks.append(lambda k=k: t_kv(k))
            return tasks

        def a_tasks(h):
            hb = HB[h % 2]
            KT, Vt, KcT, Vc = hb["KT"], hb["Vt"], hb["KcT"], hb["Vc"]
            blocks = []
            for m in range(7, 16):
                qlo = 64 if m == 7 else 0
                nq = 128 - qlo
                kts = []
                for j in range(5):
                    kt_ = m - 4 + j
                    kcol = (kt_ - 3) * 128
                    kts.append((KT[:, kcol:kcol + 128], Vt[:, kt_ - 3, :], flag_bf if kt_ < 8 else ones_bf))
                blocks.append((hb, h, m * 128 + qlo - 896, nq, kts, m * 128 + qlo - QOFF, qlo, False))
            for s_ in range(2):
                kts = []
                for j in range(4):
                    kts.append((KcT[:, s_, j * 128:(j + 1) * 128], Vc[:, s_, j, :], ones_bf))
                kts.append((KT[:, 2048 - 384:2112 - 384], Vt[0:64, 13, :], ones_bf[0:64, :], s_))
                blocks.append((hb, h, 2048 + 32 * s_ - 896, 32, kts, 2048 + 32 * s_ - QOFF, 0, True))
            st_ = {}
            tasks = []

            def t_a(k):
                if k < len(blocks):
                    st_[k] = attn_S(*blocks[k])
                if k >= 1:
                    attn_V(st_[k - 1])
            for k in range(len(blocks) + 1):
                tasks.append(lambda k=k: t_a(k))
            return tasks

        wBb_pre = [arena_t[:][:, l0_top + 2048 * i_:l0_top + 2048 * (i_ + 1)].bitcast(BF16)
                   .rearrange("p (k c) -> p k c", c=256) for i_ in range(2)]
        b_wB = [Buf() for _ in range(2)]
        for t in p_tasks(0):
            t()
        for h in range(8):
            A = a_tasks(h)
            Pn = p_tasks(h + 1) if h + 1 < 8 else []
            if h + 2 < 8:
                load_wA(h + 2)
            if h == 6:
                for i_ in range(2):
                    for g in range(4):
                        p.dma("pool", wBb_pre[i_][:, 4 * g:4 * g + 4, :],
                              wB[i_, 512 * g:512 * (g + 1), :].rearrange("(k p) c -> p k c", p=128),
                              writes=[b_wB[i_], b_wA[0]])
            na, npn = len(A), len(Pn)
            ia = 0
            for ip in range(npn):
                Pn[ip]()
                want = ((ip + 1) * na) // npn
                while ia < want:
                    A[ia]()
                    ia += 1
            while ia < na:
                A[ia]()
                ia += 1
        p.barrier()
        ar.top = l0_top

        XP0, XSA, XSB = 3, 2054, 2089
        wBb = [ar.bf(NKT * 256).rearrange("p (k c) -> p k c", c=256) for _ in range(2)]
        assert NKT * 256 // 2 == 2048
        waxb = ar.bf(2 * 8 * 128).rearrange("p (g n e) -> p g n e", g=2, n=8)
        b_wax = Buf()
        h0sb = ar.f32(16)
        b_h0 = Buf()
        xb_ = [ar.f32(2128) for _ in range(2)]
        xc = ar.f32(TALL)
        xcb = ar.bf(TALL)
        rg = ar.f32(TALL)
        ig = ar.f32(TALL)
        bb = ar.f32(TALL)
        aa = xc
        hh = rg
        sgb_ = [ar.f32(NQ) for _ in range(2)]
        hin = ar.f32(8)
        b_xc, b_xcb, b_rg, b_ig, b_bb, b_hin = (Buf() for _ in range(6))
        b_xb_ = [Buf() for _ in range(2)]
        b_sgb_ = [Buf() for _ in range(2)]
        b_aa, b_hh = b_xc, b_rg
        p.dma("pool", waxb, wax.rearrange("p (g n e) -> p g n e", g=2, n=8), writes=[b_wax])
        p.dma("sp", h0sb, h0T, writes=[b_h0])

        def load_wB(n_):
            for g in range(4):
                p.dma("pool", wBb[n_ % 2][:, 4 * g:4 * g + 4, :],
                      wB[n_, 512 * g:512 * (g + 1), :].rearrange("(k p) c -> p k c", p=128),
                      writes=[b_wB[n_ % 2]])

        def xbcol(c):
            if c < 2048:
                return XP0 + c
            if c < 2080:
                return XSA + (c - 2048)
            return XSB + (c - 2080)

        def lru_P(n_):
            wt, b_w = wBb[n_ % 2], b_wB[n_ % 2]
            xb, b_xb = xb_[n_ % 2], b_xb_[n_ % 2]
            sgb, b_sgb = sgb_[n_ % 2], b_sgb_[n_ % 2]
            p.op("pool", lambda e: e.memset(xb[:, 0:3], 0.0), writes=[b_xb])
            lbv = lb0T[:, n_ * 6:n_ * 6 + 6]
            p.dma("sp", xb[:, XSA - 3:XSA], lbv[:, 0:3], writes=[b_xb])
            p.dma("sp", xb[:, XSB - 3:XSB], lbv[:, 3:6], writes=[b_xb])
            for (c0, n) in nblocks(0, TALL):
                ps = psum_f()
                proj(wt, b_w, 0, c0, n, ps)
                if c0 < 2048:
                    p.op("act", lambda e: e.activation(xb[:, XP0 + c0:XP0 + c0 + n], ps[0][:, 0:n], AF.Copy),
                         reads=[ps[1]], writes=[b_xb])
                else:
                    p.op("act", lambda e: e.activation(xb[:, XSA:XSA + 32], ps[0][:, 0:32], AF.Copy),
                         reads=[ps[1]], writes=[b_xb])
                    p.op("act", lambda e: e.activation(xb[:, XSB:XSB + 32], ps[0][:, 32:64], AF.Copy),
                         reads=[ps[1]], writes=[b_xb])
            for (c0, n) in nblocks(QOFF, TALL):
                ps = psum_f()
                proj(wt, b_w, 128, c0, n, ps)
                p.op("act", lambda e: e.activation(sgb[:, c0 - QOFF:c0 - QOFF + n], ps[0][:, 0:n], AF.Silu),
                     reads=[ps[1]], writes=[b_sgb])

        b_xc2 = [Buf(), Buf()]
        b_xcb2 = [Buf(), Buf()]
        b_rg2 = [Buf(), Buf()]
        b_ig2 = [Buf(), Buf()]
        b_bb2 = [Buf(), Buf()]
        HALF = ((0, 1024), (1024, TALL))

        def lru_C(n_):
            xb, b_xb = xb_[n_ % 2], b_xb_[n_ % 2]
            sgb, b_sgb = sgb_[n_ % 2], b_sgb_[n_ % 2]
            cw = [pvec[:, PV_CW + n_ * 4 + j:PV_CW + n_ * 4 + j + 1] for j in range(4)]
            cbv = pvec[:, PV_CB + n_:PV_CB + n_ + 1]
            ba = pvec[:, PV_BA + n_:PV_BA + n_ + 1]
            bx = pvec[:, PV_BX + n_:PV_BX + n_ + 1]
            c8n = c8[:, n_:n_ + 1]
            c16n = c16[:, n_:n_ + 1]
            runs = (((XP0 - 3, 0, 1024),), ((XP0 - 3 + 1024, 1024, 1024), (XSA - 3, 2048, 32), (XSB - 3, 2080, 32)))
            for k in range(2):
                for (xo, co, ln) in runs[k]:
                    p.op("dve", lambda e: e.tensor_scalar(xc[:, co:co + ln], xb[:, xo:xo + ln], cw[0], cbv,
                                                          ALU.mult, ALU.add), reads=[b_xb, b_pvec], writes=[b_xc2[k]])
                    for j in range(1, 4):
                        p.op("dve", lambda e: e.scalar_tensor_tensor(
                            xc[:, co:co + ln], xb[:, xo + j:xo + j + ln], cw[j], xc[:, co:co + ln],
                            ALU.mult, ALU.add), reads=[b_xb, b_pvec, b_xc2[k]], writes=[b_xc2[k]])
            for k in range(2):
                a, b = HALF[k]
                p.op("act", lambda e: e.activation(xcb[:, a:b], xc[:, a:b], AF.Copy),
                     reads=[b_xc2[k]], writes=[b_xcb2[k]])
            p.op("pool", lambda e: e.tensor_copy(sto[:, ST_LP + 3 * n_:ST_LP + 3 * n_ + 3],
                                                 xb[:, XP0 + 2045:XP0 + 2048]), reads=[b_xb], writes=[b_sto])
            for s_, xs0 in ((0, XSA), (1, XSB)):
                p.op("pool", lambda e: e.tensor_copy(
                    sto[:, ST_LS + 6 * n_ + 3 * s_:ST_LS + 6 * n_ + 3 * s_ + 3], xb[:, xs0 + 29:xs0 + 32]),
                    reads=[b_xb], writes=[b_sto])
            for k in range(2):
                a, b = HALF[k]
                for g_, dst, b_dst, bias_ in ((0, rg, b_rg2[k], ba), (1, ig, b_ig2[k], bx)):
                    for (c0, n) in nblocks(a, b):
                        ps = psum_f()
                        p.op("pe", lambda e: e.matmul(ps[0][:, 0:n], waxb[:, g_, n_, :], xcb[:, c0:c0 + n],
                                                      start=True, stop=True), reads=[b_wax, b_xcb2[k]], writes=[ps[1]])
                        p.op("act", lambda e: e.activation(dst[:, c0:c0 + n], ps[0][:, 0:n], AF.Sigmoid, bias=bias_),
                             reads=[ps[1], b_pvec], writes=[b_dst])
            for k in range(2):
                a, b = HALF[k]
                p.op("dve", lambda e: e.tensor_tensor(ig[:, a:b], ig[:, a:b], xc[:, a:b], ALU.mult),
                     reads=[b_ig2[k], b_xc2[k]], writes=[b_ig2[k]])
            for k in range(2):
                a, b = HALF[k]
                p.op("act", lambda e: e.activation(bb[:, a:b], rg[:, a:b], AF.Exp, scale=c16n),
                     reads=[b_rg2[k], b_const], writes=[b_bb2[k]])
                p.op("act", lambda e: e.activation(aa[:, a:b], rg[:, a:b], AF.Exp, scale=c8n),
                     reads=[b_rg2[k], b_const, b_ig2[k]], writes=[b_xc2[k]])
            for k in range(2):
                a, b = HALF[k]
                p.op("act", lambda e: e.activation(bb[:, a:b], bb[:, a:b], AF.Sqrt, scale=-1.0, bias=1.0),
                     reads=[b_bb2[k]], writes=[b_bb2[k]])
            for k in range(2):
                a, b = HALF[k]
                p.op("dve", lambda e: e.tensor_tensor(bb[:, a:b], bb[:, a:b], ig[:, a:b], ALU.mult),
                     reads=[b_bb2[k], b_ig2[k]], writes=[b_bb2[k]])
            p.op("dve", lambda e: e.tensor_tensor_scan(hh[:, 0:1024], aa[:, 0:1024], bb[:, 0:1024], 0.0,
                                                       ALU.mult, ALU.add),
                 reads=[b_xc2[0], b_bb2[0]], writes=[b_rg2[0]])
            p.op("dve", lambda e: e.tensor_tensor(mixT[:, 8 + n_, 0:64], hh[:, QOFF:1024], sgb[:, 0:64], ALU.mult),
                 reads=[b_rg2[0], b_sgb], writes=[b_mix[8 + n_]])
            p.op("dve", lambda e: e.tensor_tensor(hin[:, 0:1], hh[:, 1023:1024], flag, ALU.mult),
                 reads=[b_rg2[0], b_pvec], writes=[b_hin])
            p.op("dve", lambda e: e.tensor_tensor_scan(hh[:, 1024:2048], aa[:, 1024:2048], bb[:, 1024:2048],
                                                       hin[:, 0:1], ALU.mult, ALU.add),
                 reads=[b_xc2[1], b_bb2[1], b_hin], writes=[b_rg2[1]])
            for s_ in range(2):
                c0 = 2048 + 32 * s_
                p.op("dve", lambda e: e.tensor_tensor_scan(
                    hh[:, c0:c0 + 32], aa[:, c0:c0 + 32], bb[:, c0:c0 + 32],
                    h0sb[:, 2 * n_ + s_:2 * n_ + s_ + 1], ALU.mult, ALU.add),
                    reads=[b_xc2[1], b_bb2[1], b_h0], writes=[b_rg2[1]])
            p.op("dve", lambda e: e.tensor_tensor(mixT[:, 8 + n_, 64:NQ], hh[:, 1024:TALL], sgb[:, 64:NQ], ALU.mult),
                 reads=[b_rg2[1], b_sgb], writes=[b_mix[8 + n_]])
            p.op("pool", lambda e: e.tensor_copy(sto[:, ST_HP + n_:ST_HP + n_ + 1], hh[:, 2047:2048]),
                 reads=[b_rg2[1]], writes=[b_sto])
            for s_ in range(2):
                p.op("pool", lambda e: e.tensor_copy(sto[:, ST_HS + 2 * n_ + s_:ST_HS + 2 * n_ + s_ + 1],
                                                     hh[:, 2079 + 32 * s_:2080 + 32 * s_]),
                     reads=[b_rg2[1]], writes=[b_sto])

        wo_l0 = arena_t[:][:, base_top:base_top + (NKT * D) // 2].bitcast(BF16).rearrange("p (k c) -> p k c", c=D)
        b_wo_l0 = Buf()
        lru_P(0)
        for n_ in range(8):
            if n_ + 1 < 8:
                lru_P(n_ + 1)
            if n_ + 2 < 8:
                load_wB(n_ + 2)
            if n_ == 6:
                for g in range(8):
                    p.dma("pool", wo_l0[:, 2 * g:2 * g + 2, :],
                          wo0[256 * g:256 * (g + 1), :].rearrange("(k p) c -> p k c", p=128),
                          writes=[b_wo_l0] + b_xnT)
            lru_C(n_)
        p.dma("sp", sto_d, sto, reads=[b_sto])
        p.barrier()
        ar.top = base_top

        WC0_OFF = ARENA_WORDS - 3072 - 8
        wC0_top = arena_t[:][:, WC0_OFF:WC0_OFF + 3072].bitcast(BF16).rearrange("p (k c) -> p k c", c=384)
        b_wC = [Buf() for _ in range(2)]
        for g in range(4):
            p.dma("pool", wC0_top[:, 4 * g:4 * g + 4, :],
                  wC[0, 512 * g:512 * (g + 1), :].rearrange("(k p) c -> p k c", p=128), writes=[b_wC[0]])
        ar.top = base_top
        wo = ar.bf(NKT * D).rearrange("p (k c) -> p k c", c=D)
        assert ar.top <= base_top + (NKT * TALL) // 2
        ar.top = l0_top
        xn1T = ar.bf(NKT * NQ).rearrange("p (k c) -> p k c", c=NQ)
        xn1_end = ar.top
        b_xn1 = [Buf() for _ in range(9)]
        xres_ = [ar.f32(D) for _ in range(2)]
        x1t_ = [ar.f32(D) for _ in range(2)]
        xs1_ = [ar.bf(D) for _ in range(2)]
        ss1_ = [ar.f32(8) for _ in range(2)]
        assert ar.top <= WC0_OFF, ("L0-D overlaps wC slot", ar.top, WC0_OFF)
        b_wo = b_wo_l0
        wo = wo_l0
        b_xres_ = [Buf() for _ in range(2)]
        b_x1t_ = [Buf() for _ in range(2)]
        b_xs1_ = [Buf() for _ in range(2)]
        b_ss1_ = [Buf() for _ in range(2)]
        b_scr = [Buf() for _ in range(9)]
        pend = None
        for j in range(9):
            i = j % 2
            xres, x1t = xres_[i], x1t_[i]
            p.dma("sp", xres, x_ext[QOFF + 128 * j:QOFF + 128 * (j + 1), :], writes=[b_xres_[i]])
            for cb_ in range(4):
                ps = psum_f()
                for kt in range(NKT):
                    p.op("pe", lambda e: e.matmul(
                        ps[0][:, 0:512], mixT[:, kt, 128 * j:128 * (j + 1)], wo[:, kt, 512 * cb_:512 * (cb_ + 1)],
                        start=(kt == 0), stop=(kt == NKT - 1)),
                        reads=[b_mix[kt], b_wo], writes=[ps[1]], inc=(kt == NKT - 1))
                p.op("dve", lambda e: e.tensor_tensor(
                    x1t[:, 512 * cb_:512 * (cb_ + 1)], ps[0][:, 0:512], xres[:, 512 * cb_:512 * (cb_ + 1)],
                    ALU.add), reads=[ps[1], b_xres_[i]], writes=[b_x1t_[i]])
            p.dma("sp", x1_scr[128 * j:128 * (j + 1), :], x1t, reads=[b_x1t_[i]], writes=[b_scr[j]])
            rms_part1(128, x1t, b_x1t_[i], xs1_[i], b_xs1_[i], ss1_[i][:, 0:1], ss1_[i][:, 1:2], b_ss1_[i])
            if pend is not None:
                rms_part2(*pend)
            pend = (128, xs1_[i], b_xs1_[i], PV_G1, xn1T, 128 * j, b_xn1[j])
        rms_part2(*pend)
        p.barrier()

        ar.top = base_top
        wCb = [ar.bf(NKT * 384).rearrange("p (k c) -> p k c", c=384) for _ in range(2)]
        wc_end = ar.top
        wCb[0] = wC0_top
        zcT = ar.bf(16 * 1088).rearrange("p (f c) -> p f c", c=1088)
        b_zc = [Buf() for _ in range(16)]
        sgT = ar.bf(16 * 1088).rearrange("p (f c) -> p f c", c=1088)
        b_sg = [Buf() for _ in range(16)]
        sg_end = ar.top
        assert ar.top <= l0_top, (ar.top, l0_top)
        ar.top = xn1_end
        zb = [ar.f32(1216) for _ in range(2)]
        b_zb = [Buf() for _ in range(2)]
        sig = [ar.f32(512) for _ in range(3)]
        b_sig = [Buf() for _ in range(3)]
        acc = [ar.f32(1152) for _ in range(2)]
        b_acc = [Buf() for _ in range(2)]
        cbo = ar.f32(16 * 90)
        b_cbo = Buf()
        KPE = 14
        zbf = [ar.bf(1216) for _ in range(2)]
        b_zbf = [Buf() for _ in range(2)]
        dg = [ar.bf(KPE * 128).rearrange("p (t c) -> p t c", c=128) for _ in range(2)]
        b_dg = [Buf() for _ in range(2)]
        rr1 = {"sig": 0}

        def load_wC(f):
            for g in range(4):
                p.dma("pool", wCb[f % 2][:, 4 * g:4 * g + 4, :],
                      wC[f, 512 * g:512 * (g + 1), :].rearrange("(k p) c -> p k c", p=128),
                      writes=[b_wC[f % 2]])

        def proj1(wt, b_w, ccol, c0, n, ps):
            rd = [b_w] + [b_xn1[t] for t in range(c0 // 128, (c0 + n - 1) // 128 + 1)]
            for kt in range(NKT):
                p.op("pe", (lambda kt=kt: lambda e: e.matmul(ps[0][:, 0:n], wt[:, kt, ccol:ccol + 128],
                                                              xn1T[:, kt, c0:c0 + n],
                                                              start=(kt == 0), stop=(kt == NKT - 1)))(),
                     reads=rd, writes=[ps[1]], inc=(kt == NKT - 1))

        def zcol(c):
            if c < 1088:
                return c
            if c < 1120:
                return c + 30
            return c + 60

        def l1a_proj(f, fillers):
            wt, b_w = wCb[f % 2], b_wC[f % 2]
            z, b_z = zb[f % 2], b_zb[f % 2]
            csv = cs0T[:, f * 60:f * 60 + 60]
            p.dma("sp", z[:, 1088:1118], csv[:, 0:30], writes=[b_z])
            p.dma("sp", z[:, 1150:1180], csv[:, 30:60], writes=[b_z])
            for bi, (c0, n) in enumerate(nblocks(0, NQ)):
                psv = psum_f()
                proj1(wt, b_w, 0, c0, n, psv)
                psg = psum_f()
                proj1(wt, b_w, 128, c0, n, psg)
                si = rr1["sig"]
                rr1["sig"] = (si + 1) % 3
                sg_, b_s = sig[si], b_sig[si]
                p.op("act", lambda e: e.activation(sg_[:, 0:n], psg[0][:, 0:n], AF.Sigmoid),
                     reads=[psg[1]], writes=[b_s])
                if c0 + n <= 1088:
                    p.op("dve", lambda e: e.tensor_tensor(z[:, c0:c0 + n], psv[0][:, 0:n], sg_[:, 0:n], ALU.mult),
                         reads=[psv[1], b_s], writes=[b_z])
                else:
                    for (o, ln, zc0) in ((0, 64, 1024), (64, 32, 1118), (96, 32, 1180)):
                        p.op("dve", lambda e: e.tensor_tensor(z[:, zc0:zc0 + ln], psv[0][:, o:o + ln],
                                                              sg_[:, o:o + ln], ALU.mult),
                             reads=[psv[1], b_s], writes=[b_z])
                pst = psum_f()
                proj1(wt, b_w, 256, c0, n, pst)
                lo = max(c0, 64)
                si2 = rr1["sig"]
                rr1["sig"] = (si2 + 1) % 3
                sg2, b_s2 = sig[si2], b_sig[si2]
                p.op("act", lambda e: e.activation(sg2[:, 0:n], pst[0][:, 0:n], AF.Sigmoid),
                     reads=[pst[1]], writes=[b_s2])
                p.op("dve", lambda e: e.tensor_tensor(sgT[:, f, lo - 64:c0 + n - 64], pst[0][:, lo - c0:n],
                                                      sg2[:, lo - c0:n], ALU.mult),
                     reads=[pst[1], b_s2], writes=[b_sg[f]])
                for fn in fillers[bi]:
                    fn()
            for (k_, zc0) in ((0, 1088 - 30), (1, 1150 - 30), (2, 1212 - 30)):
                p.op("pool", lambda e: e.tensor_copy(cbo[:, f * 90 + 30 * k_:f * 90 + 30 * k_ + 30],
                                                     z[:, zc0:zc0 + 30]), reads=[b_z], writes=[b_cbo])
            zb_, b_zb_ = zbf[f % 2], b_zbf[f % 2]
            dg_, b_dg_ = dg[f % 2], b_dg[f % 2]
            p.op("pool", lambda e: e.tensor_tensor(
                dg_, ident.unsqueeze(1).to_broadcast([128, KPE, 128]),
                pvec[:, PV_DWW + f * 31:PV_DWW + f * 31 + KPE].unsqueeze(2).to_broadcast([128, KPE, 128]),
                ALU.mult), reads=[b_const, b_pvec], writes=[b_dg_])
            p.op("act", lambda e: e.activation(zb_[:, 0:1212], z[:, 0:1212], AF.Copy), reads=[b_z], writes=[b_zb_])

        def l1a_taps(f):
            z, b_z = zb[f % 2], b_zb[f % 2]
            ac, b_ac = acc[f % 2], b_acc[f % 2]
            dww = [pvec[:, PV_DWW + f * 31 + j:PV_DWW + f * 31 + j + 1] for j in range(31)]
            dwb = pvec[:, PV_DWB + f:PV_DWB + f + 1]
            fns = []
            fns.append(lambda: p.op("dve", lambda e: e.tensor_scalar(
                ac[:, 0:1148], z[:, 34 + KPE:34 + KPE + 1148], dww[KPE], dwb, ALU.mult, ALU.add),
                reads=[b_z, b_pvec], writes=[b_ac]))
            for j in range(KPE + 1, 31):
                fns.append((lambda j: lambda: p.op("dve", lambda e: e.scalar_tensor_tensor(
                    ac[:, 0:1148], z[:, 34 + j:34 + j + 1148], dww[j], ac[:, 0:1148], ALU.mult, ALU.add),
                    reads=[b_z, b_pvec, b_ac], writes=[b_ac]))(j))
            k = (len(fns) + 2) // 3
            return [fns[0:k], fns[k:2 * k], fns[2 * k:]]

        def l1a_conv(f):
            ac, b_ac = acc[f % 2], b_acc[f % 2]
            zb_, b_zb_ = zbf[f % 2], b_zbf[f % 2]
            dg_, b_dg_ = dg[f % 2], b_dg[f % 2]
            for (c0, n) in nblocks(0, 1148):
                ps = psum_f()
                for t in range(KPE):
                    p.op("pe", lambda e: e.matmul(ps[0][:, 0:n], dg_[:, t, :], zb_[:, 34 + t + c0:34 + t + c0 + n],
                                                  start=(t == 0), stop=(t == KPE - 1)),
                         reads=[b_dg_, b_zb_], writes=[ps[1]], inc=(t == KPE - 1))
                p.op("dve", lambda e: e.tensor_tensor(ac[:, c0:c0 + n], ps[0][:, 0:n], ac[:, c0:c0 + n], ALU.add),
                     reads=[ps[1], b_ac], writes=[b_ac])
            p.op("act", lambda e: e.activation(zcT[:, f, 0:1024], ac[:, 0:1024], AF.Copy),
                 reads=[b_ac], writes=[b_zc[f]])
            p.op("act", lambda e: e.activation(zcT[:, f, 1024:1056], ac[:, 1054:1086], AF.Copy),
                 reads=[b_ac], writes=[b_zc[f]])
            p.op("act", lambda e: e.activation(zcT[:, f, 1056:1088], ac[:, 1116:1148], AF.Copy),
                 reads=[b_ac], writes=[b_zc[f]])

        wo_l1 = arena_t[:][:, l0_top:l0_top + (NKT * D) // 2].bitcast(BF16).rearrange("p (k c) -> p k c", c=D)
        b_woc = [Buf() for _ in range(8)]
        assert 4 * 2 * D // 2 <= (NKT * NQ) // 2
        assert ar.top <= WC0_OFF, ("L1-A overlaps wC slot", ar.top, WC0_OFF)
        load_wC(1)
        for f in range(16):
            l1a_proj(f, l1a_taps(f - 1) if f >= 1 else [[], [], []])
            if f + 2 < 16:
                load_wC(f + 2)
            if f == 15:
                for g in range(4):
                    p.dma("pool", wo_l1[:, 2 * g:2 * g + 2, :],
                          wo1[256 * g:256 * (g + 1), :].rearrange("(k p) c -> p k c", p=128),
                          writes=[b_woc[g]] + b_xn1)
            if f >= 1:
                l1a_conv(f - 1)
        for grp in l1a_taps(15):
            for fn in grp:
                fn()
        l1a_conv(15)
        p.dma("sp", cbo_d, cbo, reads=[b_cbo])
        p.barrier()

        ar.top = base_top
        mean = ar.f32(1088)
        rstd = ar.f32(1088)
        sq = [ar.bf(512) for _ in range(2)]
        tt = [ar.f32(512) for _ in range(3)]
        assert ar.top <= wc_end, (ar.top, wc_end)
        ar.top = sg_end
        xres_ = [ar.f32(D)]
        assert ar.top <= l0_top, (ar.top, l0_top)
        ar.top = l0_top
        wo = ar.bf(NKT * D).rearrange("p (k c) -> p k c", c=D)
        wo = wo_l1
        xres_.append(ar.f32(D))
        yt_ = [ar.f32(D) for _ in range(2)]
        ss2_ = [ar.f32(8) for _ in range(2)]
        b_xres_ = [Buf() for _ in range(2)]
        b_yt_ = [Buf() for _ in range(2)]
        b_ss2_ = [Buf() for _ in range(2)]
        b_sq = [Buf() for _ in range(2)]
        b_tt = [Buf() for _ in range(3)]
        NB1 = [(0, 256), (256, 512), (768, 320)]
        b_st = [Buf() for _ in NB1]
        b_zc2 = [[Buf() for _ in NB1] for _ in range(16)]
        for g in range(4, 8):
            p.dma("pool", wo[:, 2 * g:2 * g + 2, :],
                  wo1[256 * g:256 * (g + 1), :].rearrange("(k p) c -> p k c", p=128), writes=[b_woc[g]])
        rr2 = {"tt": 0}

        def l1_stats(nb):
            c0, n = NB1[nb]
            ps1 = psum_f()
            ps2 = psum_f()
            for f in range(16):
                i = f % 2
                p.op("act", lambda e: e.activation(sq[i][:, 0:n], zcT[:, f, c0:c0 + n], AF.Square),
                     reads=[b_zc2[f][nb]], writes=[b_sq[i]])
                p.op("pe", lambda e: e.matmul(ps1[0][:, 0:n], onesdiv, zcT[:, f, c0:c0 + n],
                                              start=(f == 0), stop=(f == 15)),
                     reads=[b_zc2[f][nb], b_const], writes=[ps1[1]], inc=(f == 15))
                p.op("pe", lambda e: e.matmul(ps2[0][:, 0:n], onesdiv, sq[i][:, 0:n],
                                              start=(f == 0), stop=(f == 15)),
                     reads=[b_sq[i], b_const], writes=[ps2[1]], inc=True)
            bs = b_st[nb]
            p.op("act", lambda e: e.activation(mean[:, c0:c0 + n], ps1[0][:, 0:n], AF.Copy),
                 reads=[ps1[1]], writes=[bs])
            p.op("act", lambda e: e.activation(rstd[:, c0:c0 + n], ps1[0][:, 0:n], AF.Square),
                 reads=[ps1[1]], writes=[bs])
            p.op("dve", lambda e: e.tensor_tensor(rstd[:, c0:c0 + n], ps2[0][:, 0:n], rstd[:, c0:c0 + n],
                                                  ALU.subtract), reads=[ps2[1], bs], writes=[bs])
            p.op("dve", lambda e: e.tensor_scalar(rstd[:, c0:c0 + n], rstd[:, c0:c0 + n], 0.0, None, ALU.max),
                 reads=[bs], writes=[bs])
            p.op("act", lambda e: e.activation(rstd[:, c0:c0 + n], rstd[:, c0:c0 + n], AF.Sqrt, bias=EPS),
                 reads=[bs], writes=[bs])
            p.op("dve", lambda e: e.reciprocal(rstd[:, c0:c0 + n], rstd[:, c0:c0 + n]), reads=[bs], writes=[bs])

        def l1_ln_tasks(nb):
            c0, n = NB1[nb]
            bs = b_st[nb]
            slot = {}

            def front(f):
                i = rr2["tt"]
                rr2["tt"] = (i + 1) % 3
                slot[f] = i
                t_ = tt[i][:, 0:n]
                lng = pvec[:, PV_LNG + f:PV_LNG + f + 1]
                lnb = pvec[:, PV_LNB + f:PV_LNB + f + 1]
                zsl = zcT[:, f, c0:c0 + n]
                p.op("dve", lambda e: e.tensor_tensor(t_, zsl, mean[:, c0:c0 + n], ALU.subtract),
                     reads=[b_zc2[f][nb], bs], writes=[b_tt[i]])
                p.op("dve", lambda e: e.tensor_tensor(t_, t_, rstd[:, c0:c0 + n], ALU.mult),
                     reads=[b_tt[i], bs], writes=[b_tt[i]])
                p.op("act", lambda e: e.activation(t_, t_, AF.Silu, scale=lng, bias=lnb),
                     reads=[b_tt[i], b_pvec], writes=[b_tt[i]])

            def back(f):
                i = slot[f]
                t_ = tt[i][:, 0:n]
                zsl = zcT[:, f, c0:c0 + n]
                p.op("dve", lambda e: e.tensor_tensor(zsl, t_, sgT[:, f, c0:c0 + n], ALU.mult),
                     reads=[b_tt[i], b_sg[f]], writes=[b_zc2[f][nb]])

            def one(f):
                if f < 16:
                    front(f)
                if f >= 1:
                    back(f - 1)
            return [(lambda f=f: one(f)) for f in range(17)]

        def l1_wout(nb, fillers=()):
            fillers = list(fillers)
            c0, n = NB1[nb]
            nslots = 4 * len(nblocks(c0, c0 + n, 128))
            per_slot = (len(fillers) + nslots - 1) // nslots
            for (r0, nrows) in nblocks(c0, c0 + n, 128):
                j = r0 // 128
                i = j % 2
                xres, yt, ss2 = xres_[i], yt_[i], ss2_[i]
                p.dma("sp", xres[0:nrows, :], x1_scr[64 + r0:64 + r0 + nrows, :], reads=b_scr, writes=[b_xres_[i]])
                for cb_ in range(4):
                    ps = psum_f()
                    for kt in range(NKT):
                        p.op("pe", lambda e: e.matmul(
                            ps[0][0:nrows, 0:512], zcT[:, kt, r0:r0 + nrows],
                            wo[:, kt, 512 * cb_:512 * (cb_ + 1)], start=(kt == 0), stop=(kt == NKT - 1)),
                            reads=[b_zc2[kt][nb], b_woc[kt // 2]], writes=[ps[1]], inc=(kt == NKT - 1))
                    p.op("dve", lambda e: e.tensor_tensor(
                        xres[0:nrows, 512 * cb_:512 * (cb_ + 1)], ps[0][0:nrows, 0:512],
                        xres[0:nrows, 512 * cb_:512 * (cb_ + 1)], ALU.add),
                        reads=[ps[1], b_xres_[i]], writes=[b_xres_[i]])
                    for _ in range(per_slot):
                        if fillers:
                            fillers.pop(0)()
                p.op("act", lambda e: e.activation(yt[0:nrows, :], xres[0:nrows, :], AF.Square,
                                                   accum_out=ss2[0:nrows, 0:1]),
                     reads=[b_xres_[i]], writes=[b_yt_[i], b_ss2_[i]])
                p.op("act", lambda e: e.activation(ss2[0:nrows, 1:2], ss2[0:nrows, 0:1], AF.Sqrt,
                                                   scale=1.0 / D, bias=EPS), reads=[b_ss2_[i]], writes=[b_ss2_[i]])
                p.op("dve", lambda e: e.reciprocal(ss2[0:nrows, 1:2], ss2[0:nrows, 1:2]),
                     reads=[b_ss2_[i]], writes=[b_ss2_[i]])
                p.op("dve", lambda e: e.scalar_tensor_tensor(
                    yt[0:nrows, :], xres[0:nrows, :], ss2[0:nrows, 1:2], fnb[0:nrows, :], ALU.mult, ALU.mult),
                    reads=[b_xres_[i], b_ss2_[i], b_fnb], writes=[b_yt_[i]])
                dst = y_main[r0:r0 + 128, :] if r0 < 1024 else y_samp[0:64, :]
                p.dma("sp", dst, yt[0:nrows, :], reads=[b_yt_[i]])
            for fn in fillers:
                fn()

        l1_stats(0)
        for fn in l1_ln_tasks(0):
            fn()
        l1_stats(1)
        l1_wout(0, l1_ln_tasks(1))
        l1_stats(2)
        l1_wout(1, l1_ln_tasks(2))
        l1_wout(2)
        p.finish()
        p.emit(st)
    return nc


def _bias_tables(rel):
    r = np.arange(128)
    out = np.full((8, 128, 704), NEG, np.float32)
    q = np.arange(128)
    for j in range(5):
        kpos = (j - 4) * 128 + r
        kc_ = 2 * (j - 4) + (r >= 64)
        qc = (q >= 64).astype(int)
        vis = (kc_[:, None] >= qc[None, :] - 8) & (kc_[:, None] <= qc[None, :])
        idx = np.clip(q[None, :] - kpos[:, None], -128, 128) + 128
        vals = rel[:, idx]
        out[:, :, j * 128:(j + 1) * 128] = np.where(vis[None], vals, NEG)
    idx = np.clip(q[None, :32] - r[:32, None], -128, 128) + 128
    vals = rel[:, idx]
    out[:, 0:32, 640:672] = vals
    out[:, 32:64, 672:704] = vals
    return out


_NC_CACHE = {}
LAST_DBG = None


def kernel(x_prompt, x_sample, cache_attn_k, cache_attn_v, state_lru_h, state_lru_conv, state_conv,
           norm_ab, w_in_ab, w_out_ab, rel_bias, lru_conv_w, lru_conv_b, lru_w_a, lru_b_a, lru_w_x,
           lru_b_x, lru_lambda, norm_cv, w_in_cv, w_out_cv, dw_w, dw_b, ln_g, ln_b, final_norm):
    f = lambda a: np.ascontiguousarray(np.asarray(a, dtype=np.float32))
    x_prompt, x_sample = f(x_prompt), f(x_sample)
    w_in = f(w_in_ab)[0]
    wA = np.stack([np.concatenate([w_in[:, c * 1024 + h * 128:c * 1024 + (h + 1) * 128] for c in range(4)], axis=1)
                   for h in range(8)])
    wB = np.stack([np.concatenate([w_in[:, 4096 + n * 128:4096 + (n + 1) * 128],
                                   w_in[:, 5120 + n * 128:5120 + (n + 1) * 128]], axis=1) for n in range(8)])
    w_cv = f(w_in_cv)[0]
    wC = np.stack([np.concatenate([w_cv[:, c * 2048 + k * 128:c * 2048 + (k + 1) * 128] for c in range(3)], axis=1)
                   for k in range(16)])
    wo0 = f(w_out_ab)[0]
    wo1 = f(w_out_cv)[0]
    wax = np.stack([f(lru_w_a)[0], f(lru_w_x)[0]])
    wax = np.ascontiguousarray(wax.transpose(2, 0, 1, 3)).reshape(128, 2 * 8 * 128)
    col = lambda v, k: np.ascontiguousarray(f(v).reshape(k, 128).T)
    pv = np.zeros((8, 128, PV_N), np.float32)
    base = np.zeros((128, PV_N), np.float32)
    base[:, PV_G0:PV_G0 + 16] = col(norm_ab[0], 16)
    base[:, PV_G1:PV_G1 + 16] = col(norm_cv[0], 16)
    cw = f(lru_conv_w)[0]
    base[:, PV_CW:PV_CW + 32] = cw.reshape(4, 8, 128).transpose(2, 1, 0).reshape(128, 32)
    base[:, PV_CB:PV_CB + 8] = col(lru_conv_b[0], 8)
    base[:, PV_BA:PV_BA + 8] = col(lru_b_a[0], 8)
    base[:, PV_BX:PV_BX + 8] = col(lru_b_x[0], 8)
    base[:, PV_LAM:PV_LAM + 8] = col(lru_lambda[0], 8)
    dww = f(dw_w)[0]
    base[:, PV_DWW:PV_DWW + 496] = dww.reshape(31, 16, 128).transpose(2, 1, 0).reshape(128, 496)
    base[:, PV_DWB:PV_DWB + 16] = col(dw_b[0], 16)
    base[:, PV_LNG:PV_LNG + 16] = col(ln_g[0], 16)
    base[:, PV_LNB:PV_LNB + 16] = col(ln_b[0], 16)
    fnb = np.ascontiguousarray(np.broadcast_to(f(final_norm)[None, :], (128, D)))
    biasT = _bias_tables(f(rel_bias)[0])
    ck, cv = f(cache_attn_k)[0], f(cache_attn_v)[0]
    slh, slc, scv = f(state_lru_h)[0], f(state_lru_conv)[0], f(state_conv)[0]

    in_maps = []
    for c in range(8):
        b, half = c // 2, c % 2
        xe = np.zeros((TALL, D), np.float32)
        if half == 1:
            xe[0:1024] = x_prompt[b, 0:1024]
        xe[1024:2048] = x_prompt[b, half * 1024:(half + 1) * 1024]
        xe[2048:2112] = x_sample[2 * c:2 * c + 2].reshape(64, D)
        pvc = base.copy()
        pvc[:, PV_FLAG] = float(half)
        h0 = slh[2 * c:2 * c + 2]
        h0T = np.ascontiguousarray(h0.reshape(2, 8, 128).transpose(2, 1, 0)).reshape(128, 16)
        lb = slc[2 * c:2 * c + 2]
        lb0T = np.ascontiguousarray(lb.reshape(2, 3, 8, 128).transpose(3, 2, 0, 1)).reshape(128, 48)
        cs = scv[2 * c:2 * c + 2]
        cs0T = np.ascontiguousarray(cs.reshape(2, 30, 16, 128).transpose(3, 2, 0, 1)).reshape(128, 960)
        in_maps.append({
            "x_ext": xe,
            "kc": np.ascontiguousarray(ck[2 * c:2 * c + 2].reshape(2, 512, 1024)),
            "vc": np.ascontiguousarray(cv[2 * c:2 * c + 2].reshape(2, 512, 1024)),
            "h0T": h0T, "lb0T": lb0T, "cs0T": cs0T,
            "wA": wA, "wB": wB, "wC": wC, "wo0": wo0, "wo1": wo1, "wax": wax,
            "pvec": pvc, "fnb": fnb, "biasT": biasT,
        })
    if "nc" not in _NC_CACHE:
        _NC_CACHE["nc"] = build_program()
    res = run_bass_kernel_spmd(_NC_CACHE["nc"], in_maps, core_ids=list(range(8)))
    R = res.results
    global LAST_DBG
    LAST_DBG = [r.get("dbg") for r in R] if DEBUG else None

    y_prompt = np.zeros((4, 2048, D), np.float32)
    y_sample = np.zeros((16, 32, D), np.float32)
    k_p = np.zeros((1, 4, 512, 8, 128), np.float32)
    v_p = np.zeros_like(k_p)
    h_p = np.zeros((1, 4, 1024), np.float32)
    lb_p = np.zeros((1, 4, 3, 1024), np.float32)
    cb_p = np.zeros((1, 4, 30, 2048), np.float32)
    k_s = np.zeros((1, 16, 32, 8, 128), np.float32)
    v_s = np.zeros_like(k_s)
    h_s = np.zeros((1, 16, 1024), np.float32)
    lb_s = np.zeros((1, 16, 3, 1024), np.float32)
    cb_s = np.zeros((1, 16, 30, 2048), np.float32)
    for c in range(8):
        b, half = c // 2, c % 2
        r = R[c]
        y_prompt[b, half * 1024:(half + 1) * 1024] = r["y_main"]
        y_sample[2 * c:2 * c + 2] = r["y_samp"].reshape(2, 32, D)
        sto = r["sto"]
        cbo = r["cbo"].reshape(128, 16, 3, 30)
        kvs = r["kvo_s"].reshape(2, 2, 32, 8, 128)
        k_s[0, 2 * c:2 * c + 2] = kvs[0]
        v_s[0, 2 * c:2 * c + 2] = kvs[1]
        hs = sto[:, ST_HS:ST_HS + 16].reshape(128, 8, 2)
        h_s[0, 2 * c:2 * c + 2] = hs.transpose(2, 1, 0).reshape(2, 1024)
        ls = sto[:, ST_LS:ST_LS + 48].reshape(128, 8, 2, 3)
        lb_s[0, 2 * c:2 * c + 2] = ls.transpose(2, 3, 1, 0).reshape(2, 3, 1024)
        cb_s[0, 2 * c:2 * c + 2] = cbo[:, :, 1:3, :].transpose(2, 3, 1, 0).reshape(2, 30, 2048)
        if half == 1:
            kvp = r["kvo_p"].reshape(2, 512, 8, 128)
            k_p[0, b] = kvp[0]
            v_p[0, b] = kvp[1]
            h_p[0, b] = sto[:, ST_HP:ST_HP + 8].T.reshape(1024)
            lb_p[0, b] = sto[:, ST_LP:ST_LP + 24].reshape(128, 8, 3).transpose(2, 1, 0).reshape(3, 1024)
            cb_p[0, b] = cbo[:, :, 0, :].transpose(2, 1, 0).reshape(30, 2048)
    return (y_prompt, y_sample, k_p, v_p, h_p, lb_p, cb_p, k_s, v_s, h_s, lb_s, cb_s)
```

```python
import numpy as np
from contextlib import ExitStack
import concourse.bass as bass
import concourse.mybir as mybir
from concourse.bass_utils import run_bass_kernel_spmd

F32 = mybir.dt.float32
BF16 = mybir.dt.bfloat16
AF = mybir.ActivationFunctionType
ALU = mybir.AluOpType

ENGS = ("pe", "act", "dve", "pool", "sp")
import os
DEBUG = bool(os.environ.get("KDEBUG"))
DBG_COLS = 40000
DBG_MAP = {}
D = 2048
NKT = 16
TALL = 2112
QOFF = 960
NQ = 1152
EPS = 1e-6
NEG = -30000.0

PV_G0, PV_G1, PV_CW, PV_CB, PV_BA, PV_BX, PV_LAM = 0, 16, 32, 64, 72, 80, 88
PV_DWW, PV_DWB, PV_LNG, PV_LNB, PV_FLAG, PV_N = 96, 592, 608, 624, 640, 641
ST_HP, ST_HS, ST_LP, ST_LS, ST_N = 0, 8, 24, 48, 96


class Buf:
    __slots__ = ("name", "w", "r")

    def __init__(self, name=""):
        self.name = name
        self.w = None
        self.r = {}


class _Rec:
    def __getattr__(self, name):
        def f(*a, **k):
            return (name, a, k)
        return f


_REC = _Rec()


class Prog:
    NDMA = 6

    def __init__(self, nc):
        self.nc = nc
        self.q = {e: [] for e in ENGS}
        self.cnt = {e: 0 for e in ENGS}
        self.seen = {e: {} for e in ENGS}
        self.pe_pending = False
        self.dma_rr = {e: 0 for e in ENGS}
        self.dma_val = {}

    def _need(self, eng, deps):
        best = {}
        for t in deps:
            if t is None:
                continue
            key, val = t
            if key == eng and eng == "pe":
                continue
            if key == "pe":
                assert val <= self.cnt["pe"], "dependency on un-incremented PE op"
            if best.get(key, 0) < val:
                best[key] = val
        for key, val in best.items():
            if self.seen[eng].get(key, 0) >= val:
                continue
            self.seen[eng][key] = val
            self.q[eng].append(("wait", key, val))

    def _deps(self, reads, writes):
        deps = []
        for b in reads:
            deps.append(b.w)
        for b in writes:
            deps.append(b.w)
            deps.extend(b.r.values())
        return deps

    def _commit(self, ticket, reads, writes):
        k = ticket[0]
        for b in reads:
            if b.r.get(k, (k, 0))[1] < ticket[1]:
                b.r[k] = ticket
        for b in writes:
            b.w = ticket
            b.r = {}

    def op(self, eng, fn, reads=(), writes=(), inc=True):
        self._need(eng, self._deps(reads, writes))
        if inc:
            self.cnt[eng] += 1
            ticket = (eng, self.cnt[eng])
            if eng == "pe":
                self.pe_pending = False
        else:
            assert eng == "pe"
            ticket = (eng, self.cnt[eng] + 1)
            self.pe_pending = True
        self.q[eng].append(("op", fn(_REC), inc))
        self._commit(ticket, reads, writes)
        return ticket

    DESC_LIMIT = 1400

    def dma(self, queue, out, in_, reads=(), writes=(), **kw):
        idx = self.dma_rr[queue]
        self.dma_rr[queue] = (idx + 1) % self.NDMA
        key = ("dma", queue, idx)
        prev = self.dma_val.get(key, 0)
        deps = self._deps(reads, writes)
        if prev:
            deps.append((key, prev))
        if queue == "sp":
            shp = list(out.shape)
            nd = 1
            for d_ in shp[:-1]:
                nd *= d_
            nd *= (shp[-1] * 4 + 4095) // 4096
            fifo = self.__dict__.setdefault("_fifo", [])
            while fifo and sum(x[1] for x in fifo) + nd > self.DESC_LIMIT:
                deps.append(fifo.pop(0)[0])
            self._pending_nd = nd
        self._need(queue, deps)
        val = prev + 16
        self.dma_val[key] = val
        self.q[queue].append(("dma", out, in_, key, kw))
        ticket = (key, val)
        if queue == "sp":
            self._fifo.append((ticket, self._pending_nd))
        self._commit(ticket, reads, writes)
        return ticket

    def barrier(self):
        assert not self.pe_pending
        tickets = [(e, self.cnt[e]) for e in ENGS if self.cnt[e] > 0]
        tickets += [(k, v) for k, v in self.dma_val.items()]
        for e in ENGS:
            self._need(e, [t for t in tickets if t[0] != e])

    def finish(self):
        for key, val in self.dma_val.items():
            if self.seen["sp"].get(key, 0) < val:
                self.seen["sp"][key] = val
                self.q["sp"].append(("wait", key, val))

    def emit(self, stack):
        nc = self.nc
        assert not self.pe_pending
        sems = {}
        for e in ENGS:
            sems[e] = stack.enter_context(nc.semaphore("s_" + e))
        for key in self.dma_val:
            sems[key] = stack.enter_context(nc.semaphore("d_%s_%d" % (key[1], key[2])))
        block = stack.enter_context(nc.Block())
        handles = {"pe": block.tensor, "act": block.scalar, "dve": block.vector,
                   "pool": block.gpsimd, "sp": block.sync}

        def run(ename):
            items = self.q[ename]

            def body(eng):
                for it in items:
                    if it[0] == "wait":
                        eng.wait_ge(sems[it[1]], it[2])
                    elif it[0] == "op":
                        ins = getattr(eng, it[1][0])(*it[1][1], **it[1][2])
                        if it[2]:
                            ins.then_inc(sems[ename], 1)
                    else:
                        _, out, in_, key, kw = it
                        eng.dma_start(out=out, in_=in_, **kw).then_inc(sems[key], 16)
            return body

        for e in ENGS:
            if self.q[e]:
                handles[e](run(e))


ARENA_WORDS = 53000


class Arena:
    def __init__(self, ap):
        self.ap = ap
        self.top = 0

    def f32(self, n):
        na = (n + 7) // 8 * 8
        assert self.top + na <= ARENA_WORDS, ("arena overflow", self.top, na)
        v = self.ap[:, self.top:self.top + n]
        self.top += na
        return v

    def bf(self, n):
        assert n % 2 == 0
        v = self.f32(n // 2)
        return v.bitcast(BF16)


def nblocks(lo, hi, step=512):
    out = []
    c = lo
    while c < hi:
        n = min(step, hi - c)
        out.append((c, n))
        c += n
    return out


def build_program():
    nc = bass.Bass("TRN2", target_bir_lowering=False)

    def din(name, shape):
        return nc.dram_tensor(name, list(shape), F32, kind="ExternalInput").ap()

    def dout(name, shape):
        return nc.dram_tensor(name, list(shape), F32, kind="ExternalOutput").ap()

    x_ext = din("x_ext", [TALL, D])
    kc = din("kc", [2, 512, 1024])
    vc = din("vc", [2, 512, 1024])
    h0T = din("h0T", [128, 16])
    lb0T = din("lb0T", [128, 48])
    cs0T = din("cs0T", [128, 16 * 60])
    wA = din("wA", [8, D, 512])
    wB = din("wB", [8, D, 256])
    wC = din("wC", [16, D, 384])
    wo0 = din("wo0", [D, D])
    wo1 = din("wo1", [D, D])
    wax = din("wax", [128, 2 * 8 * 128])
    pvec_d = din("pvec", [128, PV_N])
    fnb_d = din("fnb", [128, D])
    biasT_d = din("biasT", [8, 128, 704])

    y_main = dout("y_main", [1024, D])
    y_samp = dout("y_samp", [64, D])
    kvo_p = dout("kvo_p", [2, 512, 1024])
    kvo_s = dout("kvo_s", [2, 64, 1024])
    sto_d = dout("sto", [128, ST_N])
    cbo_d = dout("cbo", [128, 16 * 90])
    x1_scr = nc.dram_tensor("x1_scr", [NQ, D], F32, kind="Internal").ap()
    dbg_d = dout("dbg", [128, DBG_COLS]) if DEBUG else None
    DBG_MAP.clear()
    dbg_state = {"off": 0}

    def dbg(name, ap, ncols, buf, nrows=128):
        if not DEBUG or name in DBG_MAP:
            return
        o = dbg_state["off"]
        DBG_MAP[name] = (o, ncols, nrows)
        dbg_state["off"] = o + ncols
        p.dma("sp", dbg_d[0:nrows, o:o + ncols], ap, reads=[buf])

    st = ExitStack()
    with st:
        arena_t = st.enter_context(nc.sbuf_tensor("arena", [128, ARENA_WORDS], F32))
        psT = [st.enter_context(nc.psum_tensor("psT%d" % i, [128, 1024], BF16)) for i in range(2)]
        psF = [st.enter_context(nc.psum_tensor("psF%d" % i, [128, 512], F32)) for i in range(6)]
        psT_b = [Buf("psT%d" % i) for i in range(2)]
        psF_b = [Buf("psF%d" % i) for i in range(6)]
        p = Prog(nc)
        ar = Arena(arena_t[:])
        rr = {"T": 0, "F": 0}

        psF_all = [t_[:] for t_ in psF] + [t_[:].bitcast(F32) for t_ in psT]
        psF_b_all = psF_b + psT_b
        rr["NF"] = 6

        def set_banks(n):
            rr["NF"] = n
            rr["F"] %= n

        def psum_f():
            i = rr["F"]
            rr["F"] = (i + 1) % rr["NF"]
            return psF_all[i], psF_b_all[i]

        def psum_t():
            i = rr["T"]
            rr["T"] = (i + 1) % 2
            return psT[i][:], psT_b[i]

        pvec = ar.f32(PV_N)
        b_pvec = Buf("pvec")
        fnb = ar.f32(D)
        b_fnb = Buf("fnb")
        identf = ar.f32(128)
        ident = ar.bf(128)
        ones_bf = ar.bf(128)
        flag_bf = ar.bf(128)
        onesdiv = ar.bf(128)
        c8 = ar.f32(8)
        c16 = ar.f32(8)
        lsc = ar.f32(64)
        sto = ar.f32(ST_N)
        b_const = Buf("const")
        b_sto = Buf("sto")
        p.dma("sp", pvec, pvec_d, writes=[b_pvec])
        p.dma("sp", fnb, fnb_d, writes=[b_fnb])
        p.op("pool", lambda e: e.memset(identf, 0.0), writes=[b_const])
        p.op("pool", lambda e: e.affine_select(identf, identf, [[-1, 128]], ALU.not_equal, 1.0,
                                               base=0, channel_multiplier=1),
             reads=[b_const], writes=[b_const])
        p.op("dve", lambda e: e.tensor_copy(ident, identf), reads=[b_const], writes=[b_const])
        p.op("dve", lambda e: e.memset(ones_bf, 1.0), writes=[b_const])
        p.op("dve", lambda e: e.memset(onesdiv, 1.0 / D), writes=[b_const])
        p.op("dve", lambda e: e.memset(sto, 0.0), writes=[b_sto])
        flag = pvec[:, PV_FLAG:PV_FLAG + 1]
        p.op("dve", lambda e: e.tensor_scalar(flag_bf, ones_bf, flag, None, ALU.mult),
             reads=[b_const, b_pvec], writes=[b_const])
        lam = pvec[:, PV_LAM:PV_LAM + 8]
        t_abs, t_y, t_w, t_w2, t_s, t_m = (lsc[:, 8 * i:8 * i + 8] for i in range(6))
        RW = dict(reads=[b_const, b_pvec], writes=[b_const])
        p.op("dve", lambda e: e.tensor_scalar(t_abs, lam, -1.0, None, ALU.mult), **RW)
        p.op("dve", lambda e: e.tensor_tensor(t_abs, t_abs, lam, ALU.max), **RW)
        p.op("act", lambda e: e.activation(t_y, t_abs, AF.Exp, scale=-1.0), **RW)
        p.op("dve", lambda e: e.tensor_scalar(t_w, t_y, 2.0, None, ALU.add), **RW)
        p.op("dve", lambda e: e.reciprocal(t_w, t_w), **RW)
        p.op("dve", lambda e: e.tensor_tensor(t_w, t_w, t_y, ALU.mult), **RW)
        p.op("dve", lambda e: e.tensor_tensor(t_w2, t_w, t_w, ALU.mult), **RW)
        p.op("dve", lambda e: e.memset(t_s, 1.0 / 15.0), **RW)
        for kk in (13, 11, 9, 7, 5, 3, 1):
            p.op("dve", lambda e: e.tensor_tensor(t_s, t_s, t_w2, ALU.mult), **RW)
            p.op("dve", (lambda cst: (lambda e: e.tensor_scalar(t_s, t_s, cst, None, ALU.add)))(1.0 / kk), **RW)
        p.op("dve", lambda e: e.tensor_tensor(t_s, t_s, t_w, ALU.mult), **RW)
        p.op("dve", lambda e: e.tensor_scalar(t_m, lam, 0.0, None, ALU.min), **RW)
        p.op("dve", lambda e: e.scalar_tensor_tensor(t_s, t_s, -2.0, t_m, ALU.mult, ALU.add), **RW)
        p.op("dve", lambda e: e.tensor_scalar(c8, t_s, 8.0, None, ALU.mult), **RW)
        p.op("dve", lambda e: e.tensor_scalar(c16, t_s, 16.0, None, ALU.mult), **RW)

        base_top = ar.top

        xnT = ar.bf(NKT * TALL).rearrange("p (k c) -> p k c", c=TALL)
        b_xnT = [Buf("xnT%d" % t) for t in range(17)]
        mixT = ar.bf(NKT * NQ).rearrange("p (k c) -> p k c", c=NQ)
        b_mix = [Buf("mix%d" % k) for k in range(NKT)]
        l0_top = ar.top

        def xn_bufs(c0, n):
            return [b_xnT[t] for t in range(c0 // 128, (c0 + n - 1) // 128 + 1)]

        def rms_part1(nrows, xt, b_xt, xs, b_xs, ss, rstd, b_s):
            p.op("act", lambda e: e.activation(xs[0:nrows, :], xt[0:nrows, :], AF.Square, accum_out=ss[0:nrows, :]),
                 reads=[b_xt], writes=[b_xs, b_s])
            p.op("act", lambda e: e.activation(rstd[0:nrows, :], ss[0:nrows, :], AF.Sqrt, scale=1.0 / D, bias=EPS),
                 reads=[b_s], writes=[b_s])
            p.op("dve", lambda e: e.reciprocal(rstd[0:nrows, :], rstd[0:nrows, :]), reads=[b_s], writes=[b_s])
            p.op("act", lambda e: e.activation(xs[0:nrows, :], xt[0:nrows, :], AF.Copy, scale=rstd[0:nrows, :]),
                 reads=[b_xt, b_s], writes=[b_xs])

        def rms_part2(nrows, xs, b_xs, gcol, dstT, col0, b_dst):
            for half in range(2):
                pt, bpt = psum_t()
                for i in range(8):
                    kt = half * 8 + i
                    p.op("pe", lambda e: e.transpose(
                        pt[:, i * 128:i * 128 + nrows], xs[0:nrows, kt * 128:(kt + 1) * 128],
                        ident[0:nrows, 0:nrows]),
                        reads=[b_xs, b_const], writes=[bpt], inc=(i == 7))
                gb = pvec[:, gcol + half * 8:gcol + half * 8 + 8].unsqueeze(2).to_broadcast([128, 8, nrows])
                src = pt[:, 0:1024].rearrange("p (k c) -> p k c", c=128)[:, :, 0:nrows]
                p.op("dve", lambda e: e.tensor_tensor(dstT[:, half * 8:half * 8 + 8, col0:col0 + nrows], src, gb,
                                                      ALU.mult),
                     reads=[bpt, b_pvec], writes=[b_dst])

        wAb = [ar.bf(NKT * 512).rearrange("p (k c) -> p k c", c=512) for _ in range(2)]
        b_wA = [Buf() for _ in range(2)]
        for h_ in range(2):
            for g in range(4):
                p.dma("pool", wAb[h_][:, 4 * g:4 * g + 4, :],
                      wA[h_, 512 * g:512 * (g + 1), :].rearrange("(k p) c -> p k c", p=128),
                      writes=[b_wA[h_]])
        NXT = 4
        xt_ = [ar.f32(D) for _ in range(NXT)]
        xs_ = [ar.bf(D) for _ in range(2)]
        sst = [ar.f32(8) for _ in range(2)]
        b_xt = [Buf() for _ in range(NXT)]
        b_xs = [Buf() for _ in range(2)]
        b_ss = [Buf() for _ in range(2)]
        pend = None
        for t in range(17):
            nrows = 128 if t < 16 else 64
            i = t % 2
            ix = t % NXT
            p.dma("sp", xt_[ix][0:nrows, :], x_ext[t * 128:t * 128 + nrows, :], writes=[b_xt[ix]])
            rms_part1(nrows, xt_[ix], b_xt[ix], xs_[i], b_xs[i], sst[i][:, 0:1], sst[i][:, 1:2], b_ss[i])
            if pend is not None:
                rms_part2(*pend)
            pend = (nrows, xs_[i], b_xs[i], PV_G0, xnT, t * 128, b_xnT[t])
        rms_part2(*pend)
        p.barrier()
        ar.top = l0_top

        set_banks(8)
        wAb = [ar.bf(NKT * 512).rearrange("p (k c) -> p k c", c=512) for _ in range(2)]
        HB = []
        for _par in range(2):
            hb = dict(
                QT=ar.bf(1216), KT=ar.bf(1728), sga=ar.bf(NQ),
                Vt=ar.bf(14 * 128).rearrange("p (t d) -> p t d", d=128),
                KcT=ar.bf(1024).rearrange("p (s k) -> p s k", k=512),
                Vc=ar.bf(1024).rearrange("p (s j d) -> p s j d", s=2, j=4),
                biasb=ar.f32(704),
                b_QT=Buf(), b_KT=Buf(), b_sga=Buf(), b_Vt=Buf(), b_KcT=Buf(), b_Vc=Buf(), b_bias=Buf())
            HB.append(hb)
        kst_g = [ar.f32(512).rearrange("p (j d) -> p j d", d=128) for _ in range(2)]
        b_kst_g = [Buf(), Buf()]
        for hb in HB:
            hb["kst"] = kst_g
            hb["b_kst"] = b_kst_g
        st32 = [ar.f32(512) for _ in range(2)]
        b_st32 = [Buf() for _ in range(2)]
        tnh = [ar.f32(512) for _ in range(2)]
        b_tnh = [Buf() for _ in range(2)]
        Ssb = [ar.f32(640) for _ in range(2)]
        PT = [ar.bf(640) for _ in range(2)]
        b_Ssb = [Buf() for _ in range(2)]
        b_PT = [Buf() for _ in range(2)]
        rec = [ar.f32(128) for _ in range(2)]
        otmp = [ar.f32(128) for _ in range(2)]
        b_rec = [Buf() for _ in range(2)]
        kvst = [ar.f32(128) for _ in range(2)]
        b_kvst = [Buf() for _ in range(2)]
        rr_att = {"i": 0, "st": 0, "kv": 0, "tn": 0}

        def load_wA(h):
            for g in range(4):
                p.dma("pool", wAb[h % 2][:, 4 * g:4 * g + 4, :],
                      wA[h, 512 * g:512 * (g + 1), :].rearrange("(k p) c -> p k c", p=128),
                      writes=[b_wA[h % 2]])

        def proj(wt, b_w, ccol, c0, n, ps):
            rd = [b_w] + xn_bufs(c0, n)
            for kt in range(NKT):
                p.op("pe", lambda e: e.matmul(ps[0][:, 0:n], wt[:, kt, ccol:ccol + 128], xnT[:, kt, c0:c0 + n],
                                              start=(kt == 0), stop=(kt == NKT - 1)),
                     reads=rd, writes=[ps[1]], inc=(kt == NKT - 1))

        def attn_S(hb, h, qa, nq, keytiles, mixcol, qlo, last64):
            i = rr_att["i"]
            rr_att["i"] = (i + 1) % 2
            QT, biasb = hb["QT"], hb["biasb"]
            psA = psum_f()
            psB = psum_f()
            rK = [hb["b_QT"], hb["b_KT"], hb["b_KcT"]]
            for j in range(4):
                p.op("pe", lambda e: e.matmul(psA[0][:, j * 128 + qlo:j * 128 + qlo + nq], keytiles[j][0],
                                              QT[:, qa:qa + nq], start=True, stop=True),
                     reads=rK, writes=[psA[1]], inc=(j == 3))
            kr = 64 if last64 else 128
            p.op("pe", lambda e: e.matmul(psB[0][0:kr, qlo:qlo + nq], keytiles[4][0], QT[:, qa:qa + nq],
                                          start=True, stop=True), reads=rK, writes=[psB[1]])
            S, P_ = Ssb[i], PT[i]
            S4 = S[:, 0:512].rearrange("p (j q) -> p j q", q=128)[:, :, qlo:qlo + nq]
            A4 = psA[0][:, 0:512].rearrange("p (j q) -> p j q", q=128)[:, :, qlo:qlo + nq]
            B4 = biasb[:, 0:512].rearrange("p (j q) -> p j q", q=128)[:, :, qlo:qlo + nq]
            P4 = P_[:, 0:512].rearrange("p (j q) -> p j q", q=128)[:, :, qlo:qlo + nq]
            p.op("dve", lambda e: e.tensor_tensor(S4, A4, B4, ALU.add), reads=[psA[1], hb["b_bias"]],
                 writes=[b_Ssb[i]])
            if last64:
                s_idx = keytiles[4][3]
                bsl = biasb[0:64, 640 + 32 * s_idx:640 + 32 * s_idx + 32]
            else:
                bsl = biasb[:, 512 + qlo:512 + qlo + nq]
            p.op("dve", lambda e: e.tensor_tensor(S[0:kr, 512 + qlo:512 + qlo + nq], psB[0][0:kr, qlo:qlo + nq], bsl,
                                                  ALU.add), reads=[psB[1], hb["b_bias"]], writes=[b_Ssb[i]])
            p.op("act", lambda e: e.activation(P4, S4, AF.Exp), reads=[b_Ssb[i]], writes=[b_PT[i]])
            p.op("act", lambda e: e.activation(P_[0:kr, 512 + qlo:512 + qlo + nq], S[0:kr, 512 + qlo:512 + qlo + nq],
                                               AF.Exp), reads=[b_Ssb[i]], writes=[b_PT[i]])
            return (hb, h, i, nq, keytiles, mixcol, qlo, kr)

        def attn_V(state):
            hb, h, i, nq, keytiles, mixcol, qlo, kr = state
            P_ = PT[i]
            psO = psum_f()
            rV = [b_PT[i], hb["b_Vt"], hb["b_Vc"], b_const]
            for j in range(5):
                k_ = kr if j == 4 else 128
                p.op("pe", lambda e: e.matmul(
                    psO[0][:, 0:nq], keytiles[j][1], P_[0:k_, j * 128 + qlo:j * 128 + qlo + nq],
                    start=(j == 0), stop=(j == 4)), reads=rV, writes=[psO[1]], inc=False)
            for j in range(5):
                k_ = kr if j == 4 else 128
                p.op("pe", lambda e: e.matmul(
                    psO[0][:, 128:128 + nq], keytiles[j][2], P_[0:k_, j * 128 + qlo:j * 128 + qlo + nq],
                    start=(j == 0), stop=(j == 4)), reads=rV, writes=[psO[1]], inc=(j == 4))
            r_, o_ = rec[i], otmp[i]
            p.op("dve", lambda e: e.tensor_scalar(r_[:, 0:nq], psO[0][:, 128:128 + nq], 2.0, 1e-30,
                                                  ALU.mult, ALU.max), reads=[psO[1]], writes=[b_rec[i]])
            p.op("dve", lambda e: e.reciprocal(r_[:, 0:nq], r_[:, 0:nq]), reads=[b_rec[i]], writes=[b_rec[i]])
            p.op("dve", lambda e: e.tensor_tensor(o_[:, 0:nq], psO[0][:, 0:nq], r_[:, 0:nq], ALU.mult),
                 reads=[psO[1], b_rec[i]], writes=[b_rec[i]])
            p.op("dve", lambda e: e.tensor_tensor(mixT[:, h, mixcol:mixcol + nq], o_[:, 0:nq],
                                                  hb["sga"][:, mixcol:mixcol + nq], ALU.mult),
                 reads=[b_rec[i], hb["b_sga"]], writes=[b_mix[h]])

        def p_tasks(h):
            hb = HB[h % 2]
            wt, b_w = wAb[h % 2], b_wA[h % 2]
            QT, KT, sga, Vt, KcT, Vc = hb["QT"], hb["KT"], hb["sga"], hb["Vt"], hb["KcT"], hb["Vc"]
            tasks = []

            def t_dmas():
                p.dma("sp", hb["biasb"], biasT_d[h], writes=[hb["b_bias"]])
                for s_ in range(2):
                    p.dma("sp", hb["kst"][s_],
                          kc[s_, :, h * 128:(h + 1) * 128].rearrange("(j p) d -> p j d", p=128),
                          writes=[hb["b_kst"][s_]])
                    p.dma("pool", Vc[:, s_, :, :],
                          vc[s_, :, h * 128:(h + 1) * 128].rearrange("(j p) d -> p j d", p=128),
                          writes=[hb["b_Vc"]])
            tasks.append(t_dmas)

            def t_q(c0, n):
                ps = psum_f()
                proj(wt, b_w, 0, c0, n, ps)
                p.op("act", lambda e: e.activation(QT[:, c0 - 896:c0 - 896 + n], ps[0][:, 0:n], AF.Copy,
                                                   scale=128.0 ** -0.5), reads=[ps[1]], writes=[hb["b_QT"]])
            for (c0, n) in nblocks(896, TALL):
                tasks.append(lambda c0=c0, n=n: t_q(c0, n))

            def t_ga(c0, n):
                ps = psum_f()
                proj(wt, b_w, 384, c0, n, ps)
                ti = rr_att["tn"]
                rr_att["tn"] = (ti + 1) % 2
                p.op("act", lambda e: e.activation(tnh[ti][:, 0:n], ps[0][:, 0:n], AF.Tanh, scale=0.5),
                     reads=[ps[1]], writes=[b_tnh[ti]])
                p.op("dve", lambda e: e.scalar_tensor_tensor(sga[:, c0 - QOFF:c0 - QOFF + n], tnh[ti][:, 0:n], 1.0,
                                                             ps[0][:, 0:n], ALU.add, ALU.mult),
                     reads=[ps[1], b_tnh[ti]], writes=[hb["b_sga"]])
            for (c0, n) in nblocks(QOFF, TALL):
                tasks.append(lambda c0=c0, n=n: t_ga(c0, n))

            def t_kc(s_):
                ps = psum_f()
                for j in range(4):
                    p.op("pe", lambda e: e.transpose(ps[0][:, j * 128:(j + 1) * 128], hb["kst"][s_][:, j, :], identf),
                         reads=[hb["b_kst"][s_], b_const], writes=[ps[1]], inc=(j == 3))
                p.op("act", lambda e: e.activation(KcT[:, s_, :], ps[0][:, 0:512], AF.Copy),
                     reads=[ps[1]], writes=[hb["b_KcT"]])
            tasks.append(lambda: t_kc(0))
            tasks.append(lambda: t_kc(1))

            def kv_proj(which, c0, n):
                ps = psum_f()
                proj(wt, b_w, 128 * which, c0, n, ps)
                need32 = (which == 2) or (c0 + n > 1536)
                si = None
                if which == 1:
                    p.op("act", lambda e: e.activation(KT[:, c0 - 384:c0 - 384 + n], ps[0][:, 0:n], AF.Copy),
                         reads=[ps[1]], writes=[hb["b_KT"]])
                if need32:
                    si = rr_att["st"]
                    rr_att["st"] = (si + 1) % 2
                    eng = "dve" if which == 1 else "act"
                    if eng == "act":
                        p.op("act", lambda e: e.activation(st32[si][:, 0:n], ps[0][:, 0:n], AF.Copy),
                             reads=[ps[1]], writes=[b_st32[si]])
                    else:
                        p.op("dve", lambda e: e.tensor_copy(st32[si][:, 0:n], ps[0][:, 0:n]),
                             reads=[ps[1], hb["b_KT"]], writes=[b_st32[si]])
                return si

            def kv_post(which, c0, n, si):
                if si is None:
                    return
                s32, bs32 = st32[si], b_st32[si]
                if which == 2:
                    pv = psum_f()
                    tiles = nblocks(0, n, 128)
                    for ti, (o, m_) in enumerate(tiles):
                        p.op("pe", lambda e: e.transpose(pv[0][0:m_, ti * 128:(ti + 1) * 128], s32[:, o:o + m_],
                                                         identf),
                             reads=[bs32, b_const], writes=[pv[1]], inc=(ti == len(tiles) - 1))
                    vt0 = (c0 - 384) // 128
                    nfull = sum(1 for (_, m_) in tiles if m_ == 128)
                    if nfull:
                        p.op("dve", lambda e: e.tensor_copy(
                            Vt[:, vt0:vt0 + nfull, :], pv[0][:, 0:128 * nfull].rearrange("p (t d) -> p t d", d=128)),
                            reads=[pv[1]], writes=[hb["b_Vt"]])
                    if nfull < len(tiles):
                        p.op("dve", lambda e: e.tensor_copy(Vt[0:64, vt0 + nfull, :],
                                                            pv[0][0:64, 128 * nfull:128 * nfull + 128]),
                             reads=[pv[1]], writes=[hb["b_Vt"]])
                for (o, m_) in nblocks(0, n, 128):
                    ta = c0 + o
                    if ta < 1536:
                        continue
                    pk = psum_f()
                    p.op("pe", lambda e: e.transpose(pk[0][0:m_, 0:128], s32[:, o:o + m_], identf),
                         reads=[bs32, b_const], writes=[pk[1]])
                    ki = rr_att["kv"]
                    rr_att["kv"] = (ki + 1) % 2
                    p.op("act", lambda e: e.activation(kvst[ki][0:m_, 0:128], pk[0][0:m_, 0:128], AF.Copy),
                         reads=[pk[1]], writes=[b_kvst[ki]])
                    if ta < 2048:
                        dst = kvo_p[which - 1, ta - 1536:ta - 1536 + m_, h * 128:(h + 1) * 128]
                    else:
                        dst = kvo_s[which - 1, 0:m_, h * 128:(h + 1) * 128]
                    p.dma("sp", dst, kvst[ki][0:m_, 0:128], reads=[b_kvst[ki]])

            kvb = [(w_, c0, n) for w_ in (1, 2) for (c0, n) in nblocks(384, TALL)]
            state = {}

            def t_kv(k):
                w_, c0, n = kvb[k]
                state[k] = kv_proj(w_, c0, n)
                if k >= 1:
                    w2, c2, n2 = kvb[k - 1]
                    kv_post(w2, c2, n2, state[k - 1])
                if k == len(kvb) - 1:
                    kv_post(w_, c0, n, state[k])
            for k in range(len(kvb)):
                tasks.append(lambda k=k: t_kv(k))
            return tasks

        def a_tasks(h):
            hb = HB[h % 2]
            KT, Vt, KcT, Vc = hb["KT"], hb["Vt"], hb["KcT"], hb["Vc"]
            blocks = []
            for m in range(7, 16):
                qlo = 64 if m == 7 else 0
                nq = 128 - qlo
                kts = []
                for j in range(5):
                    kt_ = m - 4 + j
                    kcol = (kt_ - 3) * 128
                    kts.append((KT[:, kcol:kcol + 128], Vt[:, kt_ - 3, :], flag_bf if kt_ < 8 else ones_bf))
                blocks.append((hb, h, m * 128 + qlo - 896, nq, kts, m * 128 + qlo - QOFF, qlo, False))
            for s_ in range(2):
                kts = []
                for j in range(4):
                    kts.append((KcT[:, s_, j * 128:(j + 1) * 128], Vc[:, s_, j, :], ones_bf))
                kts.append((KT[:, 2048 - 384:2112 - 384], Vt[0:64, 13, :], ones_bf[0:64, :], s_))
                blocks.append((hb, h, 2048 + 32 * s_ - 896, 32, kts, 2048 + 32 * s_ - QOFF, 0, True))
            st_ = {}
            tasks = []

            def t_a(k):
                if k < len(blocks):
                    st_[k] = attn_S(*blocks[k])
                if k >= 1:
                    attn_V(st_[k - 1])
            for k in range(len(blocks) + 1):
                tasks.append(lambda k=k: t_a(k))
            return tasks

        wBb_pre = [arena_t[:][:, l0_top + 2048 * i_:l0_top + 2048 * (i_ + 1)].bitcast(BF16)
                   .rearrange("p (k c) -> p k c", c=256) for i_ in range(2)]
        b_wB = [Buf() for _ in range(2)]
        for t in p_tasks(0):
            t()
        for h in range(8):
            A = a_tasks(h)
            Pn = p_tasks(h + 1) if h + 1 < 8 else []
            if h + 2 < 8:
                load_wA(h + 2)
            if h == 6:
                for i_ in range(2):
                    for g in range(4):
                        p.dma("pool", wBb_pre[i_][:, 4 * g:4 * g + 4, :],
                              wB[i_, 512 * g:512 * (g + 1), :].rearrange("(k p) c -> p k c", p=128),
                              writes=[b_wB[i_], b_wA[0]])
            na, npn = len(A), len(Pn)
            ia = 0
            for ip in range(npn):
                Pn[ip]()
                want = ((ip + 1) * na) // npn
                while ia < want:
                    A[ia]()
                    ia += 1
            while ia < na:
                A[ia]()
                ia += 1
        p.barrier()
        ar.top = l0_top

        XP0, XSA, XSB = 3, 2054, 2089
        wBb = [ar.bf(NKT * 256).rearrange("p (k c) -> p k c", c=256) for _ in range(2)]
        assert NKT * 256 // 2 == 2048
        waxb = ar.bf(2 * 8 * 128).rearrange("p (g n e) -> p g n e", g=2, n=8)
        b_wax = Buf()
        h0sb = ar.f32(16)
        b_h0 = Buf()
        xb_ = [ar.f32(2128) for _ in range(2)]
        xc = ar.f32(TALL)
        xcb = ar.bf(TALL)
        rg = ar.f32(TALL)
        ig = ar.f32(TALL)
        bb = ar.f32(TALL)
        aa = xc
        hh = rg
        sgb_ = [ar.f32(NQ) for _ in range(2)]
        hin = ar.f32(8)
        b_xc, b_xcb, b_rg, b_ig, b_bb, b_hin = (Buf() for _ in range(6))
        b_xb_ = [Buf() for _ in range(2)]
        b_sgb_ = [Buf() for _ in range(2)]
        b_aa, b_hh = b_xc, b_rg
        p.dma("pool", waxb, wax.rearrange("p (g n e) -> p g n e", g=2, n=8), writes=[b_wax])
        p.dma("sp", h0sb, h0T, writes=[b_h0])

        def load_wB(n_):
            for g in range(4):
                p.dma("pool", wBb[n_ % 2][:, 4 * g:4 * g + 4, :],
                      wB[n_, 512 * g:512 * (g + 1), :].rearrange("(k p) c -> p k c", p=128),
                      writes=[b_wB[n_ % 2]])

        def xbcol(c):
            if c < 2048:
                return XP0 + c
            if c < 2080:
                return XSA + (c - 2048)
            return XSB + (c - 2080)

        def lru_P(n_):
            wt, b_w = wBb[n_ % 2], b_wB[n_ % 2]
            xb, b_xb = xb_[n_ % 2], b_xb_[n_ % 2]
            sgb, b_sgb = sgb_[n_ % 2], b_sgb_[n_ % 2]
            p.op("pool", lambda e: e.memset(xb[:, 0:3], 0.0), writes=[b_xb])
            lbv = lb0T[:, n_ * 6:n_ * 6 + 6]
            p.dma("sp", xb[:, XSA - 3:XSA], lbv[:, 0:3], writes=[b_xb])
            p.dma("sp", xb[:, XSB - 3:XSB], lbv[:, 3:6], writes=[b_xb])
            for (c0, n) in nblocks(0, TALL):
                ps = psum_f()
                proj(wt, b_w, 0, c0, n, ps)
                if c0 < 2048:
                    p.op("act", lambda e: e.activation(xb[:, XP0 + c0:XP0 + c0 + n], ps[0][:, 0:n], AF.Copy),
                         reads=[ps[1]], writes=[b_xb])
                else:
                    p.op("act", lambda e: e.activation(xb[:, XSA:XSA + 32], ps[0][:, 0:32], AF.Copy),
                         reads=[ps[1]], writes=[b_xb])
                    p.op("act", lambda e: e.activation(xb[:, XSB:XSB + 32], ps[0][:, 32:64], AF.Copy),
                         reads=[ps[1]], writes=[b_xb])
            for (c0, n) in nblocks(QOFF, TALL):
                ps = psum_f()
                proj(wt, b_w, 128, c0, n, ps)
                p.op("act", lambda e: e.activation(sgb[:, c0 - QOFF:c0 - QOFF + n], ps[0][:, 0:n], AF.Silu),
                     reads=[ps[1]], writes=[b_sgb])

        b_xc2 = [Buf(), Buf()]
        b_xcb2 = [Buf(), Buf()]
        b_rg2 = [Buf(), Buf()]
        b_ig2 = [Buf(), Buf()]
        b_bb2 = [Buf(), Buf()]
        HALF = ((0, 1024), (1024, TALL))

        def lru_C(n_):
            xb, b_xb = xb_[n_ % 2], b_xb_[n_ % 2]
            sgb, b_sgb = sgb_[n_ % 2], b_sgb_[n_ % 2]
            cw = [pvec[:, PV_CW + n_ * 4 + j:PV_CW + n_ * 4 + j + 1] for j in range(4)]
            cbv = pvec[:, PV_CB + n_:PV_CB + n_ + 1]
            ba = pvec[:, PV_BA + n_:PV_BA + n_ + 1]
            bx = pvec[:, PV_BX + n_:PV_BX + n_ + 1]
            c8n = c8[:, n_:n_ + 1]
            c16n = c16[:, n_:n_ + 1]
            runs = (((XP0 - 3, 0, 1024),), ((XP0 - 3 + 1024, 1024, 1024), (XSA - 3, 2048, 32), (XSB - 3, 2080, 32)))
            for k in range(2):
                for (xo, co, ln) in runs[k]:
                    p.op("dve", lambda e: e.tensor_scalar(xc[:, co:co + ln], xb[:, xo:xo + ln], cw[0], cbv,
                                                          ALU.mult, ALU.add), reads=[b_xb, b_pvec], writes=[b_xc2[k]])
                    for j in range(1, 4):
                        p.op("dve", lambda e: e.scalar_tensor_tensor(
                            xc[:, co:co + ln], xb[:, xo + j:xo + j + ln], cw[j], xc[:, co:co + ln],
                            ALU.mult, ALU.add), reads=[b_xb, b_pvec, b_xc2[k]], writes=[b_xc2[k]])
            for k in range(2):
                a, b = HALF[k]
                p.op("act", lambda e: e.activation(xcb[:, a:b], xc[:, a:b], AF.Copy),
                     reads=[b_xc2[k]], writes=[b_xcb2[k]])
            p.op("pool", lambda e: e.tensor_copy(sto[:, ST_LP + 3 * n_:ST_LP + 3 * n_ + 3],
                                                 xb[:, XP0 + 2045:XP0 + 2048]), reads=[b_xb], writes=[b_sto])
            for s_, xs0 in ((0, XSA), (1, XSB)):
                p.op("pool", lambda e: e.tensor_copy(
                    sto[:, ST_LS + 6 * n_ + 3 * s_:ST_LS + 6 * n_ + 3 * s_ + 3], xb[:, xs0 + 29:xs0 + 32]),
                    reads=[b_xb], writes=[b_sto])
            for k in range(2):
                a, b = HALF[k]
                for g_, dst, b_dst, bias_ in ((0, rg, b_rg2[k], ba), (1, ig, b_ig2[k], bx)):
                    for (c0, n) in nblocks(a, b):
                        ps = psum_f()
                        p.op("pe", lambda e: e.matmul(ps[0][:, 0:n], waxb[:, g_, n_, :], xcb[:, c0:c0 + n],
                                                      start=True, stop=True), reads=[b_wax, b_xcb2[k]], writes=[ps[1]])
                        p.op("act", lambda e: e.activation(dst[:, c0:c0 + n], ps[0][:, 0:n], AF.Sigmoid, bias=bias_),
                             reads=[ps[1], b_pvec], writes=[b_dst])
            for k in range(2):
                a, b = HALF[k]
                p.op("dve", lambda e: e.tensor_tensor(ig[:, a:b], ig[:, a:b], xc[:, a:b], ALU.mult),
                     reads=[b_ig2[k], b_xc2[k]], writes=[b_ig2[k]])
            for k in range(2):
                a, b = HALF[k]
                p.op("act", lambda e: e.activation(bb[:, a:b], rg[:, a:b], AF.Exp, scale=c16n),
                     reads=[b_rg2[k], b_const], writes=[b_bb2[k]])
                p.op("act", lambda e: e.activation(aa[:, a:b], rg[:, a:b], AF.Exp, scale=c8n),
                     reads=[b_rg2[k], b_const, b_ig2[k]], writes=[b_xc2[k]])
            for k in range(2):
                a, b = HALF[k]
                p.op("act", lambda e: e.activation(bb[:, a:b], bb[:, a:b], AF.Sqrt, scale=-1.0, bias=1.0),
                     reads=[b_bb2[k]], writes=[b_bb2[k]])
            for k in range(2):
                a, b = HALF[k]
                p.op("dve", lambda e: e.tensor_tensor(bb[:, a:b], bb[:, a:b], ig[:, a:b], ALU.mult),
                     reads=[b_bb2[k], b_ig2[k]], writes=[b_bb2[k]])
            p.op("dve", lambda e: e.tensor_tensor_scan(hh[:, 0:1024], aa[:, 0:1024], bb[:, 0:1024], 0.0,
                                                       ALU.mult, ALU.add),
                 reads=[b_xc2[0], b_bb2[0]], writes=[b_rg2[0]])
            p.op("dve", lambda e: e.tensor_tensor(mixT[:, 8 + n_, 0:64], hh[:, QOFF:1024], sgb[:, 0:64], ALU.mult),
                 reads=[b_rg2[0], b_sgb], writes=[b_mix[8 + n_]])
            p.op("dve", lambda e: e.tensor_tensor(hin[:, 0:1], hh[:, 1023:1024], flag, ALU.mult),
                 reads=[b_rg2[0], b_pvec], writes=[b_hin])
            p.op("dve", lambda e: e.tensor_tensor_scan(hh[:, 1024:2048], aa[:, 1024:2048], bb[:, 1024:2048],
                                                       hin[:, 0:1], ALU.mult, ALU.add),
                 reads=[b_xc2[1], b_bb2[1], b_hin], writes=[b_rg2[1]])
            for s_ in range(2):
                c0 = 2048 + 32 * s_
                p.op("dve", lambda e: e.tensor_tensor_scan(
                    hh[:, c0:c0 + 32], aa[:, c0:c0 + 32], bb[:, c0:c0 + 32],
                    h0sb[:, 2 * n_ + s_:2 * n_ + s_ + 1], ALU.mult, ALU.add),
                    reads=[b_xc2[1], b_bb2[1], b_h0], writes=[b_rg2[1]])
            p.op("dve", lambda e: e.tensor_tensor(mixT[:, 8 + n_, 64:NQ], hh[:, 1024:TALL], sgb[:, 64:NQ], ALU.mult),
                 reads=[b_rg2[1], b_sgb], writes=[b_mix[8 + n_]])
            p.op("pool", lambda e: e.tensor_copy(sto[:, ST_HP + n_:ST_HP + n_ + 1], hh[:, 2047:2048]),
                 reads=[b_rg2[1]], writes=[b_sto])
            for s_ in range(2):
                p.op("pool", lambda e: e.tensor_copy(sto[:, ST_HS + 2 * n_ + s_:ST_HS + 2 * n_ + s_ + 1],
                                                     hh[:, 2079 + 32 * s_:2080 + 32 * s_]),
                     reads=[b_rg2[1]], writes=[b_sto])

        wo_l0 = arena_t[:][:, base_top:base_top + (NKT * D) // 2].bitcast(BF16).rearrange("p (k c) -> p k c", c=D)
        b_wo_l0 = Buf()
        lru_P(0)
        for n_ in range(8):
            if n_ + 1 < 8:
                lru_P(n_ + 1)
            if n_ + 2 < 8:
                load_wB(n_ + 2)
            if n_ == 6:
                for g in range(8):
                    p.dma("pool", wo_l0[:, 2 * g:2 * g + 2, :],
                          wo0[256 * g:256 * (g + 1), :].rearrange("(k p) c -> p k c", p=128),
                          writes=[b_wo_l0] + b_xnT)
            lru_C(n_)
        p.dma("sp", sto_d, sto, reads=[b_sto])
        p.barrier()
        ar.top = base_top

        set_banks(6)
        WC0_OFF = ARENA_WORDS - 3072 - 8
        wC0_top = arena_t[:][:, WC0_OFF:WC0_OFF + 3072].bitcast(BF16).rearrange("p (k c) -> p k c", c=384)
        b_wC = [Buf() for _ in range(2)]
        for g in range(4):
            p.dma("pool", wC0_top[:, 4 * g:4 * g + 4, :],
                  wC[0, 512 * g:512 * (g + 1), :].rearrange("(k p) c -> p k c", p=128), writes=[b_wC[0]])
        ar.top = base_top
        wo = ar.bf(NKT * D).rearrange("p (k c) -> p k c", c=D)
        assert ar.top <= base_top + (NKT * TALL) // 2
        ar.top = l0_top
        xn1T = ar.bf(NKT * NQ).rearrange("p (k c) -> p k c", c=NQ)
        xn1_end = ar.top
        b_xn1 = [Buf() for _ in range(9)]
        xres_ = [ar.f32(D) for _ in range(2)]
        x1t_ = [ar.f32(D) for _ in range(2)]
        xs1_ = [ar.bf(D) for _ in range(2)]
        ss1_ = [ar.f32(8) for _ in range(2)]
        assert ar.top <= WC0_OFF, ("L0-D overlaps wC slot", ar.top, WC0_OFF)
        b_wo = b_wo_l0
        wo = wo_l0
        b_xres_ = [Buf() for _ in range(2)]
        b_x1t_ = [Buf() for _ in range(2)]
        b_xs1_ = [Buf() for _ in range(2)]
        b_ss1_ = [Buf() for _ in range(2)]
        b_scr = [Buf() for _ in range(9)]
        pend = None
        for j in range(9):
            i = j % 2
            xres, x1t = xres_[i], x1t_[i]
            p.dma("sp", xres, x_ext[QOFF + 128 * j:QOFF + 128 * (j + 1), :], writes=[b_xres_[i]])
            for cb_ in range(4):
                ps = psum_f()
                for kt in range(NKT):
                    p.op("pe", lambda e: e.matmul(
                        ps[0][:, 0:512], mixT[:, kt, 128 * j:128 * (j + 1)], wo[:, kt, 512 * cb_:512 * (cb_ + 1)],
                        start=(kt == 0), stop=(kt == NKT - 1)),
                        reads=[b_mix[kt], b_wo], writes=[ps[1]], inc=(kt == NKT - 1))
                p.op("dve", lambda e: e.tensor_tensor(
                    x1t[:, 512 * cb_:512 * (cb_ + 1)], ps[0][:, 0:512], xres[:, 512 * cb_:512 * (cb_ + 1)],
                    ALU.add), reads=[ps[1], b_xres_[i]], writes=[b_x1t_[i]])
            p.dma("sp", x1_scr[128 * j:128 * (j + 1), :], x1t, reads=[b_x1t_[i]], writes=[b_scr[j]])
            rms_part1(128, x1t, b_x1t_[i], xs1_[i], b_xs1_[i], ss1_[i][:, 0:1], ss1_[i][:, 1:2], b_ss1_[i])
            if pend is not None:
                rms_part2(*pend)
            pend = (128, xs1_[i], b_xs1_[i], PV_G1, xn1T, 128 * j, b_xn1[j])
        rms_part2(*pend)
        p.barrier()

        set_banks(8)
        ar.top = base_top
        wCb = [ar.bf(NKT * 384).rearrange("p (k c) -> p k c", c=384) for _ in range(2)]
        wc_end = ar.top
        wCb[0] = wC0_top
        zcT = ar.bf(16 * 1088).rearrange("p (f c) -> p f c", c=1088)
        b_zc = [Buf() for _ in range(16)]
        sgT = ar.bf(16 * 1088).rearrange("p (f c) -> p f c", c=1088)
        b_sg = [Buf() for _ in range(16)]
        sg_end = ar.top
        assert ar.top <= l0_top, (ar.top, l0_top)
        ar.top = xn1_end
        zb = [ar.f32(1216) for _ in range(2)]
        b_zb = [Buf() for _ in range(2)]
        sig = [ar.f32(512) for _ in range(3)]
        b_sig = [Buf() for _ in range(3)]
        acc = [ar.f32(1152) for _ in range(2)]
        b_acc = [Buf() for _ in range(2)]
        cbo = ar.f32(16 * 90)
        b_cbo = Buf()
        KPE = 16
        zbf = [ar.bf(1216) for _ in range(2)]
        b_zbf = [Buf() for _ in range(2)]
        dg = [ar.bf(KPE * 128).rearrange("p (t c) -> p t c", c=128) for _ in range(2)]
        b_dg = [Buf() for _ in range(2)]
        rr1 = {"sig": 0}

        def load_wC(f):
            for g in range(4):
                p.dma("pool", wCb[f % 2][:, 4 * g:4 * g + 4, :],
                      wC[f, 512 * g:512 * (g + 1), :].rearrange("(k p) c -> p k c", p=128),
                      writes=[b_wC[f % 2]])

        def proj1(wt, b_w, ccol, c0, n, ps):
            rd = [b_w] + [b_xn1[t] for t in range(c0 // 128, (c0 + n - 1) // 128 + 1)]
            for kt in range(NKT):
                p.op("pe", (lambda kt=kt: lambda e: e.matmul(ps[0][:, 0:n], wt[:, kt, ccol:ccol + 128],
                                                              xn1T[:, kt, c0:c0 + n],
                                                              start=(kt == 0), stop=(kt == NKT - 1)))(),
                     reads=rd, writes=[ps[1]], inc=(kt == NKT - 1))

        def zcol(c):
            if c < 1088:
                return c
            if c < 1120:
                return c + 30
            return c + 60

        def l1a_proj(f, fillers):
            wt, b_w = wCb[f % 2], b_wC[f % 2]
            z, b_z = zb[f % 2], b_zb[f % 2]
            csv = cs0T[:, f * 60:f * 60 + 60]
            p.dma("sp", z[:, 1088:1118], csv[:, 0:30], writes=[b_z])
            p.dma("sp", z[:, 1150:1180], csv[:, 30:60], writes=[b_z])
            for bi, (c0, n) in enumerate(nblocks(0, NQ)):
                psv = psum_f()
                proj1(wt, b_w, 0, c0, n, psv)
                psg = psum_f()
                proj1(wt, b_w, 128, c0, n, psg)
                si = rr1["sig"]
                rr1["sig"] = (si + 1) % 3
                sg_, b_s = sig[si], b_sig[si]
                p.op("act", lambda e: e.activation(sg_[:, 0:n], psg[0][:, 0:n], AF.Sigmoid),
                     reads=[psg[1]], writes=[b_s])
                if c0 + n <= 1088:
                    p.op("dve", lambda e: e.tensor_tensor(z[:, c0:c0 + n], psv[0][:, 0:n], sg_[:, 0:n], ALU.mult),
                         reads=[psv[1], b_s], writes=[b_z])
                else:
                    for (o, ln, zc0) in ((0, 64, 1024), (64, 32, 1118), (96, 32, 1180)):
                        p.op("dve", lambda e: e.tensor_tensor(z[:, zc0:zc0 + ln], psv[0][:, o:o + ln],
                                                              sg_[:, o:o + ln], ALU.mult),
                             reads=[psv[1], b_s], writes=[b_z])
                pst = psum_f()
                proj1(wt, b_w, 256, c0, n, pst)
                lo = max(c0, 64)
                si2 = rr1["sig"]
                rr1["sig"] = (si2 + 1) % 3
                sg2, b_s2 = sig[si2], b_sig[si2]
                p.op("act", lambda e: e.activation(sg2[:, 0:n], pst[0][:, 0:n], AF.Sigmoid),
                     reads=[pst[1]], writes=[b_s2])
                p.op("dve", lambda e: e.tensor_tensor(sgT[:, f, lo - 64:c0 + n - 64], pst[0][:, lo - c0:n],
                                                      sg2[:, lo - c0:n], ALU.mult),
                     reads=[pst[1], b_s2], writes=[b_sg[f]])
                for fn in fillers[bi]:
                    fn()
            for (k_, zc0) in ((0, 1088 - 30), (1, 1150 - 30), (2, 1212 - 30)):
                p.op("pool", lambda e: e.tensor_copy(cbo[:, f * 90 + 30 * k_:f * 90 + 30 * k_ + 30],
                                                     z[:, zc0:zc0 + 30]), reads=[b_z], writes=[b_cbo])
            zb_, b_zb_ = zbf[f % 2], b_zbf[f % 2]
            dg_, b_dg_ = dg[f % 2], b_dg[f % 2]
            p.op("pool", lambda e: e.tensor_tensor(
                dg_, ident.unsqueeze(1).to_broadcast([128, KPE, 128]),
                pvec[:, PV_DWW + f * 31:PV_DWW + f * 31 + KPE].unsqueeze(2).to_broadcast([128, KPE, 128]),
                ALU.mult), reads=[b_const, b_pvec], writes=[b_dg_])
            p.op("act", lambda e: e.activation(zb_[:, 0:1212], z[:, 0:1212], AF.Copy), reads=[b_z], writes=[b_zb_])

        def l1a_taps(f):
            z, b_z = zb[f % 2], b_zb[f % 2]
            ac, b_ac = acc[f % 2], b_acc[f % 2]
            dww = [pvec[:, PV_DWW + f * 31 + j:PV_DWW + f * 31 + j + 1] for j in range(31)]
            dwb = pvec[:, PV_DWB + f:PV_DWB + f + 1]
            fns = []
            fns.append(lambda: p.op("dve", lambda e: e.tensor_scalar(
                ac[:, 0:1148], z[:, 34 + KPE:34 + KPE + 1148], dww[KPE], dwb, ALU.mult, ALU.add),
                reads=[b_z, b_pvec], writes=[b_ac]))
            for j in range(KPE + 1, 31):
                fns.append((lambda j: lambda: p.op("dve", lambda e: e.scalar_tensor_tensor(
                    ac[:, 0:1148], z[:, 34 + j:34 + j + 1148], dww[j], ac[:, 0:1148], ALU.mult, ALU.add),
                    reads=[b_z, b_pvec, b_ac], writes=[b_ac]))(j))
            k = (len(fns) + 2) // 3
            return [fns[0:k], fns[k:2 * k], fns[2 * k:]]

        def l1a_conv(f):
            ac, b_ac = acc[f % 2], b_acc[f % 2]
            zb_, b_zb_ = zbf[f % 2], b_zbf[f % 2]
            dg_, b_dg_ = dg[f % 2], b_dg[f % 2]
            for (c0, n) in nblocks(0, 1148):
                ps = psum_f()
                for t in range(KPE):
                    p.op("pe", lambda e: e.matmul(ps[0][:, 0:n], dg_[:, t, :], zb_[:, 34 + t + c0:34 + t + c0 + n],
                                                  start=(t == 0), stop=(t == KPE - 1)),
                         reads=[b_dg_, b_zb_], writes=[ps[1]], inc=(t == KPE - 1))
                p.op("dve", lambda e: e.tensor_tensor(ac[:, c0:c0 + n], ps[0][:, 0:n], ac[:, c0:c0 + n], ALU.add),
                     reads=[ps[1], b_ac], writes=[b_ac])
            p.op("act", lambda e: e.activation(zcT[:, f, 0:1024], ac[:, 0:1024], AF.Copy),
                 reads=[b_ac], writes=[b_zc[f]])
            p.op("act", lambda e: e.activation(zcT[:, f, 1024:1056], ac[:, 1054:1086], AF.Copy),
                 reads=[b_ac], writes=[b_zc[f]])
            p.op("act", lambda e: e.activation(zcT[:, f, 1056:1088], ac[:, 1116:1148], AF.Copy),
                 reads=[b_ac], writes=[b_zc[f]])

        wo_l1 = arena_t[:][:, l0_top:l0_top + (NKT * D) // 2].bitcast(BF16).rearrange("p (k c) -> p k c", c=D)
        b_woc = [Buf() for _ in range(8)]
        assert 4 * 2 * D // 2 <= (NKT * NQ) // 2
        assert ar.top <= WC0_OFF, ("L1-A overlaps wC slot", ar.top, WC0_OFF)
        load_wC(1)
        for f in range(16):
            l1a_proj(f, l1a_taps(f - 1) if f >= 1 else [[], [], []])
            if f + 2 < 16:
                load_wC(f + 2)
            if f == 15:
                for g in range(4):
                    p.dma("pool", wo_l1[:, 2 * g:2 * g + 2, :],
                          wo1[256 * g:256 * (g + 1), :].rearrange("(k p) c -> p k c", p=128),
                          writes=[b_woc[g]] + b_xn1)
            if f >= 1:
                l1a_conv(f - 1)
        for grp in l1a_taps(15):
            for fn in grp:
                fn()
        l1a_conv(15)
        p.dma("sp", cbo_d, cbo, reads=[b_cbo])
        p.barrier()

        ar.top = base_top
        mean = ar.f32(1088)
        rstd = ar.f32(1088)
        sq = [ar.bf(512) for _ in range(2)]
        tt = [ar.f32(512) for _ in range(3)]
        assert ar.top <= wc_end, (ar.top, wc_end)
        ar.top = sg_end
        xres_ = [ar.f32(D)]
        assert ar.top <= l0_top, (ar.top, l0_top)
        ar.top = l0_top
        wo = ar.bf(NKT * D).rearrange("p (k c) -> p k c", c=D)
        wo = wo_l1
        xres_.append(ar.f32(D))
        yt_ = [ar.f32(D) for _ in range(2)]
        ss2_ = [ar.f32(8) for _ in range(2)]
        b_xres_ = [Buf() for _ in range(2)]
        b_yt_ = [Buf() for _ in range(2)]
        b_ss2_ = [Buf() for _ in range(2)]
        b_sq = [Buf() for _ in range(2)]
        b_tt = [Buf() for _ in range(3)]
        NB1 = [(0, 256), (256, 512), (768, 320)]
        b_st = [Buf() for _ in NB1]
        b_zc2 = [[Buf() for _ in NB1] for _ in range(16)]
        for g in range(4, 8):
            p.dma("pool", wo[:, 2 * g:2 * g + 2, :],
                  wo1[256 * g:256 * (g + 1), :].rearrange("(k p) c -> p k c", p=128), writes=[b_woc[g]])
        rr2 = {"tt": 0}

        def l1_stats(nb):
            c0, n = NB1[nb]
            ps1 = psum_f()
            ps2 = psum_f()
            for f in range(16):
                i = f % 2
                p.op("act", lambda e: e.activation(sq[i][:, 0:n], zcT[:, f, c0:c0 + n], AF.Square),
                     reads=[b_zc2[f][nb]], writes=[b_sq[i]])
                p.op("pe", lambda e: e.matmul(ps1[0][:, 0:n], onesdiv, zcT[:, f, c0:c0 + n],
                                              start=(f == 0), stop=(f == 15)),
                     reads=[b_zc2[f][nb], b_const], writes=[ps1[1]], inc=(f == 15))
                p.op("pe", lambda e: e.matmul(ps2[0][:, 0:n], onesdiv, sq[i][:, 0:n],
                                              start=(f == 0), stop=(f == 15)),
                     reads=[b_sq[i], b_const], writes=[ps2[1]], inc=True)
            bs = b_st[nb]
            p.op("act", lambda e: e.activation(mean[:, c0:c0 + n], ps1[0][:, 0:n], AF.Copy),
                 reads=[ps1[1]], writes=[bs])
            p.op("act", lambda e: e.activation(rstd[:, c0:c0 + n], ps1[0][:, 0:n], AF.Square),
                 reads=[ps1[1]], writes=[bs])
            p.op("dve", lambda e: e.tensor_tensor(rstd[:, c0:c0 + n], ps2[0][:, 0:n], rstd[:, c0:c0 + n],
                                                  ALU.subtract), reads=[ps2[1], bs], writes=[bs])
            p.op("dve", lambda e: e.tensor_scalar(rstd[:, c0:c0 + n], rstd[:, c0:c0 + n], 0.0, None, ALU.max),
                 reads=[bs], writes=[bs])
            p.op("act", lambda e: e.activation(rstd[:, c0:c0 + n], rstd[:, c0:c0 + n], AF.Sqrt, bias=EPS),
                 reads=[bs], writes=[bs])
            p.op("dve", lambda e: e.reciprocal(rstd[:, c0:c0 + n], rstd[:, c0:c0 + n]), reads=[bs], writes=[bs])

        def l1_ln_tasks(nb):
            c0, n = NB1[nb]
            bs = b_st[nb]
            slot = {}

            def front(f):
                i = rr2["tt"]
                rr2["tt"] = (i + 1) % 3
                slot[f] = i
                t_ = tt[i][:, 0:n]
                lng = pvec[:, PV_LNG + f:PV_LNG + f + 1]
                lnb = pvec[:, PV_LNB + f:PV_LNB + f + 1]
                zsl = zcT[:, f, c0:c0 + n]
                p.op("dve", lambda e: e.tensor_tensor(t_, zsl, mean[:, c0:c0 + n], ALU.subtract),
                     reads=[b_zc2[f][nb], bs], writes=[b_tt[i]])
                p.op("dve", lambda e: e.tensor_tensor(t_, t_, rstd[:, c0:c0 + n], ALU.mult),
                     reads=[b_tt[i], bs], writes=[b_tt[i]])
                p.op("act", lambda e: e.activation(t_, t_, AF.Silu, scale=lng, bias=lnb),
                     reads=[b_tt[i], b_pvec], writes=[b_tt[i]])

            def back(f):
                i = slot[f]
                t_ = tt[i][:, 0:n]
                zsl = zcT[:, f, c0:c0 + n]
                p.op("dve", lambda e: e.tensor_tensor(zsl, t_, sgT[:, f, c0:c0 + n], ALU.mult),
                     reads=[b_tt[i], b_sg[f]], writes=[b_zc2[f][nb]])

            def one(f):
                if f < 16:
                    front(f)
                if f >= 1:
                    back(f - 1)
            return [(lambda f=f: one(f)) for f in range(17)]

        def l1_wout(nb, fillers=()):
            fillers = list(fillers)
            c0, n = NB1[nb]
            nslots = 4 * len(nblocks(c0, c0 + n, 128))
            per_slot = (len(fillers) + nslots - 1) // nslots
            for (r0, nrows) in nblocks(c0, c0 + n, 128):
                j = r0 // 128
                i = j % 2
                xres, yt, ss2 = xres_[i], yt_[i], ss2_[i]
                p.dma("sp", xres[0:nrows, :], x1_scr[64 + r0:64 + r0 + nrows, :], reads=b_scr, writes=[b_xres_[i]])
                for cb_ in range(4):
                    ps = psum_f()
                    for kt in range(NKT):
                        p.op("pe", lambda e: e.matmul(
                            ps[0][0:nrows, 0:512], zcT[:, kt, r0:r0 + nrows],
                            wo[:, kt, 512 * cb_:512 * (cb_ + 1)], start=(kt == 0), stop=(kt == NKT - 1)),
                            reads=[b_zc2[kt][nb], b_woc[kt // 2]], writes=[ps[1]], inc=(kt == NKT - 1))
                    p.op("dve", lambda e: e.tensor_tensor(
                        xres[0:nrows, 512 * cb_:512 * (cb_ + 1)], ps[0][0:nrows, 0:512],
                        xres[0:nrows, 512 * cb_:512 * (cb_ + 1)], ALU.add),
                        reads=[ps[1], b_xres_[i]], writes=[b_xres_[i]])
                    for _ in range(per_slot):
                        if fillers:
                            fillers.pop(0)()
                p.op("act", lambda e: e.activation(yt[0:nrows, :], xres[0:nrows, :], AF.Square,
                                                   accum_out=ss2[0:nrows, 0:1]),
                     reads=[b_xres_[i]], writes=[b_yt_[i], b_ss2_[i]])
                p.op("act", lambda e: e.activation(ss2[0:nrows, 1:2], ss2[0:nrows, 0:1], AF.Sqrt,
                                                   scale=1.0 / D, bias=EPS), reads=[b_ss2_[i]], writes=[b_ss2_[i]])
                p.op("dve", lambda e: e.reciprocal(ss2[0:nrows, 1:2], ss2[0:nrows, 1:2]),
                     reads=[b_ss2_[i]], writes=[b_ss2_[i]])
                p.op("dve", lambda e: e.scalar_tensor_tensor(
                    yt[0:nrows, :], xres[0:nrows, :], ss2[0:nrows, 1:2], fnb[0:nrows, :], ALU.mult, ALU.mult),
                    reads=[b_xres_[i], b_ss2_[i], b_fnb], writes=[b_yt_[i]])
                dst = y_main[r0:r0 + 128, :] if r0 < 1024 else y_samp[0:64, :]
                p.dma("sp", dst, yt[0:nrows, :], reads=[b_yt_[i]])
            for fn in fillers:
                fn()

        l1_stats(0)
        for fn in l1_ln_tasks(0):
            fn()
        l1_stats(1)
        l1_wout(0, l1_ln_tasks(1))
        l1_stats(2)
        l1_wout(1, l1_ln_tasks(2))
        l1_wout(2)
        p.finish()
        p.emit(st)
    return nc


def _bias_tables(rel):
    r = np.arange(128)
    out = np.full((8, 128, 704), NEG, np.float32)
    q = np.arange(128)
    for j in range(5):
        kpos = (j - 4) * 128 + r
        kc_ = 2 * (j - 4) + (r >= 64)
        qc = (q >= 64).astype(int)
        vis = (kc_[:, None] >= qc[None, :] - 8) & (kc_[:, None] <= qc[None, :])
        idx = np.clip(q[None, :] - kpos[:, None], -128, 128) + 128
        vals = rel[:, idx]
        out[:, :, j * 128:(j + 1) * 128] = np.where(vis[None], vals, NEG)
    idx = np.clip(q[None, :32] - r[:32, None], -128, 128) + 128
    vals = rel[:, idx]
    out[:, 0:32, 640:672] = vals
    out[:, 32:64, 672:704] = vals
    return out


_NC_CACHE = {}
LAST_DBG = None


def kernel(x_prompt, x_sample, cache_attn_k, cache_attn_v, state_lru_h, state_lru_conv, state_conv,
           norm_ab, w_in_ab, w_out_ab, rel_bias, lru_conv_w, lru_conv_b, lru_w_a, lru_b_a, lru_w_x,
           lru_b_x, lru_lambda, norm_cv, w_in_cv, w_out_cv, dw_w, dw_b, ln_g, ln_b, final_norm):
    f = lambda a: np.ascontiguousarray(np.asarray(a, dtype=np.float32))
    x_prompt, x_sample = f(x_prompt), f(x_sample)
    w_in = f(w_in_ab)[0]
    wA = np.stack([np.concatenate([w_in[:, c * 1024 + h * 128:c * 1024 + (h + 1) * 128] for c in range(4)], axis=1)
                   for h in range(8)])
    wB = np.stack([np.concatenate([w_in[:, 4096 + n * 128:4096 + (n + 1) * 128],
                                   w_in[:, 5120 + n * 128:5120 + (n + 1) * 128]], axis=1) for n in range(8)])
    w_cv = f(w_in_cv)[0]
    wC = np.stack([np.concatenate([w_cv[:, c * 2048 + k * 128:c * 2048 + (k + 1) * 128] for c in range(3)], axis=1)
                   for k in range(16)])
    wo0 = f(w_out_ab)[0]
    wo1 = f(w_out_cv)[0]
    wax = np.stack([f(lru_w_a)[0], f(lru_w_x)[0]])
    wax = np.ascontiguousarray(wax.transpose(2, 0, 1, 3)).reshape(128, 2 * 8 * 128)
    col = lambda v, k: np.ascontiguousarray(f(v).reshape(k, 128).T)
    pv = np.zeros((8, 128, PV_N), np.float32)
    base = np.zeros((128, PV_N), np.float32)
    base[:, PV_G0:PV_G0 + 16] = col(norm_ab[0], 16)
    base[:, PV_G1:PV_G1 + 16] = col(norm_cv[0], 16)
    cw = f(lru_conv_w)[0]
    base[:, PV_CW:PV_CW + 32] = cw.reshape(4, 8, 128).transpose(2, 1, 0).reshape(128, 32)
    base[:, PV_CB:PV_CB + 8] = col(lru_conv_b[0], 8)
    base[:, PV_BA:PV_BA + 8] = col(lru_b_a[0], 8)
    base[:, PV_BX:PV_BX + 8] = col(lru_b_x[0], 8)
    base[:, PV_LAM:PV_LAM + 8] = col(lru_lambda[0], 8)
    dww = f(dw_w)[0]
    base[:, PV_DWW:PV_DWW + 496] = dww.reshape(31, 16, 128).transpose(2, 1, 0).reshape(128, 496)
    base[:, PV_DWB:PV_DWB + 16] = col(dw_b[0], 16)
    base[:, PV_LNG:PV_LNG + 16] = col(ln_g[0], 16)
    base[:, PV_LNB:PV_LNB + 16] = col(ln_b[0], 16)
    fnb = np.ascontiguousarray(np.broadcast_to(f(final_norm)[None, :], (128, D)))
    biasT = _bias_tables(f(rel_bias)[0])
    ck, cv = f(cache_attn_k)[0], f(cache_attn_v)[0]
    slh, slc, scv = f(state_lru_h)[0], f(state_lru_conv)[0], f(state_conv)[0]

    in_maps = []
    for c in range(8):
        b, half = c // 2, c % 2
        xe = np.zeros((TALL, D), np.float32)
        if half == 1:
            xe[0:1024] = x_prompt[b, 0:1024]
        xe[1024:2048] = x_prompt[b, half * 1024:(half + 1) * 1024]
        xe[2048:2112] = x_sample[2 * c:2 * c + 2].reshape(64, D)
        pvc = base.copy()
        pvc[:, PV_FLAG] = float(half)
        h0 = slh[2 * c:2 * c + 2]
        h0T = np.ascontiguousarray(h0.reshape(2, 8, 128).transpose(2, 1, 0)).reshape(128, 16)
        lb = slc[2 * c:2 * c + 2]
        lb0T = np.ascontiguousarray(lb.reshape(2, 3, 8, 128).transpose(3, 2, 0, 1)).reshape(128, 48)
        cs = scv[2 * c:2 * c + 2]
        cs0T = np.ascontiguousarray(cs.reshape(2, 30, 16, 128).transpose(3, 2, 0, 1)).reshape(128, 960)
        in_maps.append({
            "x_ext": xe,
            "kc": np.ascontiguousarray(ck[2 * c:2 * c + 2].reshape(2, 512, 1024)),
            "vc": np.ascontiguousarray(cv[2 * c:2 * c + 2].reshape(2, 512, 1024)),
            "h0T": h0T, "lb0T": lb0T, "cs0T": cs0T,
            "wA": wA, "wB": wB, "wC": wC, "wo0": wo0, "wo1": wo1, "wax": wax,
            "pvec": pvc, "fnb": fnb, "biasT": biasT,
        })
    if "nc" not in _NC_CACHE:
        _NC_CACHE["nc"] = build_program()
    res = run_bass_kernel_spmd(_NC_CACHE["nc"], in_maps, core_ids=list(range(8)))
    R = res.results
    global LAST_DBG
    LAST_DBG = [r.get("dbg") for r in R] if DEBUG else None

    y_prompt = np.zeros((4, 2048, D), np.float32)
    y_sample = np.zeros((16, 32, D), np.float32)
    k_p = np.zeros((1, 4, 512, 8, 128), np.float32)
    v_p = np.zeros_like(k_p)
    h_p = np.zeros((1, 4, 1024), np.float32)
    lb_p = np.zeros((1, 4, 3, 1024), np.float32)
    cb_p = np.zeros((1, 4, 30, 2048), np.float32)
    k_s = np.zeros((1, 16, 32, 8, 128), np.float32)
    v_s = np.zeros_like(k_s)
    h_s = np.zeros((1, 16, 1024), np.float32)
    lb_s = np.zeros((1, 16, 3, 1024), np.float32)
    cb_s = np.zeros((1, 16, 30, 2048), np.float32)
    for c in range(8):
        b, half = c // 2, c % 2
        r = R[c]
        y_prompt[b, half * 1024:(half + 1) * 1024] = r["y_main"]
        y_sample[2 * c:2 * c + 2] = r["y_samp"].reshape(2, 32, D)
        sto = r["sto"]
        cbo = r["cbo"].reshape(128, 16, 3, 30)
        kvs = r["kvo_s"].reshape(2, 2, 32, 8, 128)
        k_s[0, 2 * c:2 * c + 2] = kvs[0]
        v_s[0, 2 * c:2 * c + 2] = kvs[1]
        hs = sto[:, ST_HS:ST_HS + 16].reshape(128, 8, 2)
        h_s[0, 2 * c:2 * c + 2] = hs.transpose(2, 1, 0).reshape(2, 1024)
        ls = sto[:, ST_LS:ST_LS + 48].reshape(128, 8, 2, 3)
        lb_s[0, 2 * c:2 * c + 2] = ls.transpose(2, 3, 1, 0).reshape(2, 3, 1024)
        cb_s[0, 2 * c:2 * c + 2] = cbo[:, :, 1:3, :].transpose(2, 3, 1, 0).reshape(2, 30, 2048)
        if half == 1:
            kvp = r["kvo_p"].reshape(2, 512, 8, 128)
            k_p[0, b] = kvp[0]
            v_p[0, b] = kvp[1]
            h_p[0, b] = sto[:, ST_HP:ST_HP + 8].T.reshape(1024)
            lb_p[0, b] = sto[:, ST_LP:ST_LP + 24].reshape(128, 8, 3).transpose(2, 1, 0).reshape(3, 1024)
            cb_p[0, b] = cbo[:, :, 0, :].transpose(2, 1, 0).reshape(30, 2048)
    return (y_prompt, y_sample, k_p, v_p, h_p, lb_p, cb_p, k_s, v_s, h_s, lb_s, cb_s)
```

```python
import numpy as np
from contextlib import ExitStack
import concourse.bass as bass
import concourse.mybir as mybir
from concourse.bass_utils import run_bass_kernel_spmd

F32 = mybir.dt.float32
BF16 = mybir.dt.bfloat16
AF = mybir.ActivationFunctionType
ALU = mybir.AluOpType

ENGS = ("pe", "act", "dve", "pool", "sp")
import os
DEBUG = bool(os.environ.get("KDEBUG"))
DBG_COLS = 40000
DBG_MAP = {}
D = 2048
NKT = 16
TALL = 2112
QOFF = 960
NQ = 1152
EPS = 1e-6
NEG = -30000.0

PV_G0, PV_G1, PV_CW, PV_CB, PV_BA, PV_BX, PV_LAM = 0, 16, 32, 64, 72, 80, 88
PV_DWW, PV_DWB, PV_LNG, PV_LNB, PV_FLAG, PV_N = 96, 592, 608, 624, 640, 641
ST_HP, ST_HS, ST_LP, ST_LS, ST_N = 0, 8, 24, 48, 96


class Buf:
    __slots__ = ("name", "w", "r")

    def __init__(self, name=""):
        self.name = name
        self.w = None
        self.r = {}


class _Rec:
    def __getattr__(self, name):
        def f(*a, **k):
            return (name, a, k)
        return f


_REC = _Rec()


class Prog:
    NDMA = 6

    def __init__(self, nc):
        self.nc = nc
        self.q = {e: [] for e in ENGS}
        self.cnt = {e: 0 for e in ENGS}
        self.seen = {e: {} for e in ENGS}
        self.pe_pending = False
        self.dma_rr = {e: 0 for e in ENGS}
        self.dma_val = {}

    def _need(self, eng, deps):
        best = {}
        for t in deps:
            if t is None:
                continue
            key, val = t
            if key == eng and eng == "pe":
                continue
            if key == "pe":
                assert val <= self.cnt["pe"], "dependency on un-incremented PE op"
            if best.get(key, 0) < val:
                best[key] = val
        for key, val in best.items():
            if self.seen[eng].get(key, 0) >= val:
                continue
            self.seen[eng][key] = val
            self.q[eng].append(("wait", key, val))

    def _deps(self, reads, writes):
        deps = []
        for b in reads:
            deps.append(b.w)
        for b in writes:
            deps.append(b.w)
            deps.extend(b.r.values())
        return deps

    def _commit(self, ticket, reads, writes):
        k = ticket[0]
        for b in reads:
            if b.r.get(k, (k, 0))[1] < ticket[1]:
                b.r[k] = ticket
        for b in writes:
            b.w = ticket
            b.r = {}

    def op(self, eng, fn, reads=(), writes=(), inc=True):
        self._need(eng, self._deps(reads, writes))
        if inc:
            self.cnt[eng] += 1
            ticket = (eng, self.cnt[eng])
            if eng == "pe":
                self.pe_pending = False
        else:
            assert eng == "pe"
            ticket = (eng, self.cnt[eng] + 1)
            self.pe_pending = True
        self.q[eng].append(("op", fn(_REC), inc))
        self._commit(ticket, reads, writes)
        return ticket

    DESC_LIMIT = 1400

    def dma(self, queue, out, in_, reads=(), writes=(), **kw):
        idx = self.dma_rr[queue]
        self.dma_rr[queue] = (idx + 1) % self.NDMA
        key = ("dma", queue, idx)
        prev = self.dma_val.get(key, 0)
        deps = self._deps(reads, writes)
        if prev:
            deps.append((key, prev))
        if queue == "sp":
            shp = list(out.shape)
            nd = 1
            for d_ in shp[:-1]:
                nd *= d_
            nd *= (shp[-1] * 4 + 4095) // 4096
            fifo = self.__dict__.setdefault("_fifo", [])
            while fifo and sum(x[1] for x in fifo) + nd > self.DESC_LIMIT:
                deps.append(fifo.pop(0)[0])
            self._pending_nd = nd
        self._need(queue, deps)
        val = prev + 16
        self.dma_val[key] = val
        self.q[queue].append(("dma", out, in_, key, kw))
        ticket = (key, val)
        if queue == "sp":
            self._fifo.append((ticket, self._pending_nd))
        self._commit(ticket, reads, writes)
        return ticket

    def barrier(self):
        assert not self.pe_pending
        tickets = [(e, self.cnt[e]) for e in ENGS if self.cnt[e] > 0]
        tickets += [(k, v) for k, v in self.dma_val.items()]
        for e in ENGS:
            self._need(e, [t for t in tickets if t[0] != e])

    def finish(self):
        for key, val in self.dma_val.items():
            if self.seen["sp"].get(key, 0) < val:
                self.seen["sp"][key] = val
                self.q["sp"].append(("wait", key, val))

    def emit(self, stack):
        nc = self.nc
        assert not self.pe_pending
        sems = {}
        for e in ENGS:
            sems[e] = stack.enter_context(nc.semaphore("s_" + e))
        for key in self.dma_val:
            sems[key] = stack.enter_context(nc.semaphore("d_%s_%d" % (key[1], key[2])))
        block = stack.enter_context(nc.Block())
        handles = {"pe": block.tensor, "act": block.scalar, "dve": block.vector,
                   "pool": block.gpsimd, "sp": block.sync}

        def run(ename):
            items = self.q[ename]

            def body(eng):
                for it in items:
                    if it[0] == "wait":
                        eng.wait_ge(sems[it[1]], it[2])
                    elif it[0] == "op":
                        ins = getattr(eng, it[1][0])(*it[1][1], **it[1][2])
                        if it[2]:
                            ins.then_inc(sems[ename], 1)
                    else:
                        _, out, in_, key, kw = it
                        eng.dma_start(out=out, in_=in_, **kw).then_inc(sems[key], 16)
            return body

        for e in ENGS:
            if self.q[e]:
                handles[e](run(e))


ARENA_WORDS = 53000


class Arena:
    def __init__(self, ap):
        self.ap = ap
        self.top = 0

    def f32(self, n):
        na = (n + 7) // 8 * 8
        assert self.top + na <= ARENA_WORDS, ("arena overflow", self.top, na)
        v = self.ap[:, self.top:self.top + n]
        self.top += na
        return v

    def bf(self, n):
        assert n % 2 == 0
        v = self.f32(n // 2)
        return v.bitcast(BF16)


def nblocks(lo, hi, step=512):
    out = []
    c = lo
    while c < hi:
        n = min(step, hi - c)
        out.append((c, n))
        c += n
    return out


def build_program():
    nc = bass.Bass("TRN2", target_bir_lowering=False)

    def din(name, shape):
        return nc.dram_tensor(name, list(shape), F32, kind="ExternalInput").ap()

    def dout(name, shape):
        return nc.dram_tensor(name, list(shape), F32, kind="ExternalOutput").ap()

    x_ext = din("x_ext", [TALL, D])
    kc = din("kc", [2, 512, 1024])
    vc = din("vc", [2, 512, 1024])
    h0T = din("h0T", [128, 16])
    lb0T = din("lb0T", [128, 48])
    cs0T = din("cs0T", [128, 16 * 60])
    wA = din("wA", [8, D, 512])
    wB = din("wB", [8, D, 256])
    wC = din("wC", [16, D, 384])
    wo0 = din("wo0", [D, D])
    wo1 = din("wo1", [D, D])
    wax = din("wax", [128, 2 * 8 * 128])
    pvec_d = din("pvec", [128, PV_N])
    fnb_d = din("fnb", [128, D])
    biasT_d = din("biasT", [8, 128, 704])

    y_main = dout("y_main", [1024, D])
    y_samp = dout("y_samp", [64, D])
    kvo_p = dout("kvo_p", [2, 512, 1024])
    kvo_s = dout("kvo_s", [2, 64, 1024])
    sto_d = dout("sto", [128, ST_N])
    cbo_d = dout("cbo", [128, 16 * 90])
    x1_scr = nc.dram_tensor("x1_scr", [NQ, D], F32, kind="Internal").ap()
    dbg_d = dout("dbg", [128, DBG_COLS]) if DEBUG else None
    DBG_MAP.clear()
    dbg_state = {"off": 0}

    def dbg(name, ap, ncols, buf, nrows=128):
        if not DEBUG or name in DBG_MAP:
            return
        o = dbg_state["off"]
        DBG_MAP[name] = (o, ncols, nrows)
        dbg_state["off"] = o + ncols
        p.dma("sp", dbg_d[0:nrows, o:o + ncols], ap, reads=[buf])

    st = ExitStack()
    with st:
        arena_t = st.enter_context(nc.sbuf_tensor("arena", [128, ARENA_WORDS], F32))
        psT = [st.enter_context(nc.psum_tensor("psT%d" % i, [128, 1024], BF16)) for i in range(2)]
        psF = [st.enter_context(nc.psum_tensor("psF%d" % i, [128, 512], F32)) for i in range(6)]
        psT_b = [Buf("psT%d" % i) for i in range(2)]
        psF_b = [Buf("psF%d" % i) for i in range(6)]
        p = Prog(nc)
        ar = Arena(arena_t[:])
        rr = {"T": 0, "F": 0}

        psF_all = [t_[:] for t_ in psF] + [t_[:].bitcast(F32) for t_ in psT]
        psF_b_all = psF_b + psT_b
        rr["NF"] = 6

        def set_banks(n):
            rr["NF"] = n
            rr["F"] %= n

        def psum_f():
            i = rr["F"]
            rr["F"] = (i + 1) % rr["NF"]
            return psF_all[i], psF_b_all[i]

        def psum_t():
            i = rr["T"]
            rr["T"] = (i + 1) % 2
            return psT[i][:], psT_b[i]

        pvec = ar.f32(PV_N)
        b_pvec = Buf("pvec")
        fnb = ar.f32(D)
        b_fnb = Buf("fnb")
        identf = ar.f32(128)
        ident = ar.bf(128)
        ones_bf = ar.bf(128)
        flag_bf = ar.bf(128)
        onesdiv = ar.bf(128)
        c8 = ar.f32(8)
        c16 = ar.f32(8)
        lsc = ar.f32(64)
        sto = ar.f32(ST_N)
        b_const = Buf("const")
        b_sto = Buf("sto")
        p.dma("sp", pvec, pvec_d, writes=[b_pvec])
        p.dma("sp", fnb, fnb_d, writes=[b_fnb])
        p.op("pool", lambda e: e.memset(identf, 0.0), writes=[b_const])
        p.op("pool", lambda e: e.affine_select(identf, identf, [[-1, 128]], ALU.not_equal, 1.0,
                                               base=0, channel_multiplier=1),
             reads=[b_const], writes=[b_const])
        p.op("dve", lambda e: e.tensor_copy(ident, identf), reads=[b_const], writes=[b_const])
        p.op("dve", lambda e: e.memset(ones_bf, 1.0), writes=[b_const])
        p.op("dve", lambda e: e.memset(onesdiv, 1.0 / D), writes=[b_const])
        p.op("dve", lambda e: e.memset(sto, 0.0), writes=[b_sto])
        flag = pvec[:, PV_FLAG:PV_FLAG + 1]
        p.op("dve", lambda e: e.tensor_scalar(flag_bf, ones_bf, flag, None, ALU.mult),
             reads=[b_const, b_pvec], writes=[b_const])
        lam = pvec[:, PV_LAM:PV_LAM + 8]
        t_abs, t_y, t_w, t_w2, t_s, t_m = (lsc[:, 8 * i:8 * i + 8] for i in range(6))
        RW = dict(reads=[b_const, b_pvec], writes=[b_const])
        p.op("dve", lambda e: e.tensor_scalar(t_abs, lam, -1.0, None, ALU.mult), **RW)
        p.op("dve", lambda e: e.tensor_tensor(t_abs, t_abs, lam, ALU.max), **RW)
        p.op("act", lambda e: e.activation(t_y, t_abs, AF.Exp, scale=-1.0), **RW)
        p.op("dve", lambda e: e.tensor_scalar(t_w, t_y, 2.0, None, ALU.add), **RW)
        p.op("dve", lambda e: e.reciprocal(t_w, t_w), **RW)
        p.op("dve", lambda e: e.tensor_tensor(t_w, t_w, t_y, ALU.mult), **RW)
        p.op("dve", lambda e: e.tensor_tensor(t_w2, t_w, t_w, ALU.mult), **RW)
        p.op("dve", lambda e: e.memset(t_s, 1.0 / 15.0), **RW)
        for kk in (13, 11, 9, 7, 5, 3, 1):
            p.op("dve", lambda e: e.tensor_tensor(t_s, t_s, t_w2, ALU.mult), **RW)
            p.op("dve", (lambda cst: (lambda e: e.tensor_scalar(t_s, t_s, cst, None, ALU.add)))(1.0 / kk), **RW)
        p.op("dve", lambda e: e.tensor_tensor(t_s, t_s, t_w, ALU.mult), **RW)
        p.op("dve", lambda e: e.tensor_scalar(t_m, lam, 0.0, None, ALU.min), **RW)
        p.op("dve", lambda e: e.scalar_tensor_tensor(t_s, t_s, -2.0, t_m, ALU.mult, ALU.add), **RW)
        p.op("dve", lambda e: e.tensor_scalar(c8, t_s, 8.0, None, ALU.mult), **RW)
        p.op("dve", lambda e: e.tensor_scalar(c16, t_s, 16.0, None, ALU.mult), **RW)

        base_top = ar.top

        xnT = ar.bf(NKT * TALL).rearrange("p (k c) -> p k c", c=TALL)
        b_xnT = [Buf("xnT%d" % t) for t in range(17)]
        mixT = ar.bf(NKT * NQ).rearrange("p (k c) -> p k c", c=NQ)
        b_mix = [Buf("mix%d" % k) for k in range(NKT)]
        l0_top = ar.top

        def xn_bufs(c0, n):
            return [b_xnT[t] for t in range(c0 // 128, (c0 + n - 1) // 128 + 1)]

        def rms_part1(nrows, xt, b_xt, xs, b_xs, ss, rstd, b_s):
            p.op("act", lambda e: e.activation(xs[0:nrows, :], xt[0:nrows, :], AF.Square, accum_out=ss[0:nrows, :]),
                 reads=[b_xt], writes=[b_xs, b_s])
            p.op("act", lambda e: e.activation(rstd[0:nrows, :], ss[0:nrows, :], AF.Sqrt, scale=1.0 / D, bias=EPS),
                 reads=[b_s], writes=[b_s])
            p.op("dve", lambda e: e.reciprocal(rstd[0:nrows, :], rstd[0:nrows, :]), reads=[b_s], writes=[b_s])
            p.op("act", lambda e: e.activation(xs[0:nrows, :], xt[0:nrows, :], AF.Copy, scale=rstd[0:nrows, :]),
                 reads=[b_xt, b_s], writes=[b_xs])

        def rms_part2(nrows, xs, b_xs, gcol, dstT, col0, b_dst):
            for half in range(2):
                pt, bpt = psum_t()
                for i in range(8):
                    kt = half * 8 + i
                    p.op("pe", lambda e: e.transpose(
                        pt[:, i * 128:i * 128 + nrows], xs[0:nrows, kt * 128:(kt + 1) * 128],
                        ident[0:nrows, 0:nrows]),
                        reads=[b_xs, b_const], writes=[bpt], inc=(i == 7))
                gb = pvec[:, gcol + half * 8:gcol + half * 8 + 8].unsqueeze(2).to_broadcast([128, 8, nrows])
                src = pt[:, 0:1024].rearrange("p (k c) -> p k c", c=128)[:, :, 0:nrows]
                p.op("dve", lambda e: e.tensor_tensor(dstT[:, half * 8:half * 8 + 8, col0:col0 + nrows], src, gb,
                                                      ALU.mult),
                     reads=[bpt, b_pvec], writes=[b_dst])

        wAb = [ar.bf(NKT * 512).rearrange("p (k c) -> p k c", c=512) for _ in range(2)]
        b_wA = [Buf() for _ in range(2)]
        for h_ in range(2):
            for g in range(4):
                p.dma("pool", wAb[h_][:, 4 * g:4 * g + 4, :],
                      wA[h_, 512 * g:512 * (g + 1), :].rearrange("(k p) c -> p k c", p=128),
                      writes=[b_wA[h_]])
        NXT = 4
        xt_ = [ar.f32(D) for _ in range(NXT)]
        xs_ = [ar.bf(D) for _ in range(2)]
        sst = [ar.f32(8) for _ in range(2)]
        b_xt = [Buf() for _ in range(NXT)]
        b_xs = [Buf() for _ in range(2)]
        b_ss = [Buf() for _ in range(2)]
        pend = None
        for t in range(17):
            nrows = 128 if t < 16 else 64
            i = t % 2
            ix = t % NXT
            p.dma("sp", xt_[ix][0:nrows, :], x_ext[t * 128:t * 128 + nrows, :], writes=[b_xt[ix]])
            rms_part1(nrows, xt_[ix], b_xt[ix], xs_[i], b_xs[i], sst[i][:, 0:1], sst[i][:, 1:2], b_ss[i])
            if pend is not None:
                rms_part2(*pend)
            pend = (nrows, xs_[i], b_xs[i], PV_G0, xnT, t * 128, b_xnT[t])
        rms_part2(*pend)
        p.barrier()
        ar.top = l0_top

        set_banks(8)
        wAb = [ar.bf(NKT * 512).rearrange("p (k c) -> p k c", c=512) for _ in range(2)]
        HB = []
        for _par in range(2):
            hb = dict(
                QT=ar.bf(1216), KT=ar.bf(1728), sga=ar.bf(NQ),
                Vt=ar.bf(14 * 128).rearrange("p (t d) -> p t d", d=128),
                KcT=ar.bf(1024).rearrange("p (s k) -> p s k", k=512),
                Vc=ar.bf(1024).rearrange("p (s j d) -> p s j d", s=2, j=4),
                biasb=ar.f32(704),
                b_QT=Buf(), b_KT=Buf(), b_sga=Buf(), b_Vt=Buf(), b_KcT=Buf(), b_Vc=Buf(), b_bias=Buf())
            HB.append(hb)
        kst_g = [ar.f32(512).rearrange("p (j d) -> p j d", d=128) for _ in range(2)]
        b_kst_g = [Buf(), Buf()]
        for hb in HB:
            hb["kst"] = kst_g
            hb["b_kst"] = b_kst_g
        st32 = [ar.f32(512) for _ in range(2)]
        b_st32 = [Buf() for _ in range(2)]
        tnh = [ar.f32(512) for _ in range(2)]
        b_tnh = [Buf() for _ in range(2)]
        Ssb = [ar.f32(640) for _ in range(2)]
        PT = [ar.bf(640) for _ in range(2)]
        b_Ssb = [Buf() for _ in range(2)]
        b_PT = [Buf() for _ in range(2)]
        rec = [ar.f32(128) for _ in range(2)]
        otmp = [ar.f32(128) for _ in range(2)]
        b_rec = [Buf() for _ in range(2)]
        kvst = [ar.f32(128) for _ in range(2)]
        b_kvst = [Buf() for _ in range(2)]
        rr_att = {"i": 0, "st": 0, "kv": 0, "tn": 0}

        def load_wA(h):
            for g in range(4):
                p.dma("pool", wAb[h % 2][:, 4 * g:4 * g + 4, :],
                      wA[h, 512 * g:512 * (g + 1), :].rearrange("(k p) c -> p k c", p=128),
                      writes=[b_wA[h % 2]])

        def proj(wt, b_w, ccol, c0, n, ps):
            rd = [b_w] + xn_bufs(c0, n)
            for kt in range(NKT):
                p.op("pe", lambda e: e.matmul(ps[0][:, 0:n], wt[:, kt, ccol:ccol + 128], xnT[:, kt, c0:c0 + n],
                                              start=(kt == 0), stop=(kt == NKT - 1)),
                     reads=rd, writes=[ps[1]], inc=(kt == NKT - 1))

        def attn_S(hb, h, qa, nq, keytiles, mixcol, qlo, last64):
            i = rr_att["i"]
            rr_att["i"] = (i + 1) % 2
            QT, biasb = hb["QT"], hb["biasb"]
            psA = psum_f()
            psB = psum_f()
            rK = [hb["b_QT"], hb["b_KT"], hb["b_KcT"]]
            for j in range(4):
                p.op("pe", lambda e: e.matmul(psA[0][:, j * 128 + qlo:j * 128 + qlo + nq], keytiles[j][0],
                                              QT[:, qa:qa + nq], start=True, stop=True),
                     reads=rK, writes=[psA[1]], inc=(j == 3))
            kr = 64 if last64 else 128
            p.op("pe", lambda e: e.matmul(psB[0][0:kr, qlo:qlo + nq], keytiles[4][0], QT[:, qa:qa + nq],
                                          start=True, stop=True), reads=rK, writes=[psB[1]])
            S, P_ = Ssb[i], PT[i]
            S4 = S[:, 0:512].rearrange("p (j q) -> p j q", q=128)[:, :, qlo:qlo + nq]
            A4 = psA[0][:, 0:512].rearrange("p (j q) -> p j q", q=128)[:, :, qlo:qlo + nq]
            B4 = biasb[:, 0:512].rearrange("p (j q) -> p j q", q=128)[:, :, qlo:qlo + nq]
            P4 = P_[:, 0:512].rearrange("p (j q) -> p j q", q=128)[:, :, qlo:qlo + nq]
            p.op("dve", lambda e: e.tensor_tensor(S4, A4, B4, ALU.add), reads=[psA[1], hb["b_bias"]],
                 writes=[b_Ssb[i]])
            if last64:
                s_idx = keytiles[4][3]
                bsl = biasb[0:64, 640 + 32 * s_idx:640 + 32 * s_idx + 32]
            else:
                bsl = biasb[:, 512 + qlo:512 + qlo + nq]
            p.op("dve", lambda e: e.tensor_tensor(S[0:kr, 512 + qlo:512 + qlo + nq], psB[0][0:kr, qlo:qlo + nq], bsl,
                                                  ALU.add), reads=[psB[1], hb["b_bias"]], writes=[b_Ssb[i]])
            p.op("act", lambda e: e.activation(P4, S4, AF.Exp), reads=[b_Ssb[i]], writes=[b_PT[i]])
            p.op("act", lambda e: e.activation(P_[0:kr, 512 + qlo:512 + qlo + nq], S[0:kr, 512 + qlo:512 + qlo + nq],
                                               AF.Exp), reads=[b_Ssb[i]], writes=[b_PT[i]])
            return (hb, h, i, nq, keytiles, mixcol, qlo, kr)

        def attn_V(state):
            hb, h, i, nq, keytiles, mixcol, qlo, kr = state
            P_ = PT[i]
            psO = psum_f()
            rV = [b_PT[i], hb["b_Vt"], hb["b_Vc"], b_const]
            for j in range(5):
                k_ = kr if j == 4 else 128
                p.op("pe", lambda e: e.matmul(
                    psO[0][:, 0:nq], keytiles[j][1], P_[0:k_, j * 128 + qlo:j * 128 + qlo + nq],
                    start=(j == 0), stop=(j == 4)), reads=rV, writes=[psO[1]], inc=False)
            for j in range(5):
                k_ = kr if j == 4 else 128
                p.op("pe", lambda e: e.matmul(
                    psO[0][:, 128:128 + nq], keytiles[j][2], P_[0:k_, j * 128 + qlo:j * 128 + qlo + nq],
                    start=(j == 0), stop=(j == 4)), reads=rV, writes=[psO[1]], inc=(j == 4))
            r_, o_ = rec[i], otmp[i]
            p.op("dve", lambda e: e.tensor_scalar(r_[:, 0:nq], psO[0][:, 128:128 + nq], 2.0, 1e-30,
                                                  ALU.mult, ALU.max), reads=[psO[1]], writes=[b_rec[i]])
            p.op("dve", lambda e: e.reciprocal(r_[:, 0:nq], r_[:, 0:nq]), reads=[b_rec[i]], writes=[b_rec[i]])
            p.op("dve", lambda e: e.tensor_tensor(o_[:, 0:nq], psO[0][:, 0:nq], r_[:, 0:nq], ALU.mult),
                 reads=[psO[1], b_rec[i]], writes=[b_rec[i]])
            p.op("dve", lambda e: e.tensor_tensor(mixT[:, h, mixcol:mixcol + nq], o_[:, 0:nq],
                                                  hb["sga"][:, mixcol:mixcol + nq], ALU.mult),
                 reads=[b_rec[i], hb["b_sga"]], writes=[b_mix[h]])

        def p_tasks(h):
            hb = HB[h % 2]
            wt, b_w = wAb[h % 2], b_wA[h % 2]
            QT, KT, sga, Vt, KcT, Vc = hb["QT"], hb["KT"], hb["sga"], hb["Vt"], hb["KcT"], hb["Vc"]
            tasks = []

            def t_dmas():
                p.dma("sp", hb["biasb"], biasT_d[h], writes=[hb["b_bias"]])
                for s_ in range(2):
                    p.dma("sp", hb["kst"][s_],
                          kc[s_, :, h * 128:(h + 1) * 128].rearrange("(j p) d -> p j d", p=128),
                          writes=[hb["b_kst"][s_]])
                    p.dma("pool", Vc[:, s_, :, :],
                          vc[s_, :, h * 128:(h + 1) * 128].rearrange("(j p) d -> p j d", p=128),
                          writes=[hb["b_Vc"]])
            tasks.append(t_dmas)

            def t_q(c0, n):
                ps = psum_f()
                proj(wt, b_w, 0, c0, n, ps)
                p.op("act", lambda e: e.activation(QT[:, c0 - 896:c0 - 896 + n], ps[0][:, 0:n], AF.Copy,
                                                   scale=128.0 ** -0.5), reads=[ps[1]], writes=[hb["b_QT"]])
            for (c0, n) in nblocks(896, TALL):
                tasks.append(lambda c0=c0, n=n: t_q(c0, n))

            def t_ga(c0, n):
                ps = psum_f()
                proj(wt, b_w, 384, c0, n, ps)
                ti = rr_att["tn"]
                rr_att["tn"] = (ti + 1) % 2
                p.op("act", lambda e: e.activation(tnh[ti][:, 0:n], ps[0][:, 0:n], AF.Tanh, scale=0.5),
                     reads=[ps[1]], writes=[b_tnh[ti]])
                p.op("dve", lambda e: e.scalar_tensor_tensor(sga[:, c0 - QOFF:c0 - QOFF + n], tnh[ti][:, 0:n], 1.0,
                                                             ps[0][:, 0:n], ALU.add, ALU.mult),
                     reads=[ps[1], b_tnh[ti]], writes=[hb["b_sga"]])
            for (c0, n) in nblocks(QOFF, TALL):
                tasks.append(lambda c0=c0, n=n: t_ga(c0, n))

            def t_kc(s_):
                ps = psum_f()
                for j in range(4):
                    p.op("pe", lambda e: e.transpose(ps[0][:, j * 128:(j + 1) * 128], hb["kst"][s_][:, j, :], identf),
                         reads=[hb["b_kst"][s_], b_const], writes=[ps[1]], inc=(j == 3))
                p.op("act", lambda e: e.activation(KcT[:, s_, :], ps[0][:, 0:512], AF.Copy),
                     reads=[ps[1]], writes=[hb["b_KcT"]])
            tasks.append(lambda: t_kc(0))
            tasks.append(lambda: t_kc(1))

            def kv_proj(which, c0, n):
                ps = psum_f()
                proj(wt, b_w, 128 * which, c0, n, ps)
                need32 = (which == 2) or (c0 + n > 1536)
                si = None
                if which == 1:
                    p.op("act", lambda e: e.activation(KT[:, c0 - 384:c0 - 384 + n], ps[0][:, 0:n], AF.Copy),
                         reads=[ps[1]], writes=[hb["b_KT"]])
                if need32:
                    si = rr_att["st"]
                    rr_att["st"] = (si + 1) % 2
                    eng = "dve" if which == 1 else "act"
                    if eng == "act":
                        p.op("act", lambda e: e.activation(st32[si][:, 0:n], ps[0][:, 0:n], AF.Copy),
                             reads=[ps[1]], writes=[b_st32[si]])
                    else:
                        p.op("dve", lambda e: e.tensor_copy(st32[si][:, 0:n], ps[0][:, 0:n]),
                             reads=[ps[1], hb["b_KT"]], writes=[b_st32[si]])
                return si

            def kv_post(which, c0, n, si):
                if si is None:
                    return
                s32, bs32 = st32[si], b_st32[si]
                if which == 2:
                    pv = psum_f()
                    tiles = nblocks(0, n, 128)
                    for ti, (o, m_) in enumerate(tiles):
                        p.op("pe", lambda e: e.transpose(pv[0][0:m_, ti * 128:(ti + 1) * 128], s32[:, o:o + m_],
                                                         identf),
                             reads=[bs32, b_const], writes=[pv[1]], inc=(ti == len(tiles) - 1))
                    vt0 = (c0 - 384) // 128
                    nfull = sum(1 for (_, m_) in tiles if m_ == 128)
                    if nfull:
                        p.op("dve", lambda e: e.tensor_copy(
                            Vt[:, vt0:vt0 + nfull, :], pv[0][:, 0:128 * nfull].rearrange("p (t d) -> p t d", d=128)),
                            reads=[pv[1]], writes=[hb["b_Vt"]])
                    if nfull < len(tiles):
                        p.op("dve", lambda e: e.tensor_copy(Vt[0:64, vt0 + nfull, :],
                                                            pv[0][0:64, 128 * nfull:128 * nfull + 128]),
                             reads=[pv[1]], writes=[hb["b_Vt"]])
                for (o, m_) in nblocks(0, n, 128):
                    ta = c0 + o
                    if ta < 1536:
                        continue
                    pk = psum_f()
                    p.op("pe", lambda e: e.transpose(pk[0][0:m_, 0:128], s32[:, o:o + m_], identf),
                         reads=[bs32, b_const], writes=[pk[1]])
                    ki = rr_att["kv"]
                    rr_att["kv"] = (ki + 1) % 2
                    p.op("act", lambda e: e.activation(kvst[ki][0:m_, 0:128], pk[0][0:m_, 0:128], AF.Copy),
                         reads=[pk[1]], writes=[b_kvst[ki]])
                    if ta < 2048:
                        dst = kvo_p[which - 1, ta - 1536:ta - 1536 + m_, h * 128:(h + 1) * 128]
                    else:
                        dst = kvo_s[which - 1, 0:m_, h * 128:(h + 1) * 128]
                    p.dma("sp", dst, kvst[ki][0:m_, 0:128], reads=[b_kvst[ki]])

            kvb = [(w_, c0, n) for w_ in (1, 2) for (c0, n) in nblocks(384, TALL)]
            state = {}

            def t_kv(k):
                w_, c0, n = kvb[k]
                state[k] = kv_proj(w_, c0, n)
                if k >= 1:
                    w2, c2, n2 = kvb[k - 1]
                    kv_post(w2, c2, n2, state[k - 1])
                if k == len(kvb) - 1:
                    kv_post(w_, c0, n, state[k])
            for k in range(len(kvb)):
                tasks.append(lambda k=k: t_kv(k))
            return tasks

        def a_tasks(h):
            hb = HB[h % 2]
            KT, Vt, KcT, Vc = hb["KT"], hb["Vt"], hb["KcT"], hb["Vc"]
            blocks = []
            for m in range(7, 16):
                qlo = 64 if m == 7 else 0
                nq = 128 - qlo
                kts = []
                for j in range(5):
                    kt_ = m - 4 + j
                    kcol = (kt_ - 3) * 128
                    kts.append((KT[:, kcol:kcol + 128], Vt[:, kt_ - 3, :], flag_bf if kt_ < 8 else ones_bf))
                blocks.append((hb, h, m * 128 + qlo - 896, nq, kts, m * 128 + qlo - QOFF, qlo, False))
            for s_ in range(2):
                kts = []
                for j in range(4):
                    kts.append((KcT[:, s_, j * 128:(j + 1) * 128], Vc[:, s_, j, :], ones_bf))
                kts.append((KT[:, 2048 - 384:2112 - 384], Vt[0:64, 13, :], ones_bf[0:64, :], s_))
                blocks.append((hb, h, 2048 + 32 * s_ - 896, 32, kts, 2048 + 32 * s_ - QOFF, 0, True))
            st_ = {}
            tasks = []

            def t_a(k):
                if k < len(blocks):
                    st_[k] = attn_S(*blocks[k])
                if k >= 1:
                    attn_V(st_[k - 1])
            for k in range(len(blocks) + 1):
                tasks.append(lambda k=k: t_a(k))
            return tasks

        wBb_pre = [arena_t[:][:, l0_top + 2048 * i_:l0_top + 2048 * (i_ + 1)].bitcast(BF16)
                   .rearrange("p (k c) -> p k c", c=256) for i_ in range(2)]
        b_wB = [Buf() for _ in range(2)]
        for t in p_tasks(0):
            t()
        for h in range(8):
            A = a_tasks(h)
            Pn = p_tasks(h + 1) if h + 1 < 8 else []
            if h + 2 < 8:
                load_wA(h + 2)
            if h == 6:
                for i_ in range(2):
                    for g in range(4):
                        p.dma("pool", wBb_pre[i_][:, 4 * g:4 * g + 4, :],
                              wB[i_, 512 * g:512 * (g + 1), :].rearrange("(k p) c -> p k c", p=128),
                              writes=[b_wB[i_], b_wA[0]])
            na, npn = len(A), len(Pn)
            ia = 0
            for ip in range(npn):
                Pn[ip]()
                want = ((ip + 1) * na) // npn
                while ia < want:
                    A[ia]()
                    ia += 1
            while ia < na:
                A[ia]()
                ia += 1
        p.barrier()
        ar.top = l0_top

        XP0, XSA, XSB = 3, 2054, 2089
        wBb = [ar.bf(NKT * 256).rearrange("p (k c) -> p k c", c=256) for _ in range(2)]
        assert NKT * 256 // 2 == 2048
        waxb = ar.bf(2 * 8 * 128).rearrange("p (g n e) -> p g n e", g=2, n=8)
        b_wax = Buf()
        h0sb = ar.f32(16)
        b_h0 = Buf()
        xb_ = [ar.f32(2128) for _ in range(2)]
        xc = ar.f32(TALL)
        xcb = ar.bf(TALL)
        rg = ar.f32(TALL)
        ig = ar.f32(TALL)
        bb = ar.f32(TALL)
        aa = xc
        hh = rg
        sgb_ = [ar.f32(NQ) for _ in range(2)]
        hin = ar.f32(8)
        b_xc, b_xcb, b_rg, b_ig, b_bb, b_hin = (Buf() for _ in range(6))
        b_xb_ = [Buf() for _ in range(2)]
        b_sgb_ = [Buf() for _ in range(2)]
        b_aa, b_hh = b_xc, b_rg
        p.dma("pool", waxb, wax.rearrange("p (g n e) -> p g n e", g=2, n=8), writes=[b_wax])
        p.dma("sp", h0sb, h0T, writes=[b_h0])

        def load_wB(n_):
            for g in range(4):
                p.dma("pool", wBb[n_ % 2][:, 4 * g:4 * g + 4, :],
                      wB[n_, 512 * g:512 * (g + 1), :].rearrange("(k p) c -> p k c", p=128),
                      writes=[b_wB[n_ % 2]])

        def xbcol(c):
            if c < 2048:
                return XP0 + c
            if c < 2080:
                return XSA + (c - 2048)
            return XSB + (c - 2080)

        def lru_P(n_):
            wt, b_w = wBb[n_ % 2], b_wB[n_ % 2]
            xb, b_xb = xb_[n_ % 2], b_xb_[n_ % 2]
            sgb, b_sgb = sgb_[n_ % 2], b_sgb_[n_ % 2]
            p.op("pool", lambda e: e.memset(xb[:, 0:3], 0.0), writes=[b_xb])
            lbv = lb0T[:, n_ * 6:n_ * 6 + 6]
            p.dma("sp", xb[:, XSA - 3:XSA], lbv[:, 0:3], writes=[b_xb])
            p.dma("sp", xb[:, XSB - 3:XSB], lbv[:, 3:6], writes=[b_xb])
            for (c0, n) in nblocks(0, TALL):
                ps = psum_f()
                proj(wt, b_w, 0, c0, n, ps)
                if c0 < 2048:
                    p.op("act", lambda e: e.activation(xb[:, XP0 + c0:XP0 + c0 + n], ps[0][:, 0:n], AF.Copy),
                         reads=[ps[1]], writes=[b_xb])
                else:
                    p.op("act", lambda e: e.activation(xb[:, XSA:XSA + 32], ps[0][:, 0:32], AF.Copy),
                         reads=[ps[1]], writes=[b_xb])
                    p.op("act", lambda e: e.activation(xb[:, XSB:XSB + 32], ps[0][:, 32:64], AF.Copy),
                         reads=[ps[1]], writes=[b_xb])
            for (c0, n) in nblocks(QOFF, TALL):
                ps = psum_f()
                proj(wt, b_w, 128, c0, n, ps)
                p.op("act", lambda e: e.activation(sgb[:, c0 - QOFF:c0 - QOFF + n], ps[0][:, 0:n], AF.Silu),
                     reads=[ps[1]], writes=[b_sgb])

        b_xc2 = [Buf(), Buf()]
        b_xcb2 = [Buf(), Buf()]
        b_rg2 = [Buf(), Buf()]
        b_ig2 = [Buf(), Buf()]
        b_bb2 = [Buf(), Buf()]
        HALF = ((0, 1024), (1024, TALL))

        def lru_C(n_):
            xb, b_xb = xb_[n_ % 2], b_xb_[n_ % 2]
            sgb, b_sgb = sgb_[n_ % 2], b_sgb_[n_ % 2]
            cw = [pvec[:, PV_CW + n_ * 4 + j:PV_CW + n_ * 4 + j + 1] for j in range(4)]
            cbv = pvec[:, PV_CB + n_:PV_CB + n_ + 1]
            ba = pvec[:, PV_BA + n_:PV_BA + n_ + 1]
            bx = pvec[:, PV_BX + n_:PV_BX + n_ + 1]
            c8n = c8[:, n_:n_ + 1]
            c16n = c16[:, n_:n_ + 1]
            runs = (((XP0 - 3, 0, 1024),), ((XP0 - 3 + 1024, 1024, 1024), (XSA - 3, 2048, 32), (XSB - 3, 2080, 32)))
            for k in range(2):
                for (xo, co, ln) in runs[k]:
                    p.op("dve", lambda e: e.tensor_scalar(xc[:, co:co + ln], xb[:, xo:xo + ln], cw[0], cbv,
                                                          ALU.mult, ALU.add), reads=[b_xb, b_pvec], writes=[b_xc2[k]])
                    for j in range(1, 4):
                        p.op("dve", lambda e: e.scalar_tensor_tensor(
                            xc[:, co:co + ln], xb[:, xo + j:xo + j + ln], cw[j], xc[:, co:co + ln],
                            ALU.mult, ALU.add), reads=[b_xb, b_pvec, b_xc2[k]], writes=[b_xc2[k]])
            for k in range(2):
                a, b = HALF[k]
                p.op("act", lambda e: e.activation(xcb[:, a:b], xc[:, a:b], AF.Copy),
                     reads=[b_xc2[k]], writes=[b_xcb2[k]])
            p.op("pool", lambda e: e.tensor_copy(sto[:, ST_LP + 3 * n_:ST_LP + 3 * n_ + 3],
                                                 xb[:, XP0 + 2045:XP0 + 2048]), reads=[b_xb], writes=[b_sto])
            for s_, xs0 in ((0, XSA), (1, XSB)):
                p.op("pool", lambda e: e.tensor_copy(
                    sto[:, ST_LS + 6 * n_ + 3 * s_:ST_LS + 6 * n_ + 3 * s_ + 3], xb[:, xs0 + 29:xs0 + 32]),
                    reads=[b_xb], writes=[b_sto])
            for k in range(2):
                a, b = HALF[k]
                for g_, dst, b_dst, bias_ in ((0, rg, b_rg2[k], ba), (1, ig, b_ig2[k], bx)):
                    for (c0, n) in nblocks(a, b):
                        ps = psum_f()
                        p.op("pe", lambda e: e.matmul(ps[0][:, 0:n], waxb[:, g_, n_, :], xcb[:, c0:c0 + n],
                                                      start=True, stop=True), reads=[b_wax, b_xcb2[k]], writes=[ps[1]])
                        p.op("act", lambda e: e.activation(dst[:, c0:c0 + n], ps[0][:, 0:n], AF.Sigmoid, bias=bias_),
                             reads=[ps[1], b_pvec], writes=[b_dst])
            for k in range(2):
                a, b = HALF[k]
                p.op("dve", lambda e: e.tensor_tensor(ig[:, a:b], ig[:, a:b], xc[:, a:b], ALU.mult),
                     reads=[b_ig2[k], b_xc2[k]], writes=[b_ig2[k]])
            for k in range(2):
                a, b = HALF[k]
                p.op("act", lambda e: e.activation(bb[:, a:b], rg[:, a:b], AF.Exp, scale=c16n),
                     reads=[b_rg2[k], b_const], writes=[b_bb2[k]])
                p.op("act", lambda e: e.activation(aa[:, a:b], rg[:, a:b], AF.Exp, scale=c8n),
                     reads=[b_rg2[k], b_const, b_ig2[k]], writes=[b_xc2[k]])
            for k in range(2):
                a, b = HALF[k]
                p.op("act", lambda e: e.activation(bb[:, a:b], bb[:, a:b], AF.Sqrt, scale=-1.0, bias=1.0),
                     reads=[b_bb2[k]], writes=[b_bb2[k]])
            for k in range(2):
                a, b = HALF[k]
                p.op("dve", lambda e: e.tensor_tensor(bb[:, a:b], bb[:, a:b], ig[:, a:b], ALU.mult),
                     reads=[b_bb2[k], b_ig2[k]], writes=[b_bb2[k]])
            p.op("dve", lambda e: e.tensor_tensor_scan(hh[:, 0:1024], aa[:, 0:1024], bb[:, 0:1024], 0.0,
                                                       ALU.mult, ALU.add),
                 reads=[b_xc2[0], b_bb2[0]], writes=[b_rg2[0]])
            p.op("dve", lambda e: e.tensor_tensor(mixT[:, 8 + n_, 0:64], hh[:, QOFF:1024], sgb[:, 0:64], ALU.mult),
                 reads=[b_rg2[0], b_sgb], writes=[b_mix[8 + n_]])
            p.op("dve", lambda e: e.tensor_tensor(hin[:, 0:1], hh[:, 1023:1024], flag, ALU.mult),
                 reads=[b_rg2[0], b_pvec], writes=[b_hin])
            p.op("dve", lambda e: e.tensor_tensor_scan(hh[:, 1024:2048], aa[:, 1024:2048], bb[:, 1024:2048],
                                                       hin[:, 0:1], ALU.mult, ALU.add),
                 reads=[b_xc2[1], b_bb2[1], b_hin], writes=[b_rg2[1]])
            for s_ in range(2):
                c0 = 2048 + 32 * s_
                p.op("dve", lambda e: e.tensor_tensor_scan(
                    hh[:, c0:c0 + 32], aa[:, c0:c0 + 32], bb[:, c0:c0 + 32],
                    h0sb[:, 2 * n_ + s_:2 * n_ + s_ + 1], ALU.mult, ALU.add),
                    reads=[b_xc2[1], b_bb2[1], b_h0], writes=[b_rg2[1]])
            p.op("dve", lambda e: e.tensor_tensor(mixT[:, 8 + n_, 64:NQ], hh[:, 1024:TALL], sgb[:, 64:NQ], ALU.mult),
                 reads=[b_rg2[1], b_sgb], writes=[b_mix[8 + n_]])
            p.op("pool", lambda e: e.tensor_copy(sto[:, ST_HP + n_:ST_HP + n_ + 1], hh[:, 2047:2048]),
                 reads=[b_rg2[1]], writes=[b_sto])
            for s_ in range(2):
                p.op("pool", lambda e: e.tensor_copy(sto[:, ST_HS + 2 * n_ + s_:ST_HS + 2 * n_ + s_ + 1],
                                                     hh[:, 2079 + 32 * s_:2080 + 32 * s_]),
                     reads=[b_rg2[1]], writes=[b_sto])

        wo_l0 = arena_t[:][:, base_top:base_top + (NKT * D) // 2].bitcast(BF16).rearrange("p (k c) -> p k c", c=D)
        b_wo_l0 = Buf()
        lru_P(0)
        for n_ in range(8):
            if n_ + 1 < 8:
                lru_P(n_ + 1)
            if n_ + 2 < 8:
                load_wB(n_ + 2)
            if n_ == 6:
                for g in range(8):
                    p.dma("pool", wo_l0[:, 2 * g:2 * g + 2, :],
                          wo0[256 * g:256 * (g + 1), :].rearrange("(k p) c -> p k c", p=128),
                          writes=[b_wo_l0] + b_xnT)
            lru_C(n_)
        p.dma("sp", sto_d, sto, reads=[b_sto])
        p.barrier()
        ar.top = base_top

        set_banks(8)
        WC0_OFF = ARENA_WORDS - 3072 - 8
        wC0_top = arena_t[:][:, WC0_OFF:WC0_OFF + 3072].bitcast(BF16).rearrange("p (k c) -> p k c", c=384)
        b_wC = [Buf() for _ in range(2)]
        for g in range(4):
            p.dma("pool", wC0_top[:, 4 * g:4 * g + 4, :],
                  wC[0, 512 * g:512 * (g + 1), :].rearrange("(k p) c -> p k c", p=128), writes=[b_wC[0]])
        ar.top = base_top
        wo = ar.bf(NKT * D).rearrange("p (k c) -> p k c", c=D)
        assert ar.top <= base_top + (NKT * TALL) // 2
        ar.top = l0_top
        xn1T = ar.bf(NKT * NQ).rearrange("p (k c) -> p k c", c=NQ)
        xn1_end = ar.top
        b_xn1 = [Buf() for _ in range(9)]
        xres_ = [ar.f32(D) for _ in range(2)]
        x1t_ = [ar.f32(D) for _ in range(2)]
        xs1_ = [ar.bf(D) for _ in range(2)]
        ss1_ = [ar.f32(8) for _ in range(2)]
        assert ar.top <= WC0_OFF, ("L0-D overlaps wC slot", ar.top, WC0_OFF)
        b_wo = b_wo_l0
        wo = wo_l0
        b_xres_ = [Buf() for _ in range(2)]
        b_x1t_ = [Buf() for _ in range(2)]
        b_xs1_ = [Buf() for _ in range(2)]
        b_ss1_ = [Buf() for _ in range(2)]
        b_scr = [Buf() for _ in range(9)]
        pend = None
        for j in range(9):
            i = j % 2
            xres, x1t = xres_[i], x1t_[i]
            p.dma("sp", xres, x_ext[QOFF + 128 * j:QOFF + 128 * (j + 1), :], writes=[b_xres_[i]])
            for cb_ in range(4):
                ps = psum_f()
                for kt in range(NKT):
                    p.op("pe", lambda e: e.matmul(
                        ps[0][:, 0:512], mixT[:, kt, 128 * j:128 * (j + 1)], wo[:, kt, 512 * cb_:512 * (cb_ + 1)],
                        start=(kt == 0), stop=(kt == NKT - 1)),
                        reads=[b_mix[kt], b_wo], writes=[ps[1]], inc=(kt == NKT - 1))
                p.op("dve", lambda e: e.tensor_tensor(
                    x1t[:, 512 * cb_:512 * (cb_ + 1)], ps[0][:, 0:512], xres[:, 512 * cb_:512 * (cb_ + 1)],
                    ALU.add), reads=[ps[1], b_xres_[i]], writes=[b_x1t_[i]])
            p.dma("sp", x1_scr[128 * j:128 * (j + 1), :], x1t, reads=[b_x1t_[i]], writes=[b_scr[j]])
            rms_part1(128, x1t, b_x1t_[i], xs1_[i], b_xs1_[i], ss1_[i][:, 0:1], ss1_[i][:, 1:2], b_ss1_[i])
            if pend is not None:
                rms_part2(*pend)
            pend = (128, xs1_[i], b_xs1_[i], PV_G1, xn1T, 128 * j, b_xn1[j])
        rms_part2(*pend)
        p.barrier()

        set_banks(8)
        ar.top = base_top
        wCb = [ar.bf(NKT * 384).rearrange("p (k c) -> p k c", c=384) for _ in range(2)]
        wc_end = ar.top
        wCb[0] = wC0_top
        zcT = ar.bf(16 * 1088).rearrange("p (f c) -> p f c", c=1088)
        b_zc = [Buf() for _ in range(16)]
        sgT = ar.bf(16 * 1088).rearrange("p (f c) -> p f c", c=1088)
        b_sg = [Buf() for _ in range(16)]
        sg_end = ar.top
        assert ar.top <= l0_top, (ar.top, l0_top)
        ar.top = xn1_end
        zb = [ar.f32(1216) for _ in range(2)]
        b_zb = [Buf() for _ in range(2)]
        sig = [ar.f32(512) for _ in range(3)]
        b_sig = [Buf() for _ in range(3)]
        acc = [ar.f32(1152) for _ in range(2)]
        b_acc = [Buf() for _ in range(2)]
        cbo = ar.f32(16 * 90)
        b_cbo = Buf()
        KPE = 16
        zbf = [ar.bf(1216) for _ in range(2)]
        b_zbf = [Buf() for _ in range(2)]
        dg = [ar.bf(KPE * 128).rearrange("p (t c) -> p t c", c=128) for _ in range(2)]
        b_dg = [Buf() for _ in range(2)]
        rr1 = {"sig": 0}

        def load_wC(f):
            for g in range(4):
                p.dma("pool", wCb[f % 2][:, 4 * g:4 * g + 4, :],
                      wC[f, 512 * g:512 * (g + 1), :].rearrange("(k p) c -> p k c", p=128),
                      writes=[b_wC[f % 2]])

        def proj1(wt, b_w, ccol, c0, n, ps):
            rd = [b_w] + [b_xn1[t] for t in range(c0 // 128, (c0 + n - 1) // 128 + 1)]
            for kt in range(NKT):
                p.op("pe", (lambda kt=kt: lambda e: e.matmul(ps[0][:, 0:n], wt[:, kt, ccol:ccol + 128],
                                                              xn1T[:, kt, c0:c0 + n],
                                                              start=(kt == 0), stop=(kt == NKT - 1)))(),
                     reads=rd, writes=[ps[1]], inc=(kt == NKT - 1))

        def zcol(c):
            if c < 1088:
                return c
            if c < 1120:
                return c + 30
            return c + 60

        def l1a_proj(f, fillers):
            wt, b_w = wCb[f % 2], b_wC[f % 2]
            z, b_z = zb[f % 2], b_zb[f % 2]
            csv = cs0T[:, f * 60:f * 60 + 60]
            p.dma("sp", z[:, 1088:1118], csv[:, 0:30], writes=[b_z])
            p.dma("sp", z[:, 1150:1180], csv[:, 30:60], writes=[b_z])
            for bi, (c0, n) in enumerate(nblocks(0, NQ)):
                psv = psum_f()
                proj1(wt, b_w, 0, c0, n, psv)
                psg = psum_f()
                proj1(wt, b_w, 128, c0, n, psg)
                si = rr1["sig"]
                rr1["sig"] = (si + 1) % 3
                sg_, b_s = sig[si], b_sig[si]
                p.op("act", lambda e: e.activation(sg_[:, 0:n], psg[0][:, 0:n], AF.Sigmoid),
                     reads=[psg[1]], writes=[b_s])
                if c0 + n <= 1088:
                    p.op("dve", lambda e: e.tensor_tensor(z[:, c0:c0 + n], psv[0][:, 0:n], sg_[:, 0:n], ALU.mult),
                         reads=[psv[1], b_s], writes=[b_z])
                else:
                    for (o, ln, zc0) in ((0, 64, 1024), (64, 32, 1118), (96, 32, 1180)):
                        p.op("dve", lambda e: e.tensor_tensor(z[:, zc0:zc0 + ln], psv[0][:, o:o + ln],
                                                              sg_[:, o:o + ln], ALU.mult),
                             reads=[psv[1], b_s], writes=[b_z])
                pst = psum_f()
                proj1(wt, b_w, 256, c0, n, pst)
                lo = max(c0, 64)
                si2 = rr1["sig"]
                rr1["sig"] = (si2 + 1) % 3
                sg2, b_s2 = sig[si2], b_sig[si2]
                p.op("act", lambda e: e.activation(sg2[:, 0:n], pst[0][:, 0:n], AF.Sigmoid),
                     reads=[pst[1]], writes=[b_s2])
                p.op("dve", lambda e: e.tensor_tensor(sgT[:, f, lo - 64:c0 + n - 64], pst[0][:, lo - c0:n],
                                                      sg2[:, lo - c0:n], ALU.mult),
                     reads=[pst[1], b_s2], writes=[b_sg[f]])
                for fn in fillers[bi]:
                    fn()
            for (k_, zc0) in ((0, 1088 - 30), (1, 1150 - 30), (2, 1212 - 30)):
                p.op("pool", lambda e: e.tensor_copy(cbo[:, f * 90 + 30 * k_:f * 90 + 30 * k_ + 30],
                                                     z[:, zc0:zc0 + 30]), reads=[b_z], writes=[b_cbo])
            zb_, b_zb_ = zbf[f % 2], b_zbf[f % 2]
            dg_, b_dg_ = dg[f % 2], b_dg[f % 2]
            p.op("pool", lambda e: e.tensor_tensor(
                dg_, ident.unsqueeze(1).to_broadcast([128, KPE, 128]),
                pvec[:, PV_DWW + f * 31:PV_DWW + f * 31 + KPE].unsqueeze(2).to_broadcast([128, KPE, 128]),
                ALU.mult), reads=[b_const, b_pvec], writes=[b_dg_])
            p.op("act", lambda e: e.activation(zb_[:, 0:1212], z[:, 0:1212], AF.Copy), reads=[b_z], writes=[b_zb_])

        def l1a_taps(f):
            z, b_z = zb[f % 2], b_zb[f % 2]
            ac, b_ac = acc[f % 2], b_acc[f % 2]
            dww = [pvec[:, PV_DWW + f * 31 + j:PV_DWW + f * 31 + j + 1] for j in range(31)]
            dwb = pvec[:, PV_DWB + f:PV_DWB + f + 1]
            fns = []
            fns.append(lambda: p.op("dve", lambda e: e.tensor_scalar(
                ac[:, 0:1148], z[:, 34 + KPE:34 + KPE + 1148], dww[KPE], dwb, ALU.mult, ALU.add),
                reads=[b_z, b_pvec], writes=[b_ac]))
            for j in range(KPE + 1, 31):
                fns.append((lambda j: lambda: p.op("dve", lambda e: e.scalar_tensor_tensor(
                    ac[:, 0:1148], z[:, 34 + j:34 + j + 1148], dww[j], ac[:, 0:1148], ALU.mult, ALU.add),
                    reads=[b_z, b_pvec, b_ac], writes=[b_ac]))(j))
            k = (len(fns) + 2) // 3
            return [fns[0:k], fns[k:2 * k], fns[2 * k:]]

        def l1a_conv(f):
            ac, b_ac = acc[f % 2], b_acc[f % 2]
            zb_, b_zb_ = zbf[f % 2], b_zbf[f % 2]
            dg_, b_dg_ = dg[f % 2], b_dg[f % 2]
            for (c0, n) in nblocks(0, 1148):
                ps = psum_f()
                for t in range(KPE):
                    p.op("pe", lambda e: e.matmul(ps[0][:, 0:n], dg_[:, t, :], zb_[:, 34 + t + c0:34 + t + c0 + n],
                                                  start=(t == 0), stop=(t == KPE - 1)),
                         reads=[b_dg_, b_zb_], writes=[ps[1]], inc=(t == KPE - 1))
                p.op("dve", lambda e: e.tensor_tensor(ac[:, c0:c0 + n], ps[0][:, 0:n], ac[:, c0:c0 + n], ALU.add),
                     reads=[ps[1], b_ac], writes=[b_ac])
            p.op("act", lambda e: e.activation(zcT[:, f, 0:1024], ac[:, 0:1024], AF.Copy),
                 reads=[b_ac], writes=[b_zc[f]])
            p.op("act", lambda e: e.activation(zcT[:, f, 1024:1056], ac[:, 1054:1086], AF.Copy),
                 reads=[b_ac], writes=[b_zc[f]])
            p.op("act", lambda e: e.activation(zcT[:, f, 1056:1088], ac[:, 1116:1148], AF.Copy),
                 reads=[b_ac], writes=[b_zc[f]])

        wo_l1 = arena_t[:][:, l0_top:l0_top + (NKT * D) // 2].bitcast(BF16).rearrange("p (k c) -> p k c", c=D)
        b_woc = [Buf() for _ in range(8)]
        assert 4 * 2 * D // 2 <= (NKT * NQ) // 2
        assert ar.top <= WC0_OFF, ("L1-A overlaps wC slot", ar.top, WC0_OFF)
        load_wC(1)
        for f in range(16):
            l1a_proj(f, l1a_taps(f - 1) if f >= 1 else [[], [], []])
            if f + 2 < 16:
                load_wC(f + 2)
            if f == 15:
                for g in range(4):
                    p.dma("pool", wo_l1[:, 2 * g:2 * g + 2, :],
                          wo1[256 * g:256 * (g + 1), :].rearrange("(k p) c -> p k c", p=128),
                          writes=[b_woc[g]] + b_xn1)
            if f >= 1:
                l1a_conv(f - 1)
        for grp in l1a_taps(15):
            for fn in grp:
                fn()
        l1a_conv(15)
        p.dma("sp", cbo_d, cbo, reads=[b_cbo])
        p.barrier()

        ar.top = base_top
        mean = ar.f32(1088)
        rstd = ar.f32(1088)
        sq = [ar.bf(512) for _ in range(2)]
        tt = [ar.f32(512) for _ in range(3)]
        assert ar.top <= wc_end, (ar.top, wc_end)
        ar.top = sg_end
        xres_ = [ar.f32(D)]
        assert ar.top <= l0_top, (ar.top, l0_top)
        ar.top = l0_top
        wo = ar.bf(NKT * D).rearrange("p (k c) -> p k c", c=D)
        wo = wo_l1
        xres_.append(ar.f32(D))
        yt_ = [ar.f32(D) for _ in range(2)]
        ss2_ = [ar.f32(8) for _ in range(2)]
        b_xres_ = [Buf() for _ in range(2)]
        b_yt_ = [Buf() for _ in range(2)]
        b_ss2_ = [Buf() for _ in range(2)]
        b_sq = [Buf() for _ in range(2)]
        b_tt = [Buf() for _ in range(3)]
        NB1 = [(0, 256), (256, 512), (768, 320)]
        b_st = [Buf() for _ in NB1]
        b_zc2 = [[Buf() for _ in NB1] for _ in range(16)]
        for g in range(4, 8):
            p.dma("pool", wo[:, 2 * g:2 * g + 2, :],
                  wo1[256 * g:256 * (g + 1), :].rearrange("(k p) c -> p k c", p=128), writes=[b_woc[g]])
        rr2 = {"tt": 0}

        def l1_stats(nb):
            c0, n = NB1[nb]
            ps1 = psum_f()
            ps2 = psum_f()
            for f in range(16):
                i = f % 2
                p.op("act", lambda e: e.activation(sq[i][:, 0:n], zcT[:, f, c0:c0 + n], AF.Square),
                     reads=[b_zc2[f][nb]], writes=[b_sq[i]])
                p.op("pe", lambda e: e.matmul(ps1[0][:, 0:n], onesdiv, zcT[:, f, c0:c0 + n],
                                              start=(f == 0), stop=(f == 15)),
                     reads=[b_zc2[f][nb], b_const], writes=[ps1[1]], inc=(f == 15))
                p.op("pe", lambda e: e.matmul(ps2[0][:, 0:n], onesdiv, sq[i][:, 0:n],
                                              start=(f == 0), stop=(f == 15)),
                     reads=[b_sq[i], b_const], writes=[ps2[1]], inc=True)
            bs = b_st[nb]
            p.op("act", lambda e: e.activation(mean[:, c0:c0 + n], ps1[0][:, 0:n], AF.Copy),
                 reads=[ps1[1]], writes=[bs])
            p.op("act", lambda e: e.activation(rstd[:, c0:c0 + n], ps1[0][:, 0:n], AF.Square),
                 reads=[ps1[1]], writes=[bs])
            p.op("dve", lambda e: e.tensor_tensor(rstd[:, c0:c0 + n], ps2[0][:, 0:n], rstd[:, c0:c0 + n],
                                                  ALU.subtract), reads=[ps2[1], bs], writes=[bs])
            p.op("dve", lambda e: e.tensor_scalar(rstd[:, c0:c0 + n], rstd[:, c0:c0 + n], 0.0, None, ALU.max),
                 reads=[bs], writes=[bs])
            p.op("act", lambda e: e.activation(rstd[:, c0:c0 + n], rstd[:, c0:c0 + n], AF.Sqrt, bias=EPS),
                 reads=[bs], writes=[bs])
            p.op("dve", lambda e: e.reciprocal(rstd[:, c0:c0 + n], rstd[:, c0:c0 + n]), reads=[bs], writes=[bs])

        def l1_ln_tasks(nb):
            c0, n = NB1[nb]
            bs = b_st[nb]
            slot = {}

            def front(f):
                i = rr2["tt"]
                rr2["tt"] = (i + 1) % 3
                slot[f] = i
                t_ = tt[i][:, 0:n]
                lng = pvec[:, PV_LNG + f:PV_LNG + f + 1]
                lnb = pvec[:, PV_LNB + f:PV_LNB + f + 1]
                zsl = zcT[:, f, c0:c0 + n]
                p.op("dve", lambda e: e.tensor_tensor(t_, zsl, mean[:, c0:c0 + n], ALU.subtract),
                     reads=[b_zc2[f][nb], bs], writes=[b_tt[i]])
                p.op("dve", lambda e: e.tensor_tensor(t_, t_, rstd[:, c0:c0 + n], ALU.mult),
                     reads=[b_tt[i], bs], writes=[b_tt[i]])
                p.op("act", lambda e: e.activation(t_, t_, AF.Silu, scale=lng, bias=lnb),
                     reads=[b_tt[i], b_pvec], writes=[b_tt[i]])

            def back(f):
                i = slot[f]
                t_ = tt[i][:, 0:n]
                zsl = zcT[:, f, c0:c0 + n]
                p.op("dve", lambda e: e.tensor_tensor(zsl, t_, sgT[:, f, c0:c0 + n], ALU.mult),
                     reads=[b_tt[i], b_sg[f]], writes=[b_zc2[f][nb]])

            def one(f):
                if f < 16:
                    front(f)
                if f >= 1:
                    back(f - 1)
            return [(lambda f=f: one(f)) for f in range(17)]

        def l1_wout(nb, fillers=()):
            fillers = list(fillers)
            c0, n = NB1[nb]
            nslots = 4 * len(nblocks(c0, c0 + n, 128))
            per_slot = (len(fillers) + nslots - 1) // nslots
            for (r0, nrows) in nblocks(c0, c0 + n, 128):
                j = r0 // 128
                i = j % 2
                xres, yt, ss2 = xres_[i], yt_[i], ss2_[i]
                p.dma("sp", xres[0:nrows, :], x1_scr[64 + r0:64 + r0 + nrows, :], reads=b_scr, writes=[b_xres_[i]])
                for cb_ in range(4):
                    ps = psum_f()
                    for kt in range(NKT):
                        p.op("pe", lambda e: e.matmul(
                            ps[0][0:nrows, 0:512], zcT[:, kt, r0:r0 + nrows],
                            wo[:, kt, 512 * cb_:512 * (cb_ + 1)], start=(kt == 0), stop=(kt == NKT - 1)),
                            reads=[b_zc2[kt][nb], b_woc[kt // 2]], writes=[ps[1]], inc=(kt == NKT - 1))
                    p.op("dve", lambda e: e.tensor_tensor(
                        xres[0:nrows, 512 * cb_:512 * (cb_ + 1)], ps[0][0:nrows, 0:512],
                        xres[0:nrows, 512 * cb_:512 * (cb_ + 1)], ALU.add),
                        reads=[ps[1], b_xres_[i]], writes=[b_xres_[i]])
                    for _ in range(per_slot):
                        if fillers:
                            fillers.pop(0)()
                p.op("act", lambda e: e.activation(yt[0:nrows, :], xres[0:nrows, :], AF.Square,
                                                   accum_out=ss2[0:nrows, 0:1]),
                     reads=[b_xres_[i]], writes=[b_yt_[i], b_ss2_[i]])
                p.op("act", lambda e: e.activation(ss2[0:nrows, 1:2], ss2[0:nrows, 0:1], AF.Sqrt,
                                                   scale=1.0 / D, bias=EPS), reads=[b_ss2_[i]], writes=[b_ss2_[i]])
                p.op("dve", lambda e: e.reciprocal(ss2[0:nrows, 1:2], ss2[0:nrows, 1:2]),
                     reads=[b_ss2_[i]], writes=[b_ss2_[i]])
                p.op("dve", lambda e: e.scalar_tensor_tensor(
                    yt[0:nrows, :], xres[0:nrows, :], ss2[0:nrows, 1:2], fnb[0:nrows, :], ALU.mult, ALU.mult),
                    reads=[b_xres_[i], b_ss2_[i], b_fnb], writes=[b_yt_[i]])
                dst = y_main[r0:r0 + 128, :] if r0 < 1024 else y_samp[0:64, :]
                p.dma("sp", dst, yt[0:nrows, :], reads=[b_yt_[i]])
            for fn in fillers:
                fn()

        l1_stats(0)
        for fn in l1_ln_tasks(0):
            fn()
        l1_stats(1)
        l1_wout(0, l1_ln_tasks(1))
        l1_stats(2)
        l1_wout(1, l1_ln_tasks(2))
        l1_wout(2)
        p.finish()
        p.emit(st)
    return nc


def _bias_tables(rel):
    r = np.arange(128)
    out = np.full((8, 128, 704), NEG, np.float32)
    q = np.arange(128)
    for j in range(5):
        kpos = (j - 4) * 128 + r
        kc_ = 2 * (j - 4) + (r >= 64)
        qc = (q >= 64).astype(int)
        vis = (kc_[:, None] >= qc[None, :] - 8) & (kc_[:, None] <= qc[None, :])
        idx = np.clip(q[None, :] - kpos[:, None], -128, 128) + 128
        vals = rel[:, idx]
        out[:, :, j * 128:(j + 1) * 128] = np.where(vis[None], vals, NEG)
    idx = np.clip(q[None, :32] - r[:32, None], -128, 128) + 128
    vals = rel[:, idx]
    out[:, 0:32, 640:672] = vals
    out[:, 32:64, 672:704] = vals
    return out


_NC_CACHE = {}
LAST_DBG = None


def kernel(x_prompt, x_sample, cache_attn_k, cache_attn_v, state_lru_h, state_lru_conv, state_conv,
           norm_ab, w_in_ab, w_out_ab, rel_bias, lru_conv_w, lru_conv_b, lru_w_a, lru_b_a, lru_w_x,
           lru_b_x, lru_lambda, norm_cv, w_in_cv, w_out_cv, dw_w, dw_b, ln_g, ln_b, final_norm):
    f = lambda a: np.ascontiguousarray(np.asarray(a, dtype=np.float32))
    x_prompt, x_sample = f(x_prompt), f(x_sample)
    w_in = f(w_in_ab)[0]
    wA = np.stack([np.concatenate([w_in[:, c * 1024 + h * 128:c * 1024 + (h + 1) * 128] for c in range(4)], axis=1)
                   for h in range(8)])
    wB = np.stack([np.concatenate([w_in[:, 4096 + n * 128:4096 + (n + 1) * 128],
                                   w_in[:, 5120 + n * 128:5120 + (n + 1) * 128]], axis=1) for n in range(8)])
    w_cv = f(w_in_cv)[0]
    wC = np.stack([np.concatenate([w_cv[:, c * 2048 + k * 128:c * 2048 + (k + 1) * 128] for c in range(3)], axis=1)
                   for k in range(16)])
    wo0 = f(w_out_ab)[0]
    wo1 = f(w_out_cv)[0]
    wax = np.stack([f(lru_w_a)[0], f(lru_w_x)[0]])
    wax = np.ascontiguousarray(wax.transpose(2, 0, 1, 3)).reshape(128, 2 * 8 * 128)
    col = lambda v, k: np.ascontiguousarray(f(v).reshape(k, 128).T)
    pv = np.zeros((8, 128, PV_N), np.float32)
    base = np.zeros((128, PV_N), np.float32)
    base[:, PV_G0:PV_G0 + 16] = col(norm_ab[0], 16)
    base[:, PV_G1:PV_G1 + 16] = col(norm_cv[0], 16)
    cw = f(lru_conv_w)[0]
    base[:, PV_CW:PV_CW + 32] = cw.reshape(4, 8, 128).transpose(2, 1, 0).reshape(128, 32)
    base[:, PV_CB:PV_CB + 8] = col(lru_conv_b[0], 8)
    base[:, PV_BA:PV_BA + 8] = col(lru_b_a[0], 8)
    base[:, PV_BX:PV_BX + 8] = col(lru_b_x[0], 8)
    base[:, PV_LAM:PV_LAM + 8] = col(lru_lambda[0], 8)
    dww = f(dw_w)[0]
    base[:, PV_DWW:PV_DWW + 496] = dww.reshape(31, 16, 128).transpose(2, 1, 0).reshape(128, 496)
    base[:, PV_DWB:PV_DWB + 16] = col(dw_b[0], 16)
    base[:, PV_LNG:PV_LNG + 16] = col(ln_g[0], 16)
    base[:, PV_LNB:PV_LNB + 16] = col(ln_b[0], 16)
    fnb = np.ascontiguousarray(np.broadcast_to(f(final_norm)[None, :], (128, D)))
    biasT = _bias_tables(f(rel_bias)[0])
    ck, cv = f(cache_attn_k)[0], f(cache_attn_v)[0]
    slh, slc, scv = f(state_lru_h)[0], f(state_lru_conv)[0], f(state_conv)[0]

    in_maps = []
    for c in range(8):
        b, half = c // 2, c % 2
        xe = np.zeros((TALL, D), np.float32)
        if half == 1:
            xe[0:1024] = x_prompt[b, 0:1024]
        xe[1024:2048] = x_prompt[b, half * 1024:(half + 1) * 1024]
        xe[2048:2112] = x_sample[2 * c:2 * c + 2].reshape(64, D)
        pvc = base.copy()
        pvc[:, PV_FLAG] = float(half)
        h0 = slh[2 * c:2 * c + 2]
        h0T = np.ascontiguousarray(h0.reshape(2, 8, 128).transpose(2, 1, 0)).reshape(128, 16)
        lb = slc[2 * c:2 * c + 2]
        lb0T = np.ascontiguousarray(lb.reshape(2, 3, 8, 128).transpose(3, 2, 0, 1)).reshape(128, 48)
        cs = scv[2 * c:2 * c + 2]
        cs0T = np.ascontiguousarray(cs.reshape(2, 30, 16, 128).transpose(3, 2, 0, 1)).reshape(128, 960)
        in_maps.append({
            "x_ext": xe,
            "kc": np.ascontiguousarray(ck[2 * c:2 * c + 2].reshape(2, 512, 1024)),
            "vc": np.ascontiguousarray(cv[2 * c:2 * c + 2].reshape(2, 512, 1024)),
            "h0T": h0T, "lb0T": lb0T, "cs0T": cs0T,
            "wA": wA, "wB": wB, "wC": wC, "wo0": wo0, "wo1": wo1, "wax": wax,
            "pvec": pvc, "fnb": fnb, "biasT": biasT,
        })
    if "nc" not in _NC_CACHE:
        _NC_CACHE["nc"] = build_program()
    res = run_bass_kernel_spmd(_NC_CACHE["nc"], in_maps, core_ids=list(range(8)))
    R = res.results
    global LAST_DBG
    LAST_DBG = [r.get("dbg") for r in R] if DEBUG else None

    y_prompt = np.zeros((4, 2048, D), np.float32)
    y_sample = np.zeros((16, 32, D), np.float32)
    k_p = np.zeros((1, 4, 512, 8, 128), np.float32)
    v_p = np.zeros_like(k_p)
    h_p = np.zeros((1, 4, 1024), np.float32)
    lb_p = np.zeros((1, 4, 3, 1024), np.float32)
    cb_p = np.zeros((1, 4, 30, 2048), np.float32)
    k_s = np.zeros((1, 16, 32, 8, 128), np.float32)
    v_s = np.zeros_like(k_s)
    h_s = np.zeros((1, 16, 1024), np.float32)
    lb_s = np.zeros((1, 16, 3, 1024), np.float32)
    cb_s = np.zeros((1, 16, 30, 2048), np.float32)
    for c in range(8):
        b, half = c // 2, c % 2
        r = R[c]
        y_prompt[b, half * 1024:(half + 1) * 1024] = r["y_main"]
        y_sample[2 * c:2 * c + 2] = r["y_samp"].reshape(2, 32, D)
        sto = r["sto"]
        cbo = r["cbo"].reshape(128, 16, 3, 30)
        kvs = r["kvo_s"].reshape(2, 2, 32, 8, 128)
        k_s[0, 2 * c:2 * c + 2] = kvs[0]
        v_s[0, 2 * c:2 * c + 2] = kvs[1]
        hs = sto[:, ST_HS:ST_HS + 16].reshape(128, 8, 2)
        h_s[0, 2 * c:2 * c + 2] = hs.transpose(2, 1, 0).reshape(2, 1024)
        ls = sto[:, ST_LS:ST_LS + 48].reshape(128, 8, 2, 3)
        lb_s[0, 2 * c:2 * c + 2] = ls.transpose(2, 3, 1, 0).reshape(2, 3, 1024)
        cb_s[0, 2 * c:2 * c + 2] = cbo[:, :, 1:3, :].transpose(2, 3, 1, 0).reshape(2, 30, 2048)
        if half == 1:
            kvp = r["kvo_p"].reshape(2, 512, 8, 128)
            k_p[0, b] = kvp[0]
            v_p[0, b] = kvp[1]
            h_p[0, b] = sto[:, ST_HP:ST_HP + 8].T.reshape(1024)
            lb_p[0, b] = sto[:, ST_LP:ST_LP + 24].reshape(128, 8, 3).transpose(2, 1, 0).reshape(3, 1024)
            cb_p[0, b] = cbo[:, :, 0, :].transpose(2, 1, 0).reshape(30, 2048)
    return (y_prompt, y_sample, k_p, v_p, h_p, lb_p, cb_p, k_s, v_s, h_s, lb_s, cb_s)
```

```python
import numpy as np
from contextlib import ExitStack
import concourse.bass as bass
import concourse.mybir as mybir
from concourse.bass_utils import run_bass_kernel_spmd

F32 = mybir.dt.float32
BF16 = mybir.dt.bfloat16
AF = mybir.ActivationFunctionType
ALU = mybir.AluOpType

ENGS = ("pe", "act", "dve", "pool", "sp")
import os
DEBUG = bool(os.environ.get("KDEBUG"))
DBG_COLS = 40000
DBG_MAP = {}
D = 2048
NKT = 16
TALL = 2112
QOFF = 960
NQ = 1152
EPS = 1e-6
NEG = -30000.0

PV_G0, PV_G1, PV_CW, PV_CB, PV_BA, PV_BX, PV_LAM = 0, 16, 32, 64, 72, 80, 88
PV_DWW, PV_DWB, PV_LNG, PV_LNB, PV_FLAG, PV_N = 96, 592, 608, 624, 640, 641
ST_HP, ST_HS, ST_LP, ST_LS, ST_N = 0, 8, 24, 48, 96


class Buf:
    __slots__ = ("name", "w", "r")

    def __init__(self, name=""):
        self.name = name
        self.w = None
        self.r = {}


class _Rec:
    def __getattr__(self, name):
        def f(*a, **k):
            return (name, a, k)
        return f


_REC = _Rec()


class Prog:
    NDMA = 6

    def __init__(self, nc):
        self.nc = nc
        self.q = {e: [] for e in ENGS}
        self.cnt = {e: 0 for e in ENGS}
        self.seen = {e: {} for e in ENGS}
        self.pe_pending = False
        self.dma_rr = {e: 0 for e in ENGS}
        self.dma_val = {}

    def _need(self, eng, deps):
        best = {}
        for t in deps:
            if t is None:
                continue
            key, val = t
            if key == eng and eng == "pe":
                continue
            if key == "pe":
                assert val <= self.cnt["pe"], "dependency on un-incremented PE op"
            if best.get(key, 0) < val:
                best[key] = val
        for key, val in best.items():
            if self.seen[eng].get(key, 0) >= val:
                continue
            self.seen[eng][key] = val
            self.q[eng].append(("wait", key, val))

    def _deps(self, reads, writes):
        deps = []
        for b in reads:
            deps.append(b.w)
        for b in writes:
            deps.append(b.w)
            deps.extend(b.r.values())
        return deps

    def _commit(self, ticket, reads, writes):
        k = ticket[0]
        for b in reads:
            if b.r.get(k, (k, 0))[1] < ticket[1]:
                b.r[k] = ticket
        for b in writes:
            b.w = ticket
            b.r = {}

    def op(self, eng, fn, reads=(), writes=(), inc=True):
        self._need(eng, self._deps(reads, writes))
        if inc:
            self.cnt[eng] += 1
            ticket = (eng, self.cnt[eng])
            if eng == "pe":
                self.pe_pending = False
        else:
            assert eng == "pe"
            ticket = (eng, self.cnt[eng] + 1)
            self.pe_pending = True
        self.q[eng].append(("op", fn(_REC), inc))
        self._commit(ticket, reads, writes)
        return ticket

    DESC_LIMIT = 1400

    def dma(self, queue, out, in_, reads=(), writes=(), **kw):
        idx = self.dma_rr[queue]
        self.dma_rr[queue] = (idx + 1) % self.NDMA
        key = ("dma", queue, idx)
        prev = self.dma_val.get(key, 0)
        deps = self._deps(reads, writes)
        if prev:
            deps.append((key, prev))
        if queue == "sp":
            shp = list(out.shape)
            nd = 1
            for d_ in shp[:-1]:
                nd *= d_
            nd *= (shp[-1] * 4 + 4095) // 4096
            fifo = self.__dict__.setdefault("_fifo", [])
            while fifo and sum(x[1] for x in fifo) + nd > self.DESC_LIMIT:
                deps.append(fifo.pop(0)[0])
            self._pending_nd = nd
        self._need(queue, deps)
        val = prev + 16
        self.dma_val[key] = val
        self.q[queue].append(("dma", out, in_, key, kw))
        ticket = (key, val)
        if queue == "sp":
            self._fifo.append((ticket, self._pending_nd))
        self._commit(ticket, reads, writes)
        return ticket

    def barrier(self):
        assert not self.pe_pending
        tickets = [(e, self.cnt[e]) for e in ENGS if self.cnt[e] > 0]
        tickets += [(k, v) for k, v in self.dma_val.items()]
        for e in ENGS:
            self._need(e, [t for t in tickets if t[0] != e])

    def finish(self):
        for key, val in self.dma_val.items():
            if self.seen["sp"].get(key, 0) < val:
                self.seen["sp"][key] = val
                self.q["sp"].append(("wait", key, val))

    def emit(self, stack):
        nc = self.nc
        assert not self.pe_pending
        sems = {}
        for e in ENGS:
            sems[e] = stack.enter_context(nc.semaphore("s_" + e))
        for key in self.dma_val:
            sems[key] = stack.enter_context(nc.semaphore("d_%s_%d" % (key[1], key[2])))
        block = stack.enter_context(nc.Block())
        handles = {"pe": block.tensor, "act": block.scalar, "dve": block.vector,
                   "pool": block.gpsimd, "sp": block.sync}

        def run(ename):
            items = self.q[ename]

            def body(eng):
                for it in items:
                    if it[0] == "wait":
                        eng.wait_ge(sems[it[1]], it[2])
                    elif it[0] == "op":
                        ins = getattr(eng, it[1][0])(*it[1][1], **it[1][2])
                        if it[2]:
                            ins.then_inc(sems[ename], 1)
                    else:
                        _, out, in_, key, kw = it
                        eng.dma_start(out=out, in_=in_, **kw).then_inc(sems[key], 16)
            return body

        for e in ENGS:
            if self.q[e]:
                handles[e](run(e))


ARENA_WORDS = 53000


class Arena:
    def __init__(self, ap):
        self.ap = ap
        self.top = 0

    def f32(self, n):
        na = (n + 7) // 8 * 8
        assert self.top + na <= ARENA_WORDS, ("arena overflow", self.top, na)
        v = self.ap[:, self.top:self.top + n]
        self.top += na
        return v

    def bf(self, n):
        assert n % 2 == 0
        v = self.f32(n // 2)
        return v.bitcast(BF16)


def nblocks(lo, hi, step=512):
    out = []
    c = lo
    while c < hi:
        n = min(step, hi - c)
        out.append((c, n))
        c += n
    return out


def build_program():
    nc = bass.Bass("TRN2", target_bir_lowering=False)

    def din(name, shape):
        return nc.dram_tensor(name, list(shape), F32, kind="ExternalInput").ap()

    def dout(name, shape):
        return nc.dram_tensor(name, list(shape), F32, kind="ExternalOutput").ap()

    x_ext = din("x_ext", [TALL, D])
    kc = din("kc", [2, 512, 1024])
    vc = din("vc", [2, 512, 1024])
    h0T = din("h0T", [128, 16])
    lb0T = din("lb0T", [128, 48])
    cs0T = din("cs0T", [128, 16 * 60])
    wA = din("wA", [8, D, 512])
    wB = din("wB", [8, D, 256])
    wC = din("wC", [16, D, 384])
    wo0 = din("wo0", [D, D])
    wo1 = din("wo1", [D, D])
    wax = din("wax", [128, 2 * 8 * 128])
    pvec_d = din("pvec", [128, PV_N])
    fnb_d = din("fnb", [128, D])
    biasT_d = din("biasT", [8, 128, 704])

    y_main = dout("y_main", [1024, D])
    y_samp = dout("y_samp", [64, D])
    kvo_p = dout("kvo_p", [2, 512, 1024])
    kvo_s = dout("kvo_s", [2, 64, 1024])
    sto_d = dout("sto", [128, ST_N])
    cbo_d = dout("cbo", [128, 16 * 90])
    x1_scr = nc.dram_tensor("x1_scr", [NQ, D], F32, kind="Internal").ap()
    dbg_d = dout("dbg", [128, DBG_COLS]) if DEBUG else None
    DBG_MAP.clear()
    dbg_state = {"off": 0}

    def dbg(name, ap, ncols, buf, nrows=128):
        if not DEBUG or name in DBG_MAP:
            return
        o = dbg_state["off"]
        DBG_MAP[name] = (o, ncols, nrows)
        dbg_state["off"] = o + ncols
        p.dma("sp", dbg_d[0:nrows, o:o + ncols], ap, reads=[buf])

    st = ExitStack()
    with st:
        arena_t = st.enter_context(nc.sbuf_tensor("arena", [128, ARENA_WORDS], F32))
        psT = [st.enter_context(nc.psum_tensor("psT%d" % i, [128, 1024], BF16)) for i in range(2)]
        psF = [st.enter_context(nc.psum_tensor("psF%d" % i, [128, 512], F32)) for i in range(6)]
        psT_b = [Buf("psT%d" % i) for i in range(2)]
        psF_b = [Buf("psF%d" % i) for i in range(6)]
        p = Prog(nc)
        ar = Arena(arena_t[:])
        rr = {"T": 0, "F": 0}

        psF_all = [t_[:] for t_ in psF] + [t_[:].bitcast(F32) for t_ in psT]
        psF_b_all = psF_b + psT_b
        rr["NF"] = 6

        def set_banks(n):
            rr["NF"] = n
            rr["F"] %= n

        def psum_f():
            i = rr["F"]
            rr["F"] = (i + 1) % rr["NF"]
            return psF_all[i], psF_b_all[i]

        def psum_t():
            i = rr["T"]
            rr["T"] = (i + 1) % 2
            return psT[i][:], psT_b[i]

        pvec = ar.f32(PV_N)
        b_pvec = Buf("pvec")
        fnb = ar.f32(D)
        b_fnb = Buf("fnb")
        identf = ar.f32(128)
        ident = ar.bf(128)
        ones_bf = ar.bf(128)
        flag_bf = ar.bf(128)
        onesdiv = ar.bf(128)
        c8 = ar.f32(8)
        c16 = ar.f32(8)
        lsc = ar.f32(64)
        sto = ar.f32(ST_N)
        b_const = Buf("const")
        b_sto = Buf("sto")
        p.dma("sp", pvec, pvec_d, writes=[b_pvec])
        p.dma("sp", fnb, fnb_d, writes=[b_fnb])
        p.op("pool", lambda e: e.memset(identf, 0.0), writes=[b_const])
        p.op("pool", lambda e: e.affine_select(identf, identf, [[-1, 128]], ALU.not_equal, 1.0,
                                               base=0, channel_multiplier=1),
             reads=[b_const], writes=[b_const])
        p.op("dve", lambda e: e.tensor_copy(ident, identf), reads=[b_const], writes=[b_const])
        p.op("dve", lambda e: e.memset(ones_bf, 1.0), writes=[b_const])
        p.op("dve", lambda e: e.memset(onesdiv, 1.0 / D), writes=[b_const])
        p.op("dve", lambda e: e.memset(sto, 0.0), writes=[b_sto])
        flag = pvec[:, PV_FLAG:PV_FLAG + 1]
        p.op("dve", lambda e: e.tensor_scalar(flag_bf, ones_bf, flag, None, ALU.mult),
             reads=[b_const, b_pvec], writes=[b_const])
        lam = pvec[:, PV_LAM:PV_LAM + 8]
        t_abs, t_y, t_w, t_w2, t_s, t_m = (lsc[:, 8 * i:8 * i + 8] for i in range(6))
        RW = dict(reads=[b_const, b_pvec], writes=[b_const])
        p.op("dve", lambda e: e.tensor_scalar(t_abs, lam, -1.0, None, ALU.mult), **RW)
        p.op("dve", lambda e: e.tensor_tensor(t_abs, t_abs, lam, ALU.max), **RW)
        p.op("act", lambda e: e.activation(t_y, t_abs, AF.Exp, scale=-1.0), **RW)
        p.op("dve", lambda e: e.tensor_scalar(t_w, t_y, 2.0, None, ALU.add), **RW)
        p.op("dve", lambda e: e.reciprocal(t_w, t_w), **RW)
        p.op("dve", lambda e: e.tensor_tensor(t_w, t_w, t_y, ALU.mult), **RW)
        p.op("dve", lambda e: e.tensor_tensor(t_w2, t_w, t_w, ALU.mult), **RW)
        p.op("dve", lambda e: e.memset(t_s, 1.0 / 15.0), **RW)
        for kk in (13, 11, 9, 7, 5, 3, 1):
            p.op("dve", lambda e: e.tensor_tensor(t_s, t_s, t_w2, ALU.mult), **RW)
            p.op("dve", (lambda cst: (lambda e: e.tensor_scalar(t_s, t_s, cst, None, ALU.add)))(1.0 / kk), **RW)
        p.op("dve", lambda e: e.tensor_tensor(t_s, t_s, t_w, ALU.mult), **RW)
        p.op("dve", lambda e: e.tensor_scalar(t_m, lam, 0.0, None, ALU.min), **RW)
        p.op("dve", lambda e: e.scalar_tensor_tensor(t_s, t_s, -2.0, t_m, ALU.mult, ALU.add), **RW)
        p.op("dve", lambda e: e.tensor_scalar(c8, t_s, 8.0, None, ALU.mult), **RW)
        p.op("dve", lambda e: e.tensor_scalar(c16, t_s, 16.0, None, ALU.mult), **RW)

        base_top = ar.top

        xnT = ar.bf(NKT * TALL).rearrange("p (k c) -> p k c", c=TALL)
        b_xnT = [Buf("xnT%d" % t) for t in range(17)]
        mixT = ar.bf(NKT * NQ).rearrange("p (k c) -> p k c", c=NQ)
        b_mix = [Buf("mix%d" % k) for k in range(NKT)]
        l0_top = ar.top

        def xn_bufs(c0, n):
            return [b_xnT[t] for t in range(c0 // 128, (c0 + n - 1) // 128 + 1)]

        def rms_part1(nrows, xt, b_xt, xs, b_xs, ss, rstd, b_s):
            p.op("act", lambda e: e.activation(xs[0:nrows, :], xt[0:nrows, :], AF.Square, accum_out=ss[0:nrows, :]),
                 reads=[b_xt], writes=[b_xs, b_s])
            p.op("act", lambda e: e.activation(rstd[0:nrows, :], ss[0:nrows, :], AF.Sqrt, scale=1.0 / D, bias=EPS),
                 reads=[b_s], writes=[b_s])
            p.op("dve", lambda e: e.reciprocal(rstd[0:nrows, :], rstd[0:nrows, :]), reads=[b_s], writes=[b_s])
            p.op("act", lambda e: e.activation(xs[0:nrows, :], xt[0:nrows, :], AF.Copy, scale=rstd[0:nrows, :]),
                 reads=[b_xt, b_s], writes=[b_xs])

        def rms_part2(nrows, xs, b_xs, gcol, dstT, col0, b_dst):
            for half in range(2):
                pt, bpt = psum_t()
                for i in range(8):
                    kt = half * 8 + i
                    p.op("pe", lambda e: e.transpose(
                        pt[:, i * 128:i * 128 + nrows], xs[0:nrows, kt * 128:(kt + 1) * 128],
                        ident[0:nrows, 0:nrows]),
                        reads=[b_xs, b_const], writes=[bpt], inc=(i == 7))
                gb = pvec[:, gcol + half * 8:gcol + half * 8 + 8].unsqueeze(2).to_broadcast([128, 8, nrows])
                src = pt[:, 0:1024].rearrange("p (k c) -> p k c", c=128)[:, :, 0:nrows]
                p.op("dve", lambda e: e.tensor_tensor(dstT[:, half * 8:half * 8 + 8, col0:col0 + nrows], src, gb,
                                                      ALU.mult),
                     reads=[bpt, b_pvec], writes=[b_dst])

        wAb = [ar.bf(NKT * 512).rearrange("p (k c) -> p k c", c=512) for _ in range(2)]
        b_wA = [Buf() for _ in range(2)]
        for h_ in range(2):
            for g in range(4):
                p.dma("pool", wAb[h_][:, 4 * g:4 * g + 4, :],
                      wA[h_, 512 * g:512 * (g + 1), :].rearrange("(k p) c -> p k c", p=128),
                      writes=[b_wA[h_]])
        NXT = 4
        xt_ = [ar.f32(D) for _ in range(NXT)]
        xs_ = [ar.bf(D) for _ in range(2)]
        sst = [ar.f32(8) for _ in range(2)]
        b_xt = [Buf() for _ in range(NXT)]
        b_xs = [Buf() for _ in range(2)]
        b_ss = [Buf() for _ in range(2)]
        pend = None
        for t in range(17):
            nrows = 128 if t < 16 else 64
            i = t % 2
            ix = t % NXT
            p.dma("sp", xt_[ix][0:nrows, :], x_ext[t * 128:t * 128 + nrows, :], writes=[b_xt[ix]])
            rms_part1(nrows, xt_[ix], b_xt[ix], xs_[i], b_xs[i], sst[i][:, 0:1], sst[i][:, 1:2], b_ss[i])
            if pend is not None:
                rms_part2(*pend)
            pend = (nrows, xs_[i], b_xs[i], PV_G0, xnT, t * 128, b_xnT[t])
        rms_part2(*pend)
        p.barrier()
        ar.top = l0_top

        set_banks(8)
        wAb = [ar.bf(NKT * 512).rearrange("p (k c) -> p k c", c=512) for _ in range(2)]
        HB = []
        for _par in range(2):
            hb = dict(
                QT=ar.bf(1216), KT=ar.bf(1728), sga=ar.bf(NQ),
                Vt=ar.bf(14 * 128).rearrange("p (t d) -> p t d", d=128),
                KcT=ar.bf(1024).rearrange("p (s k) -> p s k", k=512),
                Vc=ar.bf(1024).rearrange("p (s j d) -> p s j d", s=2, j=4),
                biasb=ar.f32(704),
                b_QT=Buf(), b_KT=Buf(), b_sga=Buf(), b_Vt=Buf(), b_KcT=Buf(), b_Vc=Buf(), b_bias=Buf())
            HB.append(hb)
        kst_g = [ar.f32(512).rearrange("p (j d) -> p j d", d=128) for _ in range(2)]
        b_kst_g = [Buf(), Buf()]
        for hb in HB:
            hb["kst"] = kst_g
            hb["b_kst"] = b_kst_g
        st32 = [ar.f32(512) for _ in range(2)]
        b_st32 = [Buf() for _ in range(2)]
        tnh = [ar.f32(512) for _ in range(2)]
        b_tnh = [Buf() for _ in range(2)]
        Ssb = [ar.f32(640) for _ in range(2)]
        PT = [ar.bf(640) for _ in range(2)]
        b_Ssb = [Buf() for _ in range(2)]
        b_PT = [Buf() for _ in range(2)]
        rec = [ar.f32(128) for _ in range(2)]
        otmp = [ar.f32(128) for _ in range(2)]
        b_rec = [Buf() for _ in range(2)]
        kvst = [ar.f32(128) for _ in range(2)]
        b_kvst = [Buf() for _ in range(2)]
        rr_att = {"i": 0, "st": 0, "kv": 0, "tn": 0}

        def load_wA(h):
            for g in range(4):
                p.dma("pool", wAb[h % 2][:, 4 * g:4 * g + 4, :],
                      wA[h, 512 * g:512 * (g + 1), :].rearrange("(k p) c -> p k c", p=128),
                      writes=[b_wA[h % 2]])

        def proj(wt, b_w, ccol, c0, n, ps):
            rd = [b_w] + xn_bufs(c0, n)
            for kt in range(NKT):
                p.op("pe", lambda e: e.matmul(ps[0][:, 0:n], wt[:, kt, ccol:ccol + 128], xnT[:, kt, c0:c0 + n],
                                              start=(kt == 0), stop=(kt == NKT - 1)),
                     reads=rd, writes=[ps[1]], inc=(kt == NKT - 1))

        def attn_S(hb, h, qa, nq, keytiles, mixcol, qlo, last64):
            i = rr_att["i"]
            rr_att["i"] = (i + 1) % 2
            QT, biasb = hb["QT"], hb["biasb"]
            psA = psum_f()
            psB = psum_f()
            rK = [hb["b_QT"], hb["b_KT"], hb["b_KcT"]]
            for j in range(4):
                p.op("pe", lambda e: e.matmul(psA[0][:, j * 128 + qlo:j * 128 + qlo + nq], keytiles[j][0],
                                              QT[:, qa:qa + nq], start=True, stop=True),
                     reads=rK, writes=[psA[1]], inc=(j == 3))
            kr = 64 if last64 else 128
            p.op("pe", lambda e: e.matmul(psB[0][0:kr, qlo:qlo + nq], keytiles[4][0], QT[:, qa:qa + nq],
                                          start=True, stop=True), reads=rK, writes=[psB[1]])
            S, P_ = Ssb[i], PT[i]
            S4 = S[:, 0:512].rearrange("p (j q) -> p j q", q=128)[:, :, qlo:qlo + nq]
            A4 = psA[0][:, 0:512].rearrange("p (j q) -> p j q", q=128)[:, :, qlo:qlo + nq]
            B4 = biasb[:, 0:512].rearrange("p (j q) -> p j q", q=128)[:, :, qlo:qlo + nq]
            P4 = P_[:, 0:512].rearrange("p (j q) -> p j q", q=128)[:, :, qlo:qlo + nq]
            p.op("dve", lambda e: e.tensor_tensor(S4, A4, B4, ALU.add), reads=[psA[1], hb["b_bias"]],
                 writes=[b_Ssb[i]])
            if last64:
                s_idx = keytiles[4][3]
                bsl = biasb[0:64, 640 + 32 * s_idx:640 + 32 * s_idx + 32]
            else:
                bsl = biasb[:, 512 + qlo:512 + qlo + nq]
            p.op("dve", lambda e: e.tensor_tensor(S[0:kr, 512 + qlo:512 + qlo + nq], psB[0][0:kr, qlo:qlo + nq], bsl,
                                                  ALU.add), reads=[psB[1], hb["b_bias"]], writes=[b_Ssb[i]])
            p.op("act", lambda e: e.activation(P4, S4, AF.Exp), reads=[b_Ssb[i]], writes=[b_PT[i]])
            p.op("act", lambda e: e.activation(P_[0:kr, 512 + qlo:512 + qlo + nq], S[0:kr, 512 + qlo:512 + qlo + nq],
                                               AF.Exp), reads=[b_Ssb[i]], writes=[b_PT[i]])
            return (hb, h, i, nq, keytiles, mixcol, qlo, kr)

        def attn_V(state):
            hb, h, i, nq, keytiles, mixcol, qlo, kr = state
            P_ = PT[i]
            psO = psum_f()
            rV = [b_PT[i], hb["b_Vt"], hb["b_Vc"], b_const]
            for j in range(5):
                k_ = kr if j == 4 else 128
                p.op("pe", lambda e: e.matmul(
                    psO[0][:, 0:nq], keytiles[j][1], P_[0:k_, j * 128 + qlo:j * 128 + qlo + nq],
                    start=(j == 0), stop=(j == 4)), reads=rV, writes=[psO[1]], inc=False)
            for j in range(5):
                k_ = kr if j == 4 else 128
                p.op("pe", lambda e: e.matmul(
                    psO[0][:, 128:128 + nq], keytiles[j][2], P_[0:k_, j * 128 + qlo:j * 128 + qlo + nq],
                    start=(j == 0), stop=(j == 4)), reads=rV, writes=[psO[1]], inc=(j == 4))
            r_, o_ = rec[i], otmp[i]
            p.op("dve", lambda e: e.tensor_scalar(r_[:, 0:nq], psO[0][:, 128:128 + nq], 2.0, 1e-30,
                                                  ALU.mult, ALU.max), reads=[psO[1]], writes=[b_rec[i]])
            p.op("dve", lambda e: e.reciprocal(r_[:, 0:nq], r_[:, 0:nq]), reads=[b_rec[i]], writes=[b_rec[i]])
            p.op("dve", lambda e: e.tensor_tensor(o_[:, 0:nq], psO[0][:, 0:nq], r_[:, 0:nq], ALU.mult),
                 reads=[psO[1], b_rec[i]], writes=[b_rec[i]])
            p.op("dve", lambda e: e.tensor_tensor(mixT[:, h, mixcol:mixcol + nq], o_[:, 0:nq],
                                                  hb["sga"][:, mixcol:mixcol + nq], ALU.mult),
                 reads=[b_rec[i], hb["b_sga"]], writes=[b_mix[h]])

        def p_tasks(h):
            hb = HB[h % 2]
            wt, b_w = wAb[h % 2], b_wA[h % 2]
            QT, KT, sga, Vt, KcT, Vc = hb["QT"], hb["KT"], hb["sga"], hb["Vt"], hb["KcT"], hb["Vc"]
            tasks = []

            def t_dmas():
                p.dma("sp", hb["biasb"], biasT_d[h], writes=[hb["b_bias"]])
                for s_ in range(2):
                    p.dma("sp", hb["kst"][s_],
                          kc[s_, :, h * 128:(h + 1) * 128].rearrange("(j p) d -> p j d", p=128),
                          writes=[hb["b_kst"][s_]])
                    p.dma("pool", Vc[:, s_, :, :],
                          vc[s_, :, h * 128:(h + 1) * 128].rearrange("(j p) d -> p j d", p=128),
                          writes=[hb["b_Vc"]])
            tasks.append(t_dmas)

            def t_q(c0, n):
                ps = psum_f()
                proj(wt, b_w, 0, c0, n, ps)
                p.op("act", lambda e: e.activation(QT[:, c0 - 896:c0 - 896 + n], ps[0][:, 0:n], AF.Copy,
                                                   scale=128.0 ** -0.5), reads=[ps[1]], writes=[hb["b_QT"]])
            for (c0, n) in nblocks(896, TALL):
                tasks.append(lambda c0=c0, n=n: t_q(c0, n))

            def t_ga(c0, n):
                ps = psum_f()
                proj(wt, b_w, 384, c0, n, ps)
                ti = rr_att["tn"]
                rr_att["tn"] = (ti + 1) % 2
                p.op("act", lambda e: e.activation(tnh[ti][:, 0:n], ps[0][:, 0:n], AF.Tanh, scale=0.5),
                     reads=[ps[1]], writes=[b_tnh[ti]])
                p.op("dve", lambda e: e.scalar_tensor_tensor(sga[:, c0 - QOFF:c0 - QOFF + n], tnh[ti][:, 0:n], 1.0,
                                                             ps[0][:, 0:n], ALU.add, ALU.mult),
                     reads=[ps[1], b_tnh[ti]], writes=[hb["b_sga"]])
            for (c0, n) in nblocks(QOFF, TALL):
                tasks.append(lambda c0=c0, n=n: t_ga(c0, n))

            def t_kc(s_):
                ps = psum_f()
                for j in range(4):
                    p.op("pe", lambda e: e.transpose(ps[0][:, j * 128:(j + 1) * 128], hb["kst"][s_][:, j, :], identf),
                         reads=[hb["b_kst"][s_], b_const], writes=[ps[1]], inc=(j == 3))
                p.op("act", lambda e: e.activation(KcT[:, s_, :], ps[0][:, 0:512], AF.Copy),
                     reads=[ps[1]], writes=[hb["b_KcT"]])
            tasks.append(lambda: t_kc(0))
            tasks.append(lambda: t_kc(1))

            def kv_proj(which, c0, n):
                ps = psum_f()
                proj(wt, b_w, 128 * which, c0, n, ps)
                need32 = (which == 2) or (c0 + n > 1536)
                si = None
                if which == 1:
                    p.op("act", lambda e: e.activation(KT[:, c0 - 384:c0 - 384 + n], ps[0][:, 0:n], AF.Copy),
                         reads=[ps[1]], writes=[hb["b_KT"]])
                if need32:
                    si = rr_att["st"]
                    rr_att["st"] = (si + 1) % 2
                    eng = "dve" if which == 1 else "act"
                    if eng == "act":
                        p.op("act", lambda e: e.activation(st32[si][:, 0:n], ps[0][:, 0:n], AF.Copy),
                             reads=[ps[1]], writes=[b_st32[si]])
                    else:
                        p.op("dve", lambda e: e.tensor_copy(st32[si][:, 0:n], ps[0][:, 0:n]),
                             reads=[ps[1], hb["b_KT"]], writes=[b_st32[si]])
                return si

            def kv_post(which, c0, n, si):
                if si is None:
                    return
                s32, bs32 = st32[si], b_st32[si]
                if which == 2:
                    pv = psum_f()
                    tiles = nblocks(0, n, 128)
                    for ti, (o, m_) in enumerate(tiles):
                        p.op("pe", lambda e: e.transpose(pv[0][0:m_, ti * 128:(ti + 1) * 128], s32[:, o:o + m_],
                                                         identf),
                             reads=[bs32, b_const], writes=[pv[1]], inc=(ti == len(tiles) - 1))
                    vt0 = (c0 - 384) // 128
                    nfull = sum(1 for (_, m_) in tiles if m_ == 128)
                    if nfull:
                        p.op("dve", lambda e: e.tensor_copy(
                            Vt[:, vt0:vt0 + nfull, :], pv[0][:, 0:128 * nfull].rearrange("p (t d) -> p t d", d=128)),
                            reads=[pv[1]], writes=[hb["b_Vt"]])
                    if nfull < len(tiles):
                        p.op("dve", lambda e: e.tensor_copy(Vt[0:64, vt0 + nfull, :],
                                                            pv[0][0:64, 128 * nfull:128 * nfull + 128]),
                             reads=[pv[1]], writes=[hb["b_Vt"]])
                for (o, m_) in nblocks(0, n, 128):
                    ta = c0 + o
                    if ta < 1536:
                        continue
                    pk = psum_f()
                    p.op("pe", lambda e: e.transpose(pk[0][0:m_, 0:128], s32[:, o:o + m_], identf),
                         reads=[bs32, b_const], writes=[pk[1]])
                    ki = rr_att["kv"]
                    rr_att["kv"] = (ki + 1) % 2
                    p.op("act", lambda e: e.activation(kvst[ki][0:m_, 0:128], pk[0][0:m_, 0:128], AF.Copy),
                         reads=[pk[1]], writes=[b_kvst[ki]])
                    if ta < 2048:
                        dst = kvo_p[which - 1, ta - 1536:ta - 1536 + m_, h * 128:(h + 1) * 128]
                    else:
                        dst = kvo_s[which - 1, 0:m_, h * 128:(h + 1) * 128]
                    p.dma("sp", dst, kvst[ki][0:m_, 0:128], reads=[b_kvst[ki]])

            kvb = [(w_, c0, n) for w_ in (1, 2) for (c0, n) in nblocks(384, TALL)]
            state = {}

            def t_kv(k):
                w_, c0, n = kvb[k]
                state[k] = kv_proj(w_, c0, n)
                if k >= 1:
                    w2, c2, n2 = kvb[k - 1]
                    kv_post(w2, c2, n2, state[k - 1])
                if k == len(kvb) - 1:
                    kv_post(w_, c0, n, state[k])
            for k in range(len(kvb)):
                tasks.append(lambda k=k: t_kv(k))
            return tasks

        def a_tasks(h):
            hb = HB[h % 2]
            KT, Vt, KcT, Vc = hb["KT"], hb["Vt"], hb["KcT"], hb["Vc"]
            blocks = []
            for m in range(7, 16):
                qlo = 64 if m == 7 else 0
                nq = 128 - qlo
                kts = []
                for j in range(5):
                    kt_ = m - 4 + j
                    kcol = (kt_ - 3) * 128
                    kts.append((KT[:, kcol:kcol + 128], Vt[:, kt_ - 3, :], flag_bf if kt_ < 8 else ones_bf))
                blocks.append((hb, h, m * 128 + qlo - 896, nq, kts, m * 128 + qlo - QOFF, qlo, False))
            for s_ in range(2):
                kts = []
                for j in range(4):
                    kts.append((KcT[:, s_, j * 128:(j + 1) * 128], Vc[:, s_, j, :], ones_bf))
                kts.append((KT[:, 2048 - 384:2112 - 384], Vt[0:64, 13, :], ones_bf[0:64, :], s_))
                blocks.append((hb, h, 2048 + 32 * s_ - 896, 32, kts, 2048 + 32 * s_ - QOFF, 0, True))
            st_ = {}
            tasks = []

            def t_a(k):
                if k < len(blocks):
                    st_[k] = attn_S(*blocks[k])
                if k >= 1:
                    attn_V(st_[k - 1])
            for k in range(len(blocks) + 1):
                tasks.append(lambda k=k: t_a(k))
            return tasks

        wBb_pre = [arena_t[:][:, l0_top + 2048 * i_:l0_top + 2048 * (i_ + 1)].bitcast(BF16)
                   .rearrange("p (k c) -> p k c", c=256) for i_ in range(2)]
        b_wB = [Buf() for _ in range(2)]
        for t in p_tasks(0):
            t()
        for h in range(8):
            A = a_tasks(h)
            Pn = p_tasks(h + 1) if h + 1 < 8 else []
            if h + 2 < 8:
                load_wA(h + 2)
            if h == 6:
                for i_ in range(2):
                    for g in range(4):
                        p.dma("pool", wBb_pre[i_][:, 4 * g:4 * g + 4, :],
                              wB[i_, 512 * g:512 * (g + 1), :].rearrange("(k p) c -> p k c", p=128),
                              writes=[b_wB[i_], b_wA[0]])
            na, npn = len(A), len(Pn)
            ia = 0
            for ip in range(npn):
                Pn[ip]()
                want = ((ip + 1) * na) // npn
                while ia < want:
                    A[ia]()
                    ia += 1
            while ia < na:
                A[ia]()
                ia += 1
        p.barrier()
        ar.top = l0_top

        XP0, XSA, XSB = 3, 2054, 2089
        wBb = [ar.bf(NKT * 256).rearrange("p (k c) -> p k c", c=256) for _ in range(2)]
        assert NKT * 256 // 2 == 2048
        waxb = ar.bf(2 * 8 * 128).rearrange("p (g n e) -> p g n e", g=2, n=8)
        b_wax = Buf()
        h0sb = ar.f32(16)
        b_h0 = Buf()
        xb_ = [ar.f32(2128) for _ in range(2)]
        xc = ar.f32(TALL)
        xcb = ar.bf(TALL)
        rg = ar.f32(TALL)
        ig = ar.f32(TALL)
        bb = ar.f32(TALL)
        aa = xc
        hh = rg
        sgb_ = [ar.f32(NQ) for _ in range(2)]
        hin = ar.f32(8)
        b_xc, b_xcb, b_rg, b_ig, b_bb, b_hin = (Buf() for _ in range(6))
        b_xb_ = [Buf() for _ in range(2)]
        b_sgb_ = [Buf() for _ in range(2)]
        b_aa, b_hh = b_xc, b_rg
        p.dma("pool", waxb, wax.rearrange("p (g n e) -> p g n e", g=2, n=8), writes=[b_wax])
        p.dma("sp", h0sb, h0T, writes=[b_h0])

        def load_wB(n_):
            for g in range(4):
                p.dma("pool", wBb[n_ % 2][:, 4 * g:4 * g + 4, :],
                      wB[n_, 512 * g:512 * (g + 1), :].rearrange("(k p) c -> p k c", p=128),
                      writes=[b_wB[n_ % 2]])

        def xbcol(c):
            if c < 2048:
                return XP0 + c
            if c < 2080:
                return XSA + (c - 2048)
            return XSB + (c - 2080)

        def lru_P(n_):
            wt, b_w = wBb[n_ % 2], b_wB[n_ % 2]
            xb, b_xb = xb_[n_ % 2], b_xb_[n_ % 2]
            sgb, b_sgb = sgb_[n_ % 2], b_sgb_[n_ % 2]
            p.op("pool", lambda e: e.memset(xb[:, 0:3], 0.0), writes=[b_xb])
            lbv = lb0T[:, n_ * 6:n_ * 6 + 6]
            p.dma("sp", xb[:, XSA - 3:XSA], lbv[:, 0:3], writes=[b_xb])
            p.dma("sp", xb[:, XSB - 3:XSB], lbv[:, 3:6], writes=[b_xb])
            for (c0, n) in nblocks(0, TALL):
                ps = psum_f()
                proj(wt, b_w, 0, c0, n, ps)
                if c0 < 2048:
                    p.op("act", lambda e: e.activation(xb[:, XP0 + c0:XP0 + c0 + n], ps[0][:, 0:n], AF.Copy),
                         reads=[ps[1]], writes=[b_xb])
                else:
                    p.op("act", lambda e: e.activation(xb[:, XSA:XSA + 32], ps[0][:, 0:32], AF.Copy),
                         reads=[ps[1]], writes=[b_xb])
                    p.op("act", lambda e: e.activation(xb[:, XSB:XSB + 32], ps[0][:, 32:64], AF.Copy),
                         reads=[ps[1]], writes=[b_xb])
            for (c0, n) in nblocks(QOFF, TALL):
                ps = psum_f()
                proj(wt, b_w, 128, c0, n, ps)
                p.op("act", lambda e: e.activation(sgb[:, c0 - QOFF:c0 - QOFF + n], ps[0][:, 0:n], AF.Silu),
                     reads=[ps[1]], writes=[b_sgb])

        b_xc2 = [Buf(), Buf()]
        b_xcb2 = [Buf(), Buf()]
        b_rg2 = [Buf(), Buf()]
        b_ig2 = [Buf(), Buf()]
        b_bb2 = [Buf(), Buf()]
        HALF = ((0, 1024), (1024, TALL))

        def lru_C(n_):
            xb, b_xb = xb_[n_ % 2], b_xb_[n_ % 2]
            sgb, b_sgb = sgb_[n_ % 2], b_sgb_[n_ % 2]
            cw = [pvec[:, PV_CW + n_ * 4 + j:PV_CW + n_ * 4 + j + 1] for j in range(4)]
            cbv = pvec[:, PV_CB + n_:PV_CB + n_ + 1]
            ba = pvec[:, PV_BA + n_:PV_BA + n_ + 1]
            bx = pvec[:, PV_BX + n_:PV_BX + n_ + 1]
            c8n = c8[:, n_:n_ + 1]
            c16n = c16[:, n_:n_ + 1]
            runs = (((XP0 - 3, 0, 1024),), ((XP0 - 3 + 1024, 1024, 1024), (XSA - 3, 2048, 32), (XSB - 3, 2080, 32)))
            for k in range(2):
                for (xo, co, ln) in runs[k]:
                    p.op("dve", lambda e: e.tensor_scalar(xc[:, co:co + ln], xb[:, xo:xo + ln], cw[0], cbv,
                                                          ALU.mult, ALU.add), reads=[b_xb, b_pvec], writes=[b_xc2[k]])
                    for j in range(1, 4):
                        p.op("dve", lambda e: e.scalar_tensor_tensor(
                            xc[:, co:co + ln], xb[:, xo + j:xo + j + ln], cw[j], xc[:, co:co + ln],
                            ALU.mult, ALU.add), reads=[b_xb, b_pvec, b_xc2[k]], writes=[b_xc2[k]])
            for k in range(2):
                a, b = HALF[k]
                p.op("act", lambda e: e.activation(xcb[:, a:b], xc[:, a:b], AF.Copy),
                     reads=[b_xc2[k]], writes=[b_xcb2[k]])
            p.op("pool", lambda e: e.tensor_copy(sto[:, ST_LP + 3 * n_:ST_LP + 3 * n_ + 3],
                                                 xb[:, XP0 + 2045:XP0 + 2048]), reads=[b_xb], writes=[b_sto])
            for s_, xs0 in ((0, XSA), (1, XSB)):
                p.op("pool", lambda e: e.tensor_copy(
                    sto[:, ST_LS + 6 * n_ + 3 * s_:ST_LS + 6 * n_ + 3 * s_ + 3], xb[:, xs0 + 29:xs0 + 32]),
                    reads=[b_xb], writes=[b_sto])
            for k in range(2):
                a, b = HALF[k]
                for g_, dst, b_dst, bias_ in ((0, rg, b_rg2[k], ba), (1, ig, b_ig2[k], bx)):
                    for (c0, n) in nblocks(a, b):
                        ps = psum_f()
                        p.op("pe", lambda e: e.matmul(ps[0][:, 0:n], waxb[:, g_, n_, :], xcb[:, c0:c0 + n],
                                                      start=True, stop=True), reads=[b_wax, b_xcb2[k]], writes=[ps[1]])
                        p.op("act", lambda e: e.activation(dst[:, c0:c0 + n], ps[0][:, 0:n], AF.Sigmoid, bias=bias_),
                             reads=[ps[1], b_pvec], writes=[b_dst])
            for k in range(2):
                a, b = HALF[k]
                p.op("dve", lambda e: e.tensor_tensor(ig[:, a:b], ig[:, a:b], xc[:, a:b], ALU.mult),
                     reads=[b_ig2[k], b_xc2[k]], writes=[b_ig2[k]])
            for k in range(2):
                a, b = HALF[k]
                p.op("act", lambda e: e.activation(bb[:, a:b], rg[:, a:b], AF.Exp, scale=c16n),
                     reads=[b_rg2[k], b_const], writes=[b_bb2[k]])
                p.op("act", lambda e: e.activation(aa[:, a:b], rg[:, a:b], AF.Exp, scale=c8n),
                     reads=[b_rg2[k], b_const, b_ig2[k]], writes=[b_xc2[k]])
            for k in range(2):
                a, b = HALF[k]
                p.op("act", lambda e: e.activation(bb[:, a:b], bb[:, a:b], AF.Sqrt, scale=-1.0, bias=1.0),
                     reads=[b_bb2[k]], writes=[b_bb2[k]])
            for k in range(2):
                a, b = HALF[k]
                p.op("dve", lambda e: e.tensor_tensor(bb[:, a:b], bb[:, a:b], ig[:, a:b], ALU.mult),
                     reads=[b_bb2[k], b_ig2[k]], writes=[b_bb2[k]])
            p.op("dve", lambda e: e.tensor_tensor_scan(hh[:, 0:1024], aa[:, 0:1024], bb[:, 0:1024], 0.0,
                                                       ALU.mult, ALU.add),
                 reads=[b_xc2[0], b_bb2[0]], writes=[b_rg2[0]])
            p.op("dve", lambda e: e.tensor_tensor(mixT[:, 8 + n_, 0:64], hh[:, QOFF:1024], sgb[:, 0:64], ALU.mult),
                 reads=[b_rg2[0], b_sgb], writes=[b_mix[8 + n_]])
            p.op("dve", lambda e: e.tensor_tensor(hin[:, 0:1], hh[:, 1023:1024], flag, ALU.mult),
                 reads=[b_rg2[0], b_pvec], writes=[b_hin])
            p.op("dve", lambda e: e.tensor_tensor_scan(hh[:, 1024:2048], aa[:, 1024:2048], bb[:, 1024:2048],
                                                       hin[:, 0:1], ALU.mult, ALU.add),
                 reads=[b_xc2[1], b_bb2[1], b_hin], writes=[b_rg2[1]])
            for s_ in range(2):
                c0 = 2048 + 32 * s_
                p.op("dve", lambda e: e.tensor_tensor_scan(
                    hh[:, c0:c0 + 32], aa[:, c0:c0 + 32], bb[:, c0:c0 + 32],
                    h0sb[:, 2 * n_ + s_:2 * n_ + s_ + 1], ALU.mult, ALU.add),
                    reads=[b_xc2[1], b_bb2[1], b_h0], writes=[b_rg2[1]])
            p.op("dve", lambda e: e.tensor_tensor(mixT[:, 8 + n_, 64:NQ], hh[:, 1024:TALL], sgb[:, 64:NQ], ALU.mult),
                 reads=[b_rg2[1], b_sgb], writes=[b_mix[8 + n_]])
            p.op("pool", lambda e: e.tensor_copy(sto[:, ST_HP + n_:ST_HP + n_ + 1], hh[:, 2047:2048]),
                 reads=[b_rg2[1]], writes=[b_sto])
            for s_ in range(2):
                p.op("pool", lambda e: e.tensor_copy(sto[:, ST_HS + 2 * n_ + s_:ST_HS + 2 * n_ + s_ + 1],
                                                     hh[:, 2079 + 32 * s_:2080 + 32 * s_]),
                     reads=[b_rg2[1]], writes=[b_sto])

        wo_l0 = arena_t[:][:, base_top:base_top + (NKT * D) // 2].bitcast(BF16).rearrange("p (k c) -> p k c", c=D)
        b_wo_l0 = Buf()
        lru_P(0)
        for n_ in range(8):
            if n_ + 1 < 8:
                lru_P(n_ + 1)
            if n_ + 2 < 8:
                load_wB(n_ + 2)
            if n_ == 6:
                for g in range(8):
                    p.dma("pool", wo_l0[:, 2 * g:2 * g + 2, :],
                          wo0[256 * g:256 * (g + 1), :].rearrange("(k p) c -> p k c", p=128),
                          writes=[b_wo_l0] + b_xnT)
            lru_C(n_)
        p.dma("sp", sto_d, sto, reads=[b_sto])
        p.barrier()
        ar.top = base_top

        set_banks(6)
        WC0_OFF = ARENA_WORDS - 3072 - 8
        wC0_top = arena_t[:][:, WC0_OFF:WC0_OFF + 3072].bitcast(BF16).rearrange("p (k c) -> p k c", c=384)
        b_wC = [Buf() for _ in range(2)]
        for g in range(4):
            p.dma("pool", wC0_top[:, 4 * g:4 * g + 4, :],
                  wC[0, 512 * g:512 * (g + 1), :].rearrange("(k p) c -> p k c", p=128), writes=[b_wC[0]])
        ar.top = base_top
        wo = ar.bf(NKT * D).rearrange("p (k c) -> p k c", c=D)
        assert ar.top <= base_top + (NKT * TALL) // 2
        ar.top = l0_top
        xn1T = ar.bf(NKT * NQ).rearrange("p (k c) -> p k c", c=NQ)
        xn1_end = ar.top
        b_xn1 = [Buf() for _ in range(9)]
        xres_ = [ar.f32(D) for _ in range(2)]
        x1t_ = [ar.f32(D) for _ in range(2)]
        xs1_ = [ar.bf(D) for _ in range(2)]
        ss1_ = [ar.f32(8) for _ in range(2)]
        assert ar.top <= WC0_OFF, ("L0-D overlaps wC slot", ar.top, WC0_OFF)
        b_wo = b_wo_l0
        wo = wo_l0
        b_xres_ = [Buf() for _ in range(2)]
        b_x1t_ = [Buf() for _ in range(2)]
        b_xs1_ = [Buf() for _ in range(2)]
        b_ss1_ = [Buf() for _ in range(2)]
        b_scr = [Buf() for _ in range(9)]
        pend = None
        for j in range(9):
            i = j % 2
            xres, x1t = xres_[i], x1t_[i]
            p.dma("sp", xres, x_ext[QOFF + 128 * j:QOFF + 128 * (j + 1), :], writes=[b_xres_[i]])
            for cb_ in range(4):
                ps = psum_f()
                for kt in range(NKT):
                    p.op("pe", lambda e: e.matmul(
                        ps[0][:, 0:512], mixT[:, kt, 128 * j:128 * (j + 1)], wo[:, kt, 512 * cb_:512 * (cb_ + 1)],
                        start=(kt == 0), stop=(kt == NKT - 1)),
                        reads=[b_mix[kt], b_wo], writes=[ps[1]], inc=(kt == NKT - 1))
                p.op("dve", lambda e: e.tensor_tensor(
                    x1t[:, 512 * cb_:512 * (cb_ + 1)], ps[0][:, 0:512], xres[:, 512 * cb_:512 * (cb_ + 1)],
                    ALU.add), reads=[ps[1], b_xres_[i]], writes=[b_x1t_[i]])
            p.dma("sp", x1_scr[128 * j:128 * (j + 1), :], x1t, reads=[b_x1t_[i]], writes=[b_scr[j]])
            rms_part1(128, x1t, b_x1t_[i], xs1_[i], b_xs1_[i], ss1_[i][:, 0:1], ss1_[i][:, 1:2], b_ss1_[i])
            if pend is not None:
                rms_part2(*pend)
            pend = (128, xs1_[i], b_xs1_[i], PV_G1, xn1T, 128 * j, b_xn1[j])
        rms_part2(*pend)
        p.barrier()

        set_banks(8)
        ar.top = base_top
        wCb = [ar.bf(NKT * 384).rearrange("p (k c) -> p k c", c=384) for _ in range(2)]
        wc_end = ar.top
        wCb[0] = wC0_top
        zcT = ar.bf(16 * 1088).rearrange("p (f c) -> p f c", c=1088)
        b_zc = [Buf() for _ in range(16)]
        sgT = ar.bf(16 * 1088).rearrange("p (f c) -> p f c", c=1088)
        b_sg = [Buf() for _ in range(16)]
        sg_end = ar.top
        assert ar.top <= l0_top, (ar.top, l0_top)
        ar.top = xn1_end
        zb = [ar.f32(1216) for _ in range(2)]
        b_zb = [Buf() for _ in range(2)]
        sig = [ar.f32(512) for _ in range(3)]
        b_sig = [Buf() for _ in range(3)]
        acc = [ar.f32(1152) for _ in range(2)]
        b_acc = [Buf() for _ in range(2)]
        cbo = ar.f32(16 * 90)
        b_cbo = Buf()
        KPE = 18
        zbf = [ar.bf(1216) for _ in range(2)]
        b_zbf = [Buf() for _ in range(2)]
        dg = [ar.bf(KPE * 128).rearrange("p (t c) -> p t c", c=128) for _ in range(2)]
        b_dg = [Buf() for _ in range(2)]
        rr1 = {"sig": 0}

        def load_wC(f):
            for g in range(4):
                p.dma("pool", wCb[f % 2][:, 4 * g:4 * g + 4, :],
                      wC[f, 512 * g:512 * (g + 1), :].rearrange("(k p) c -> p k c", p=128),
                      writes=[b_wC[f % 2]])

        def proj1(wt, b_w, ccol, c0, n, ps):
            rd = [b_w] + [b_xn1[t] for t in range(c0 // 128, (c0 + n - 1) // 128 + 1)]
            for kt in range(NKT):
                p.op("pe", (lambda kt=kt: lambda e: e.matmul(ps[0][:, 0:n], wt[:, kt, ccol:ccol + 128],
                                                              xn1T[:, kt, c0:c0 + n],
                                                              start=(kt == 0), stop=(kt == NKT - 1)))(),
                     reads=rd, writes=[ps[1]], inc=(kt == NKT - 1))

        def zcol(c):
            if c < 1088:
                return c
            if c < 1120:
                return c + 30
            return c + 60

        def l1a_proj(f, fillers):
            wt, b_w = wCb[f % 2], b_wC[f % 2]
            z, b_z = zb[f % 2], b_zb[f % 2]
            csv = cs0T[:, f * 60:f * 60 + 60]
            p.dma("sp", z[:, 1088:1118], csv[:, 0:30], writes=[b_z])
            p.dma("sp", z[:, 1150:1180], csv[:, 30:60], writes=[b_z])
            for bi, (c0, n) in enumerate(nblocks(0, NQ)):
                psv = psum_f()
                proj1(wt, b_w, 0, c0, n, psv)
                psg = psum_f()
                proj1(wt, b_w, 128, c0, n, psg)
                si = rr1["sig"]
                rr1["sig"] = (si + 1) % 3
                sg_, b_s = sig[si], b_sig[si]
                p.op("act", lambda e: e.activation(sg_[:, 0:n], psg[0][:, 0:n], AF.Sigmoid),
                     reads=[psg[1]], writes=[b_s])
                if c0 + n <= 1088:
                    p.op("dve", lambda e: e.tensor_tensor(z[:, c0:c0 + n], psv[0][:, 0:n], sg_[:, 0:n], ALU.mult),
                         reads=[psv[1], b_s], writes=[b_z])
                else:
                    for (o, ln, zc0) in ((0, 64, 1024), (64, 32, 1118), (96, 32, 1180)):
                        p.op("dve", lambda e: e.tensor_tensor(z[:, zc0:zc0 + ln], psv[0][:, o:o + ln],
                                                              sg_[:, o:o + ln], ALU.mult),
                             reads=[psv[1], b_s], writes=[b_z])
                pst = psum_f()
                proj1(wt, b_w, 256, c0, n, pst)
                lo = max(c0, 64)
                si2 = rr1["sig"]
                rr1["sig"] = (si2 + 1) % 3
                sg2, b_s2 = sig[si2], b_sig[si2]
                p.op("act", lambda e: e.activation(sg2[:, 0:n], pst[0][:, 0:n], AF.Sigmoid),
                     reads=[pst[1]], writes=[b_s2])
                p.op("dve", lambda e: e.tensor_tensor(sgT[:, f, lo - 64:c0 + n - 64], pst[0][:, lo - c0:n],
                                                      sg2[:, lo - c0:n], ALU.mult),
                     reads=[pst[1], b_s2], writes=[b_sg[f]])
                for fn in fillers[bi]:
                    fn()
            for (k_, zc0) in ((0, 1088 - 30), (1, 1150 - 30), (2, 1212 - 30)):
                p.op("pool", lambda e: e.tensor_copy(cbo[:, f * 90 + 30 * k_:f * 90 + 30 * k_ + 30],
                                                     z[:, zc0:zc0 + 30]), reads=[b_z], writes=[b_cbo])
            zb_, b_zb_ = zbf[f % 2], b_zbf[f % 2]
            dg_, b_dg_ = dg[f % 2], b_dg[f % 2]
            p.op("pool", lambda e: e.tensor_tensor(
                dg_, ident.unsqueeze(1).to_broadcast([128, KPE, 128]),
                pvec[:, PV_DWW + f * 31:PV_DWW + f * 31 + KPE].unsqueeze(2).to_broadcast([128, KPE, 128]),
                ALU.mult), reads=[b_const, b_pvec], writes=[b_dg_])
            p.op("act", lambda e: e.activation(zb_[:, 0:1212], z[:, 0:1212], AF.Copy), reads=[b_z], writes=[b_zb_])

        def l1a_taps(f):
            z, b_z = zb[f % 2], b_zb[f % 2]
            ac, b_ac = acc[f % 2], b_acc[f % 2]
            dww = [pvec[:, PV_DWW + f * 31 + j:PV_DWW + f * 31 + j + 1] for j in range(31)]
            dwb = pvec[:, PV_DWB + f:PV_DWB + f + 1]
            fns = []
            fns.append(lambda: p.op("dve", lambda e: e.tensor_scalar(
                ac[:, 0:1148], z[:, 34 + KPE:34 + KPE + 1148], dww[KPE], dwb, ALU.mult, ALU.add),
                reads=[b_z, b_pvec], writes=[b_ac]))
            for j in range(KPE + 1, 31):
                fns.append((lambda j: lambda: p.op("dve", lambda e: e.scalar_tensor_tensor(
                    ac[:, 0:1148], z[:, 34 + j:34 + j + 1148], dww[j], ac[:, 0:1148], ALU.mult, ALU.add),
                    reads=[b_z, b_pvec, b_ac], writes=[b_ac]))(j))
            k = (len(fns) + 2) // 3
            return [fns[0:k], fns[k:2 * k], fns[2 * k:]]

        def l1a_conv(f):
            ac, b_ac = acc[f % 2], b_acc[f % 2]
            zb_, b_zb_ = zbf[f % 2], b_zbf[f % 2]
            dg_, b_dg_ = dg[f % 2], b_dg[f % 2]
            for (c0, n) in nblocks(0, 1148):
                ps = psum_f()
                for t in range(KPE):
                    p.op("pe", lambda e: e.matmul(ps[0][:, 0:n], dg_[:, t, :], zb_[:, 34 + t + c0:34 + t + c0 + n],
                                                  start=(t == 0), stop=(t == KPE - 1)),
                         reads=[b_dg_, b_zb_], writes=[ps[1]], inc=(t == KPE - 1))
                p.op("dve", lambda e: e.tensor_tensor(ac[:, c0:c0 + n], ps[0][:, 0:n], ac[:, c0:c0 + n], ALU.add),
                     reads=[ps[1], b_ac], writes=[b_ac])
            p.op("act", lambda e: e.activation(zcT[:, f, 0:1024], ac[:, 0:1024], AF.Copy),
                 reads=[b_ac], writes=[b_zc[f]])
            p.op("act", lambda e: e.activation(zcT[:, f, 1024:1056], ac[:, 1054:1086], AF.Copy),
                 reads=[b_ac], writes=[b_zc[f]])
            p.op("act", lambda e: e.activation(zcT[:, f, 1056:1088], ac[:, 1116:1148], AF.Copy),
                 reads=[b_ac], writes=[b_zc[f]])

        wo_l1 = arena_t[:][:, l0_top:l0_top + (NKT * D) // 2].bitcast(BF16).rearrange("p (k c) -> p k c", c=D)
        b_woc = [Buf() for _ in range(8)]
        assert 4 * 2 * D // 2 <= (NKT * NQ) // 2
        assert ar.top <= WC0_OFF, ("L1-A overlaps wC slot", ar.top, WC0_OFF)
        load_wC(1)
        for f in range(16):
            l1a_proj(f, l1a_taps(f - 1) if f >= 1 else [[], [], []])
            if f + 2 < 16:
                load_wC(f + 2)
            if f == 15:
                for g in range(4):
                    p.dma("pool", wo_l1[:, 2 * g:2 * g + 2, :],
                          wo1[256 * g:256 * (g + 1), :].rearrange("(k p) c -> p k c", p=128),
                          writes=[b_woc[g]] + b_xn1)
            if f >= 1:
                l1a_conv(f - 1)
        for grp in l1a_taps(15):
            for fn in grp:
                fn()
        l1a_conv(15)
        p.dma("sp", cbo_d, cbo, reads=[b_cbo])
        p.barrier()

        ar.top = base_top
        mean = ar.f32(1088)
        rstd = ar.f32(1088)
        sq = [ar.bf(512) for _ in range(2)]
        tt = [ar.f32(512) for _ in range(3)]
        assert ar.top <= wc_end, (ar.top, wc_end)
        ar.top = sg_end
        xres_ = [ar.f32(D)]
        assert ar.top <= l0_top, (ar.top, l0_top)
        ar.top = l0_top
        wo = ar.bf(NKT * D).rearrange("p (k c) -> p k c", c=D)
        wo = wo_l1
        xres_.append(ar.f32(D))
        yt_ = [ar.f32(D) for _ in range(2)]
        ss2_ = [ar.f32(8) for _ in range(2)]
        b_xres_ = [Buf() for _ in range(2)]
        b_yt_ = [Buf() for _ in range(2)]
        b_ss2_ = [Buf() for _ in range(2)]
        b_sq = [Buf() for _ in range(2)]
        b_tt = [Buf() for _ in range(3)]
        NB1 = [(0, 256), (256, 512), (768, 320)]
        b_st = [Buf() for _ in NB1]
        b_zc2 = [[Buf() for _ in NB1] for _ in range(16)]
        for g in range(4, 8):
            p.dma("pool", wo[:, 2 * g:2 * g + 2, :],
                  wo1[256 * g:256 * (g + 1), :].rearrange("(k p) c -> p k c", p=128), writes=[b_woc[g]])
        rr2 = {"tt": 0}

        def l1_stats(nb):
            c0, n = NB1[nb]
            ps1 = psum_f()
            ps2 = psum_f()
            for f in range(16):
                i = f % 2
                p.op("act", lambda e: e.activation(sq[i][:, 0:n], zcT[:, f, c0:c0 + n], AF.Square),
                     reads=[b_zc2[f][nb]], writes=[b_sq[i]])
                p.op("pe", lambda e: e.matmul(ps1[0][:, 0:n], onesdiv, zcT[:, f, c0:c0 + n],
                                              start=(f == 0), stop=(f == 15)),
                     reads=[b_zc2[f][nb], b_const], writes=[ps1[1]], inc=(f == 15))
                p.op("pe", lambda e: e.matmul(ps2[0][:, 0:n], onesdiv, sq[i][:, 0:n],
                                              start=(f == 0), stop=(f == 15)),
                     reads=[b_sq[i], b_const], writes=[ps2[1]], inc=True)
            bs = b_st[nb]
            p.op("act", lambda e: e.activation(mean[:, c0:c0 + n], ps1[0][:, 0:n], AF.Copy),
                 reads=[ps1[1]], writes=[bs])
            p.op("act", lambda e: e.activation(rstd[:, c0:c0 + n], ps1[0][:, 0:n], AF.Square),
                 reads=[ps1[1]], writes=[bs])
            p.op("dve", lambda e: e.tensor_tensor(rstd[:, c0:c0 + n], ps2[0][:, 0:n], rstd[:, c0:c0 + n],
                                                  ALU.subtract), reads=[ps2[1], bs], writes=[bs])
            p.op("dve", lambda e: e.tensor_scalar(rstd[:, c0:c0 + n], rstd[:, c0:c0 + n], 0.0, None, ALU.max),
                 reads=[bs], writes=[bs])
            p.op("act", lambda e: e.activation(rstd[:, c0:c0 + n], rstd[:, c0:c0 + n], AF.Sqrt, bias=EPS),
                 reads=[bs], writes=[bs])
            p.op("dve", lambda e: e.reciprocal(rstd[:, c0:c0 + n], rstd[:, c0:c0 + n]), reads=[bs], writes=[bs])

        def l1_ln_tasks(nb):
            c0, n = NB1[nb]
            bs = b_st[nb]
            slot = {}

            def front(f):
                i = rr2["tt"]
                rr2["tt"] = (i + 1) % 3
                slot[f] = i
                t_ = tt[i][:, 0:n]
                lng = pvec[:, PV_LNG + f:PV_LNG + f + 1]
                lnb = pvec[:, PV_LNB + f:PV_LNB + f + 1]
                zsl = zcT[:, f, c0:c0 + n]
                p.op("dve", lambda e: e.tensor_tensor(t_, zsl, mean[:, c0:c0 + n], ALU.subtract),
                     reads=[b_zc2[f][nb], bs], writes=[b_tt[i]])
                p.op("dve", lambda e: e.tensor_tensor(t_, t_, rstd[:, c0:c0 + n], ALU.mult),
                     reads=[b_tt[i], bs], writes=[b_tt[i]])
                p.op("act", lambda e: e.activation(t_, t_, AF.Silu, scale=lng, bias=lnb),
                     reads=[b_tt[i], b_pvec], writes=[b_tt[i]])

            def back(f):
                i = slot[f]
                t_ = tt[i][:, 0:n]
                zsl = zcT[:, f, c0:c0 + n]
                p.op("dve", lambda e: e.tensor_tensor(zsl, t_, sgT[:, f, c0:c0 + n], ALU.mult),
                     reads=[b_tt[i], b_sg[f]], writes=[b_zc2[f][nb]])

            def one(f):
                if f < 16:
                    front(f)
                if f >= 1:
                    back(f - 1)
            return [(lambda f=f: one(f)) for f in range(17)]

        def l1_wout(nb, fillers=()):
            fillers = list(fillers)
            c0, n = NB1[nb]
            nslots = 4 * len(nblocks(c0, c0 + n, 128))
            per_slot = (len(fillers) + nslots - 1) // nslots
            for (r0, nrows) in nblocks(c0, c0 + n, 128):
                j = r0 // 128
                i = j % 2
                xres, yt, ss2 = xres_[i], yt_[i], ss2_[i]
                p.dma("sp", xres[0:nrows, :], x1_scr[64 + r0:64 + r0 + nrows, :], reads=b_scr, writes=[b_xres_[i]])
                for cb_ in range(4):
                    ps = psum_f()
                    for kt in range(NKT):
                        p.op("pe", lambda e: e.matmul(
                            ps[0][0:nrows, 0:512], zcT[:, kt, r0:r0 + nrows],
                            wo[:, kt, 512 * cb_:512 * (cb_ + 1)], start=(kt == 0), stop=(kt == NKT - 1)),
                            reads=[b_zc2[kt][nb], b_woc[kt // 2]], writes=[ps[1]], inc=(kt == NKT - 1))
                    p.op("dve", lambda e: e.tensor_tensor(
                        xres[0:nrows, 512 * cb_:512 * (cb_ + 1)], ps[0][0:nrows, 0:512],
                        xres[0:nrows, 512 * cb_:512 * (cb_ + 1)], ALU.add),
                        reads=[ps[1], b_xres_[i]], writes=[b_xres_[i]])
                    for _ in range(per_slot):
                        if fillers:
                            fillers.pop(0)()
                p.op("act", lambda e: e.activation(yt[0:nrows, :], xres[0:nrows, :], AF.Square,
                                                   accum_out=ss2[0:nrows, 0:1]),
                     reads=[b_xres_[i]], writes=[b_yt_[i], b_ss2_[i]])
                p.op("act", lambda e: e.activation(ss2[0:nrows, 1:2], ss2[0:nrows, 0:1], AF.Sqrt,
                                                   scale=1.0 / D, bias=EPS), reads=[b_ss2_[i]], writes=[b_ss2_[i]])
                p.op("dve", lambda e: e.reciprocal(ss2[0:nrows, 1:2], ss2[0:nrows, 1:2]),
                     reads=[b_ss2_[i]], writes=[b_ss2_[i]])
                p.op("dve", lambda e: e.scalar_tensor_tensor(
                    yt[0:nrows, :], xres[0:nrows, :], ss2[0:nrows, 1:2], fnb[0:nrows, :], ALU.mult, ALU.mult),
                    reads=[b_xres_[i], b_ss2_[i], b_fnb], writes=[b_yt_[i]])
                dst = y_main[r0:r0 + 128, :] if r0 < 1024 else y_samp[0:64, :]
                p.dma("sp", dst, yt[0:nrows, :], reads=[b_yt_[i]])
            for fn in fillers:
                fn()

        l1_stats(0)
        for fn in l1_ln_tasks(0):
            fn()
        l1_stats(1)
        l1_wout(0, l1_ln_tasks(1))
        l1_stats(2)
        l1_wout(1, l1_ln_tasks(2))
        l1_wout(2)
        p.finish()
        p.emit(st)
    return nc


def _bias_tables(rel):
    r = np.arange(128)
    out = np.full((8, 128, 704), NEG, np.float32)
    q = np.arange(128)
    for j in range(5):
        kpos = (j - 4) * 128 + r
        kc_ = 2 * (j - 4) + (r >= 64)
        qc = (q >= 64).astype(int)
        vis = (kc_[:, None] >= qc[None, :] - 8) & (kc_[:, None] <= qc[None, :])
        idx = np.clip(q[None, :] - kpos[:, None], -128, 128) + 128
        vals = rel[:, idx]
        out[:, :, j * 128:(j + 1) * 128] = np.where(vis[None], vals, NEG)
    idx = np.clip(q[None, :32] - r[:32, None], -128, 128) + 128
    vals = rel[:, idx]
    out[:, 0:32, 640:672] = vals
    out[:, 32:64, 672:704] = vals
    return out


_NC_CACHE = {}
LAST_DBG = None


def kernel(x_prompt, x_sample, cache_attn_k, cache_attn_v, state_lru_h, state_lru_conv, state_conv,
           norm_ab, w_in_ab, w_out_ab, rel_bias, lru_conv_w, lru_conv_b, lru_w_a, lru_b_a, lru_w_x,
           lru_b_x, lru_lambda, norm_cv, w_in_cv, w_out_cv, dw_w, dw_b, ln_g, ln_b, final_norm):
    f = lambda a: np.ascontiguousarray(np.asarray(a, dtype=np.float32))
    x_prompt, x_sample = f(x_prompt), f(x_sample)
    w_in = f(w_in_ab)[0]
    wA = np.stack([np.concatenate([w_in[:, c * 1024 + h * 128:c * 1024 + (h + 1) * 128] for c in range(4)], axis=1)
                   for h in range(8)])
    wB = np.stack([np.concatenate([w_in[:, 4096 + n * 128:4096 + (n + 1) * 128],
                                   w_in[:, 5120 + n * 128:5120 + (n + 1) * 128]], axis=1) for n in range(8)])
    w_cv = f(w_in_cv)[0]
    wC = np.stack([np.concatenate([w_cv[:, c * 2048 + k * 128:c * 2048 + (k + 1) * 128] for c in range(3)], axis=1)
                   for k in range(16)])
    wo0 = f(w_out_ab)[0]
    wo1 = f(w_out_cv)[0]
    wax = np.stack([f(lru_w_a)[0], f(lru_w_x)[0]])
    wax = np.ascontiguousarray(wax.transpose(2, 0, 1, 3)).reshape(128, 2 * 8 * 128)
    col = lambda v, k: np.ascontiguousarray(f(v).reshape(k, 128).T)
    pv = np.zeros((8, 128, PV_N), np.float32)
    base = np.zeros((128, PV_N), np.float32)
    base[:, PV_G0:PV_G0 + 16] = col(norm_ab[0], 16)
    base[:, PV_G1:PV_G1 + 16] = col(norm_cv[0], 16)
    cw = f(lru_conv_w)[0]
    base[:, PV_CW:PV_CW + 32] = cw.reshape(4, 8, 128).transpose(2, 1, 0).reshape(128, 32)
    base[:, PV_CB:PV_CB + 8] = col(lru_conv_b[0], 8)
    base[:, PV_BA:PV_BA + 8] = col(lru_b_a[0], 8)
    base[:, PV_BX:PV_BX + 8] = col(lru_b_x[0], 8)
    base[:, PV_LAM:PV_LAM + 8] = col(lru_lambda[0], 8)
    dww = f(dw_w)[0]
    base[:, PV_DWW:PV_DWW + 496] = dww.reshape(31, 16, 128).transpose(2, 1, 0).reshape(128, 496)
    base[:, PV_DWB:PV_DWB + 16] = col(dw_b[0], 16)
    base[:, PV_LNG:PV_LNG + 16] = col(ln_g[0], 16)
    base[:, PV_LNB:PV_LNB + 16] = col(ln_b[0], 16)
    fnb = np.ascontiguousarray(np.broadcast_to(f(final_norm)[None, :], (128, D)))
    biasT = _bias_tables(f(rel_bias)[0])
    ck, cv = f(cache_attn_k)[0], f(cache_attn_v)[0]
    slh, slc, scv = f(state_lru_h)[0], f(state_lru_conv)[0], f(state_conv)[0]

    in_maps = []
    for c in range(8):
        b, half = c // 2, c % 2
        xe = np.zeros((TALL, D), np.float32)
        if half == 1:
            xe[0:1024] = x_prompt[b, 0:1024]
        xe[1024:2048] = x_prompt[b, half * 1024:(half + 1) * 1024]
        xe[2048:2112] = x_sample[2 * c:2 * c + 2].reshape(64, D)
        pvc = base.copy()
        pvc[:, PV_FLAG] = float(half)
        h0 = slh[2 * c:2 * c + 2]
        h0T = np.ascontiguousarray(h0.reshape(2, 8, 128).transpose(2, 1, 0)).reshape(128, 16)
        lb = slc[2 * c:2 * c + 2]
        lb0T = np.ascontiguousarray(lb.reshape(2, 3, 8, 128).transpose(3, 2, 0, 1)).reshape(128, 48)
        cs = scv[2 * c:2 * c + 2]
        cs0T = np.ascontiguousarray(cs.reshape(2, 30, 16, 128).transpose(3, 2, 0, 1)).reshape(128, 960)
        in_maps.append({
            "x_ext": xe,
            "kc": np.ascontiguousarray(ck[2 * c:2 * c + 2].reshape(2, 512, 1024)),
            "vc": np.ascontiguousarray(cv[2 * c:2 * c + 2].reshape(2, 512, 1024)),
            "h0T": h0T, "lb0T": lb0T, "cs0T": cs0T,
            "wA": wA, "wB": wB, "wC": wC, "wo0": wo0, "wo1": wo1, "wax": wax,
            "pvec": pvc, "fnb": fnb, "biasT": biasT,
        })
    if "nc" not in _NC_CACHE:
        _NC_CACHE["nc"] = build_program()
    res = run_bass_kernel_spmd(_NC_CACHE["nc"], in_maps, core_ids=list(range(8)))
    R = res.results
    global LAST_DBG
    LAST_DBG = [r.get("dbg") for r in R] if DEBUG else None

    y_prompt = np.zeros((4, 2048, D), np.float32)
    y_sample = np.zeros((16, 32, D), np.float32)
    k_p = np.zeros((1, 4, 512, 8, 128), np.float32)
    v_p = np.zeros_like(k_p)
    h_p = np.zeros((1, 4, 1024), np.float32)
    lb_p = np.zeros((1, 4, 3, 1024), np.float32)
    cb_p = np.zeros((1, 4, 30, 2048), np.float32)
    k_s = np.zeros((1, 16, 32, 8, 128), np.float32)
    v_s = np.zeros_like(k_s)
    h_s = np.zeros((1, 16, 1024), np.float32)
    lb_s = np.zeros((1, 16, 3, 1024), np.float32)
    cb_s = np.zeros((1, 16, 30, 2048), np.float32)
    for c in range(8):
        b, half = c // 2, c % 2
        r = R[c]
        y_prompt[b, half * 1024:(half + 1) * 1024] = r["y_main"]
        y_sample[2 * c:2 * c + 2] = r["y_samp"].reshape(2, 32, D)
        sto = r["sto"]
        cbo = r["cbo"].reshape(128, 16, 3, 30)
        kvs = r["kvo_s"].reshape(2, 2, 32, 8, 128)
        k_s[0, 2 * c:2 * c + 2] = kvs[0]
        v_s[0, 2 * c:2 * c + 2] = kvs[1]
        hs = sto[:, ST_HS:ST_HS + 16].reshape(128, 8, 2)
        h_s[0, 2 * c:2 * c + 2] = hs.transpose(2, 1, 0).reshape(2, 1024)
        ls = sto[:, ST_LS:ST_LS + 48].reshape(128, 8, 2, 3)
        lb_s[0, 2 * c:2 * c + 2] = ls.transpose(2, 3, 1, 0).reshape(2, 3, 1024)
        cb_s[0, 2 * c:2 * c + 2] = cbo[:, :, 1:3, :].transpose(2, 3, 1, 0).reshape(2, 30, 2048)
        if half == 1:
            kvp = r["kvo_p"].reshape(2, 512, 8, 128)
            k_p[0, b] = kvp[0]
            v_p[0, b] = kvp[1]
            h_p[0, b] = sto[:, ST_HP:ST_HP + 8].T.reshape(1024)
            lb_p[0, b] = sto[:, ST_LP:ST_LP + 24].reshape(128, 8, 3).transpose(2, 1, 0).reshape(3, 1024)
            cb_p[0, b] = cbo[:, :, 0, :].transpose(2, 1, 0).reshape(30, 2048)
    return (y_prompt, y_sample, k_p, v_p, h_p, lb_p, cb_p, k_s, v_s, h_s, lb_s, cb_s)
```
